# Optimizing a Trainium2 kernel written in Bass

```python
import math
import jax, jax.numpy as jnp
from jax import lax
import numpy as np

D_MODEL = 1024
BATCH = 8
SEQ = 2048
DEPTH = 2
DEC_BATCH = 128
DEC_SEQ = 4
PAST_LEN = 16384
PAGE_SIZE = 128

MIX_WIDTH = D_MODEL
N_MIXERS = 4
GROUP_WIDTH = MIX_WIDTH // N_MIXERS

SSM_WIDTH = GROUP_WIDTH
SSM_CH = 16
SSM_GROUPS = SSM_WIDTH // SSM_CH
SSM_STATE = 64

HGRN_WIDTH = GROUP_WIDTH
HGRN_HEAD = 64
HGRN_HEADS = HGRN_WIDTH // HGRN_HEAD
HGRN_CHUNK = 32
HGRN_NORM_EPS = 1e-5

RWKV_WIDTH = GROUP_WIDTH
RWKV_HEAD = 64
RWKV_HEADS = RWKV_WIDTH // RWKV_HEAD
DECAY_LORA = 64
AAA_LORA = 64
GATE_LORA = 128
RWKV_PROJ = 3 * RWKV_WIDTH + DECAY_LORA + AAA_LORA + GATE_LORA
RWKV_GN_EPS = 64e-5

POOL_WIDTH = MIX_WIDTH - SSM_WIDTH - HGRN_WIDTH - RWKV_WIDTH
POOL_WINDOWS = (2, 4, 8, 16)
POOL_CH = POOL_WIDTH // 4
POOL_BUF = 16 - 1

D_FF = 4 * D_MODEL
NORM_EPS = 1e-6
PROJ_WIDTH = SSM_WIDTH + 4 * HGRN_WIDTH + RWKV_PROJ + POOL_WIDTH
PROJ_SPLITS = (SSM_WIDTH, SSM_WIDTH + HGRN_WIDTH, SSM_WIDTH + 2 * HGRN_WIDTH, SSM_WIDTH + 3 * HGRN_WIDTH,
               SSM_WIDTH + 4 * HGRN_WIDTH, SSM_WIDTH + 4 * HGRN_WIDTH + RWKV_PROJ)
RWKV_SPLITS = (RWKV_WIDTH, 2 * RWKV_WIDTH, 3 * RWKV_WIDTH, 3 * RWKV_WIDTH + DECAY_LORA,
               3 * RWKV_WIDTH + DECAY_LORA + AAA_LORA)

kernel_name = 'hymba_s5_hgrn2_rwkv7_pool_step'


def _f32(a):
    return a.astype(jnp.float32)


def rmsnorm(x, g, eps=NORM_EPS):
    xf = _f32(x)
    y = xf * lax.rsqrt(jnp.mean(xf * xf, axis=-1, keepdims=True) + eps)
    return (y * _f32(g)).astype(x.dtype)


def s5_mixer(u, h0_re, h0_im, lam_re, lam_im, log_dt, b_re, b_im, c_re, c_im, d_skip, glu_w, glu_b):
    bsz, t_len, _ = u.shape
    uf = _f32(u)
    ug = uf.reshape(bsz, t_len, SSM_GROUPS, SSM_CH)
    lr, li = _f32(lam_re), _f32(lam_im)
    dt = jnp.exp(_f32(log_dt))[:, None]
    mag = jnp.exp(lr * dt)
    ab_re, ab_im = mag * jnp.cos(li * dt), mag * jnp.sin(li * dt)
    den = lr * lr + li * li
    zr, zi = ab_re - 1.0, ab_im
    cr = (zr * lr + zi * li) / den
    ci = (zi * lr - zr * li) / den
    br, bi = _f32(b_re), _f32(b_im)
    bb_re = cr[..., None] * br - ci[..., None] * bi
    bb_im = cr[..., None] * bi + ci[..., None] * br
    bu_re = jnp.einsum('gph,btgh->btgp', bb_re, ug)
    bu_im = jnp.einsum('gph,btgh->btgp', bb_im, ug)
    a_re = jnp.broadcast_to(ab_re, bu_re.shape)
    a_im = jnp.broadcast_to(ab_im, bu_im.shape)

    def combine(e1, e2):
        a1r, a1i, b1r, b1i = e1
        a2r, a2i, b2r, b2i = e2
        return (a2r * a1r - a2i * a1i, a2r * a1i + a2i * a1r,
                a2r * b1r - a2i * b1i + b2r, a2r * b1i + a2i * b1r + b2i)

    _, _, hr, hi = lax.associative_scan(combine, (a_re, a_im, bu_re, bu_im), axis=1)
    steps = jnp.arange(1, t_len + 1, dtype=jnp.float32)[:, None, None]
    pmag = jnp.exp(steps * (lr * dt))
    pw_re = pmag * jnp.cos(steps * (li * dt))
    pw_im = pmag * jnp.sin(steps * (li * dt))
    g_re = _f32(h0_re)[:, None]
    g_im = _f32(h0_im)[:, None]
    hr = hr + pw_re * g_re - pw_im * g_im
    hi = hi + pw_re * g_im + pw_im * g_re
    y = jnp.einsum('ghp,btgp->btgh', _f32(c_re), hr) - jnp.einsum('ghp,btgp->btgh', _f32(c_im), hi)
    y = y.reshape(bsz, t_len, SSM_WIDTH) + _f32(d_skip) * uf
    z = jax.nn.gelu(y)
    out = z * jax.nn.sigmoid(z @ _f32(glu_w) + _f32(glu_b))
    return out, hr[:, -1], hi[:, -1]


def gla_chunked(q, k, v, log_f, s0):
    bsz, t_len, n_h, _ = q.shape
    blk = min(HGRN_CHUNK, t_len)
    n_blk = -(-t_len // blk)
    pad = n_blk * blk - t_len

    def prep(a):
        a = jnp.pad(a, ((0, 0), (0, pad), (0, 0), (0, 0)))
        return a.reshape(bsz, n_blk, blk, n_h, a.shape[-1]).transpose(1, 0, 3, 2, 4)

    qc, kc, vc, gc = prep(q), prep(k), prep(v), prep(log_f)
    causal = jnp.tril(jnp.ones((blk, blk), dtype=bool))[:, :, None]

    def step(S, inp):
        qb, kb, vb, gb = inp
        b = jnp.cumsum(gb, axis=2)
        diff = jnp.where(causal, b[:, :, :, None, :] - b[:, :, None, :, :], -jnp.inf)
        att = jnp.einsum('bhtk,bhsk,bhtsk->bhts', qb, kb, jnp.exp(diff))
        o = jnp.einsum('bhts,bhsv->bhtv', att, vb) + jnp.einsum('bhtk,bhkv->bhtv', qb * jnp.exp(b), S)
        b_last = b[:, :, -1:, :]
        S = jnp.exp(b_last[:, :, 0, :])[..., None] * S + jnp.einsum('bhsk,bhsv->bhkv', kb * jnp.exp(b_last - b), vb)
        return S, o

    s_last, o = lax.scan(step, s0, (qc, kc, vc, gc))
    o = o.transpose(1, 0, 3, 2, 4).reshape(bsz, n_blk * blk, n_h, -1)[:, :t_len]
    return o, s_last


def hgrn2_mixer(p_q, p_f, p_i, p_g, s0, lb, norm_g):
    bsz, t_len, _ = p_q.shape
    hs = (bsz, t_len, HGRN_HEADS, HGRN_HEAD)
    zf = _f32(p_f)
    log_f = jnp.logaddexp(jnp.log1p(-lb) + jax.nn.log_sigmoid(zf), jnp.log(lb))
    k = (1.0 - lb) * jax.nn.sigmoid(-zf)
    q = jax.nn.silu(_f32(p_q))
    o, s_last = gla_chunked(q.reshape(hs), k.reshape(hs), _f32(p_i).reshape(hs), log_f.reshape(hs), _f32(s0))
    o = o * lax.rsqrt(jnp.mean(o * o, axis=-1, keepdims=True) + HGRN_NORM_EPS)
    o = o.reshape(bsz, t_len, HGRN_WIDTH) * _f32(norm_g) * jax.nn.silu(_f32(p_g))
    return o, s_last


def rwkv7_mixer(p, shift_buf, s0, mu, w0, w2, a0, a2, g2, k_k, k_a, r_k, ln_g, ln_b):
    bsz, t_len, _ = p.shape
    pf = _f32(p)
    prev = jnp.concatenate([_f32(shift_buf), pf[:, :-1]], axis=1)
    xs = pf + (prev - pf) * _f32(mu)
    xr, xk, xv, xw, xa, xg = jnp.split(xs, RWKV_SPLITS, axis=-1)
    w = -jax.nn.softplus(-(_f32(w0) + jnp.tanh(xw) @ _f32(w2))) - 0.5
    decay = jnp.exp(-jnp.exp(w))
    a = jax.nn.sigmoid(_f32(a0) + xa @ _f32(a2))
    g = jax.nn.sigmoid(xg) @ _f32(g2)
    hs = (bsz, t_len, RWKV_HEADS, RWKV_HEAD)
    kk = (xk * _f32(k_k)).reshape(hs)
    kk = kk / jnp.maximum(jnp.sqrt(jnp.sum(kk * kk, axis=-1, keepdims=True)), 1e-12)
    k = (xk * (1.0 + (a - 1.0) * _f32(k_a))).reshape(hs)
    r = xr.reshape(hs)
    v = xv.reshape(hs)

    def step(S, inp):
        r_t, w_t, k_t, v_t, kk_t, a_t = inp
        sa = jnp.einsum('bhvk,bhk->bhv', S, -kk_t)
        S = S * w_t[:, :, None, :] + sa[..., None] * (kk_t * a_t)[:, :, None, :] + v_t[..., None] * k_t[:, :, None, :]
        return S, jnp.einsum('bhvk,bhk->bhv', S, r_t)

    seq = tuple(jnp.moveaxis(z, 1, 0) for z in (r, decay.reshape(hs), k, v, kk, a.reshape(hs)))
    s_last, y = lax.scan(step, _f32(s0), seq)
    y = jnp.moveaxis(y, 0, 1)
    mean = jnp.mean(y, axis=-1, keepdims=True)
    var = jnp.mean(jnp.square(y - mean), axis=-1, keepdims=True)
    y = ((y - mean) * lax.rsqrt(var + RWKV_GN_EPS)).reshape(bsz, t_len, RWKV_WIDTH) * _f32(ln_g) + _f32(ln_b)
    bonus = jnp.sum(r * k * _f32(r_k).reshape(RWKV_HEADS, RWKV_HEAD), axis=-1, keepdims=True) * v
    y = (y + bonus.reshape(bsz, t_len, RWKV_WIDTH)) * g
    return y, s_last, pf[:, -1:]


def pool_mixer(u, buf, pos0, w_pool, scale):
    bsz, t_len, _ = u.shape
    ext = jnp.concatenate([_f32(buf), _f32(u)], axis=1)
    cs = jnp.pad(jnp.cumsum(ext, axis=1), ((0, 0), (1, 0), (0, 0)))
    end = cs[:, POOL_BUF + 1:]
    pos = pos0 + jnp.arange(t_len)
    means = []
    for gi, win in enumerate(POOL_WINDOWS):
        sl = slice(gi * POOL_CH, (gi + 1) * POOL_CH)
        s = end[..., sl] - cs[:, POOL_BUF + 1 - win: POOL_BUF + 1 - win + t_len, sl]
        cnt = jnp.minimum(pos + 1, win).astype(jnp.float32)[None, :, None]
        means.append(s / cnt)
    pooled = jnp.concatenate(means, axis=-1) - ext[:, POOL_BUF:]
    y = jnp.einsum('btgc,gcd->btgd', pooled.reshape(bsz, t_len, len(POOL_WINDOWS), POOL_CH), _f32(w_pool))
    y = y.reshape(bsz, t_len, POOL_WIDTH) * _f32(scale)
    return y, ext[:, -POOL_BUF:]


def sqrelu_mlp(x, w_up, w_down):
    return jnp.square(jax.nn.relu(x @ w_up)) @ w_down


def trunk(x, ssm_re0, ssm_im0, hgrn0, wkv0, shift0, pool0, pos0, P):
    lb_all = jnp.cumsum(jax.nn.softmax(_f32(P['hgrn_lb_logits']), axis=0), axis=0)
    lb_all = lb_all - lb_all[:1]
    h = x
    n_re, n_im, n_hg, n_wkv, n_sh, n_pool = [], [], [], [], [], []
    for l in range(DEPTH):
        xn = rmsnorm(h, P['norm1_g'][l])
        proj = xn @ P['w_in'][l]
        p_ssm, p_q, p_f, p_i, p_g, p_rwkv, p_pool = jnp.split(proj, PROJ_SPLITS, axis=-1)
        y_a, s_re, s_im = s5_mixer(p_ssm, ssm_re0[l], ssm_im0[l], P['ssm_lambda_re'][l], P['ssm_lambda_im'][l],
                                   P['ssm_log_dt'][l], P['ssm_b_re'][l], P['ssm_b_im'][l], P['ssm_c_re'][l],
                                   P['ssm_c_im'][l], P['ssm_d'][l], P['ssm_glu_w'][l], P['ssm_glu_b'][l])
        y_b, s_hg = hgrn2_mixer(p_q, p_f, p_i, p_g, hgrn0[l], lb_all[l], P['hgrn_norm_g'][l])
        y_c, s_wkv, s_sh = rwkv7_mixer(p_rwkv, shift0[l], wkv0[l], P['rwkv_mu'][l], P['rwkv_w0'][l], P['rwkv_w2'][l],
                                       P['rwkv_a0'][l], P['rwkv_a2'][l], P['rwkv_g2'][l], P['rwkv_k_k'][l],
                                       P['rwkv_k_a'][l], P['rwkv_r_k'][l], P['rwkv_ln_g'][l], P['rwkv_ln_b'][l])
        y_d, s_pool = pool_mixer(p_pool, pool0[l], pos0, P['pool_w'][l], P['pool_scale'][l])
        mixed = jnp.concatenate([y_a, y_b, y_c, y_d], axis=-1).astype(h.dtype)
        h = h + mixed @ P['w_out'][l]
        h = h + sqrelu_mlp(rmsnorm(h, P['norm2_g'][l]), P['mlp_up'][l], P['mlp_down'][l])
        n_re.append(s_re)
        n_im.append(s_im)
        n_hg.append(s_hg)
        n_wkv.append(s_wkv)
        n_sh.append(s_sh)
        n_pool.append(s_pool)
    y = rmsnorm(h, P['norm_f_g'])
    return y, (jnp.stack(n_re), jnp.stack(n_im), jnp.stack(n_hg), jnp.stack(n_wkv), jnp.stack(n_sh), jnp.stack(n_pool))


def setup_inputs(seed: int = 0) -> dict:
    key = jax.random.key(seed)
    keys = list(jax.random.split(key, 48))

    def nrm(shape, std):
        return std * jax.random.normal(keys.pop(), shape, jnp.float32)

    L = DEPTH
    d = {}
    d['x_prompt'] = nrm((BATCH, SEQ, D_MODEL), 1.0)
    d['x_sample'] = nrm((DEC_BATCH, DEC_SEQ, D_MODEL), 1.0)
    d['state_ssm_re'] = nrm((L, DEC_BATCH, SSM_GROUPS, SSM_STATE), 0.5)
    d['state_ssm_im'] = nrm((L, DEC_BATCH, SSM_GROUPS, SSM_STATE), 0.5)
    d['state_hgrn'] = nrm((L, DEC_BATCH, HGRN_HEADS, HGRN_HEAD, HGRN_HEAD), 0.5)
    d['state_wkv'] = nrm((L, DEC_BATCH, RWKV_HEADS, RWKV_HEAD, RWKV_HEAD), 0.3)
    d['state_shift'] = nrm((L, DEC_BATCH, 1, RWKV_PROJ), 1.0)
    d['state_pool'] = nrm((L, DEC_BATCH, POOL_BUF, POOL_WIDTH), 1.0)
    d['norm1_g'] = 1.0 + nrm((L, D_MODEL), 0.02)
    d['w_in'] = nrm((L, D_MODEL, PROJ_WIDTH), D_MODEL ** -0.5)
    n_idx = jnp.arange(SSM_STATE, dtype=jnp.float32)
    d['ssm_lambda_re'] = -0.5 + nrm((L, SSM_GROUPS, SSM_STATE), 0.01)
    d['ssm_lambda_im'] = jnp.pi * n_idx + nrm((L, SSM_GROUPS, SSM_STATE), 0.01)
    d['ssm_log_dt'] = jax.random.uniform(keys.pop(), (L, SSM_GROUPS), jnp.float32,
                                         minval=math.log(1e-3), maxval=math.log(1e-1))
    d['ssm_b_re'] = nrm((L, SSM_GROUPS, SSM_STATE, SSM_CH), (2 * SSM_CH) ** -0.5)
    d['ssm_b_im'] = nrm((L, SSM_GROUPS, SSM_STATE, SSM_CH), (2 * SSM_CH) ** -0.5)
    d['ssm_c_re'] = nrm((L, SSM_GROUPS, SSM_CH, SSM_STATE), 1.0)
    d['ssm_c_im'] = nrm((L, SSM_GROUPS, SSM_CH, SSM_STATE), 1.0)
    d['ssm_d'] = nrm((L, SSM_WIDTH), 1.0)
    d['ssm_glu_w'] = nrm((L, SSM_WIDTH, SSM_WIDTH), SSM_WIDTH ** -0.5)
    d['ssm_glu_b'] = nrm((L, SSM_WIDTH), 0.01)
    d['hgrn_lb_logits'] = nrm((L, HGRN_WIDTH), 0.1)
    d['hgrn_norm_g'] = 1.0 + nrm((L, HGRN_WIDTH), 0.02)
    d['rwkv_mu'] = jax.random.uniform(keys.pop(), (L, RWKV_PROJ), jnp.float32)
    ratio = jnp.arange(RWKV_WIDTH, dtype=jnp.float32) / (RWKV_WIDTH - 1)
    d['rwkv_w0'] = (-7.0 + 5.0 * ratio ** 0.85 + 0.5) + nrm((L, RWKV_WIDTH), 0.1)
    d['rwkv_w2'] = nrm((L, DECAY_LORA, RWKV_WIDTH), 0.1 * DECAY_LORA ** -0.5)
    d['rwkv_a0'] = nrm((L, RWKV_WIDTH), 0.1)
    d['rwkv_a2'] = nrm((L, AAA_LORA, RWKV_WIDTH), AAA_LORA ** -0.5)
    d['rwkv_g2'] = nrm((L, GATE_LORA, RWKV_WIDTH), GATE_LORA ** -0.5)
    d['rwkv_k_k'] = 0.85 + nrm((L, RWKV_WIDTH), 0.02)
    d['rwkv_k_a'] = 1.0 + nrm((L, RWKV_WIDTH), 0.02)
    d['rwkv_r_k'] = nrm((L, RWKV_WIDTH), 0.1)
    d['rwkv_ln_g'] = 1.0 + nrm((L, RWKV_WIDTH), 0.02)
    d['rwkv_ln_b'] = nrm((L, RWKV_WIDTH), 0.01)
    d['pool_w'] = nrm((L, len(POOL_WINDOWS), POOL_CH, POOL_CH), POOL_CH ** -0.5)
    d['pool_scale'] = 1.0 + nrm((L, POOL_WIDTH), 0.1)
    d['w_out'] = nrm((L, MIX_WIDTH, D_MODEL), MIX_WIDTH ** -0.5)
    d['norm2_g'] = 1.0 + nrm((L, D_MODEL), 0.02)
    d['mlp_up'] = nrm((L, D_MODEL, D_FF), D_MODEL ** -0.5)
    d['mlp_down'] = nrm((L, D_FF, D_MODEL), D_FF ** -0.5)
    d['norm_f_g'] = 1.0 + nrm((D_MODEL,), 0.02)
    return d


def reference(x_prompt, x_sample, state_ssm_re, state_ssm_im, state_hgrn, state_wkv, state_shift, state_pool,
              norm1_g, w_in, ssm_lambda_re, ssm_lambda_im, ssm_log_dt, ssm_b_re, ssm_b_im, ssm_c_re, ssm_c_im,
              ssm_d, ssm_glu_w, ssm_glu_b, hgrn_lb_logits, hgrn_norm_g, rwkv_mu, rwkv_w0, rwkv_w2, rwkv_a0,
              rwkv_a2, rwkv_g2, rwkv_k_k, rwkv_k_a, rwkv_r_k, rwkv_ln_g, rwkv_ln_b, pool_w, pool_scale, w_out,
              norm2_g, mlp_up, mlp_down, norm_f_g):
    P = dict(norm1_g=norm1_g, w_in=w_in, ssm_lambda_re=ssm_lambda_re, ssm_lambda_im=ssm_lambda_im,
             ssm_log_dt=ssm_log_dt, ssm_b_re=ssm_b_re, ssm_b_im=ssm_b_im, ssm_c_re=ssm_c_re, ssm_c_im=ssm_c_im,
             ssm_d=ssm_d, ssm_glu_w=ssm_glu_w, ssm_glu_b=ssm_glu_b, hgrn_lb_logits=hgrn_lb_logits,
             hgrn_norm_g=hgrn_norm_g, rwkv_mu=rwkv_mu, rwkv_w0=rwkv_w0, rwkv_w2=rwkv_w2, rwkv_a0=rwkv_a0,
             rwkv_a2=rwkv_a2, rwkv_g2=rwkv_g2, rwkv_k_k=rwkv_k_k, rwkv_k_a=rwkv_k_a, rwkv_r_k=rwkv_r_k,
             rwkv_ln_g=rwkv_ln_g, rwkv_ln_b=rwkv_ln_b, pool_w=pool_w, pool_scale=pool_scale, w_out=w_out,
             norm2_g=norm2_g, mlp_up=mlp_up, mlp_down=mlp_down, norm_f_g=norm_f_g)
    bp = x_prompt.shape[0]
    f32 = jnp.float32
    y_prompt, st_p = trunk(x_prompt,
                           jnp.zeros((DEPTH, bp, SSM_GROUPS, SSM_STATE), f32),
                           jnp.zeros((DEPTH, bp, SSM_GROUPS, SSM_STATE), f32),
                           jnp.zeros((DEPTH, bp, HGRN_HEADS, HGRN_HEAD, HGRN_HEAD), f32),
                           jnp.zeros((DEPTH, bp, RWKV_HEADS, RWKV_HEAD, RWKV_HEAD), f32),
                           jnp.zeros((DEPTH, bp, 1, RWKV_PROJ), f32),
                           jnp.zeros((DEPTH, bp, POOL_BUF, POOL_WIDTH), f32),
                           0, P)
    y_sample, st_s = trunk(x_sample, state_ssm_re, state_ssm_im, state_hgrn, state_wkv, state_shift, state_pool,
                           PAST_LEN, P)
    p_ssm_re, p_ssm_im, p_hgrn, p_wkv, p_shift, p_pool = st_p
    s_ssm_re, s_ssm_im, s_hgrn, s_wkv, s_shift, s_pool = st_s
    return (y_prompt, y_sample, p_ssm_re, p_ssm_im, p_hgrn, p_wkv, p_shift, p_pool,
            s_ssm_re, s_ssm_im, s_hgrn, s_wkv, s_shift, s_pool)
```

```python
import contextlib
import numpy as np
import concourse.bass as bass
import concourse.mybir as mybir
from concourse.bass_utils import run_bass_kernel_spmd

F32 = mybir.dt.float32
BF16 = mybir.dt.bfloat16
I32 = mybir.dt.int32
F32R = mybir.dt.float32r


_ARENA_R = []


def R(ap):
    assert ap.tensor.name == 'arena', ap.tensor.name
    return bass.AP(tensor=_ARENA_R[0], offset=ap.offset, ap=list(ap.ap))
AF = mybir.ActivationFunctionType
ALU = mybir.AluOpType

NCORES = 8
D = 1024
SEQ = 2048
NS = 16
TS = 4
NT = SEQ + NS * TS
DEPTH = 2
DFF = 4096
PROJ = 2560
TTS = [(0, 512), (512, 512), (1024, 512), (1536, 512), (2048, 64)]
NORM_EPS = 1e-6

ENABLE = dict(s5=True, hgrn=True, rwkv=True, pool=True)


class Sched:
    ENG = ['pe', 'act', 'dve', 'pool', 'sp']

    def __init__(self, nc, stack):
        self.nc = nc
        self.stack = stack
        self.ops = {e: [] for e in self.ENG}
        self.ecount = {e: 0 for e in self.ENG}
        self.dsem = {}
        self.last_w = {}
        self.readers = {}
        self.waited = {e: {} for e in self.ENG}
        self.nops = 0

    def op(self, eng, fn, reads=(), writes=(), dma=None):
        deps = []
        for k in reads:
            if k in self.last_w:
                deps.append(self.last_w[k])
        for k in writes:
            if k in self.last_w:
                deps.append(self.last_w[k])
            deps += self.readers.get(k, [])
        if dma is None:
            self.ecount[eng] += 1
            tok = (('e', eng), self.ecount[eng])
        else:
            d = self.dsem.setdefault(dma, [0])
            d[0] += 16
            tok = (('d', dma), d[0])
        need = {}
        for (s, v) in deps:
            if s[0] == 'd' and s != tok[0]:
                v = self.dsem[s[1]][0]
            if s == tok[0]:
                if s[0] == 'd':
                    continue
            need[s] = max(need.get(s, 0), v)
        waits = []
        for s, v in need.items():
            if self.waited[eng].get(s, 0) >= v:
                continue
            self.waited[eng][s] = v
            waits.append((s, v))
        self.ops[eng].append((fn, waits, tok))
        for k in reads:
            self.readers.setdefault(k, []).append(tok)
        for k in writes:
            self.last_w[k] = tok
            self.readers[k] = []
        self.nops += 1

    def barrier(self):
        snap_e = dict(self.ecount)
        snap_d = {k: v[0] for k, v in self.dsem.items()}
        for e in self.ENG:
            waits = []
            for f in self.ENG:
                v = snap_e[f]
                if v > 0 and self.waited[e].get(('e', f), 0) < v:
                    self.waited[e][('e', f)] = v
                    waits.append((('e', f), v))
            for k, v in snap_d.items():
                if v > 0 and self.waited[e].get(('d', k), 0) < v:
                    self.waited[e][('d', k)] = v
                    waits.append((('d', k), v))
            self.ecount[e] += 1
            self.ops[e].append((None, waits, (('e', e), self.ecount[e])))

    def emit(self):
        nc = self.nc
        sems = {}
        for e in self.ENG:
            sems[('e', e)] = self.stack.enter_context(nc.semaphore('s_' + e))
        for i, k in enumerate(self.dsem):
            sems[('d', k)] = self.stack.enter_context(nc.semaphore('d%d' % i))
        engobj = {'pe': nc.tensor, 'act': nc.scalar, 'dve': nc.vector, 'pool': nc.gpsimd, 'sp': nc.sync}

        def run(e):
            eng = engobj[e]
            for fn, waits, tok in self.ops[e]:
                for s, v in waits:
                    eng.wait_ge(sems[s], v)
                if fn is None:
                    if tok[0][0] == 'e':
                        eng.sem_inc(sems[tok[0]], 1)
                    continue
                inst = fn(eng)
                if tok[0][0] == 'e':
                    inst.then_inc(sems[tok[0]], 1)
                else:
                    inst.then_inc(sems[tok[0]], 16)

        with nc.Block() as block:
            @block.tensor
            def _(x):
                run('pe')

            @block.scalar
            def _(x):
                run('act')

            @block.vector
            def _(x):
                run('dve')

            @block.gpsimd
            def _(x):
                run('pool')

            @block.sync
            def _(x):
                run('sp')


class Builder:
    def __init__(self):
        self.nc = bass.Bass('TRN2', target_bir_lowering=False)
        self.uid = 0

    def dram_in(self, name, shape):
        return self.nc.dram_tensor(name, list(shape), F32, kind='ExternalInput').ap()

    def dram_out(self, name, shape):
        return self.nc.dram_tensor(name, list(shape), F32, kind='ExternalOutput').ap()

    def sb(self, name, shape, dt=F32):
        return self.st.enter_context(self.nc.sbuf_tensor(name, list(shape), dt))

    def newkey(self, base):
        self.uid += 1
        return (base, self.uid)

    def pn(self):
        i = self.pcur
        self.pcur = (self.pcur + 1) % 6
        return self.pbanks[i], ('ps', i)

    def build(self):
        nc = self.nc
        di = self.dram_in
        self.x = di('x', [NT, D])
        self.st_ssm_re = di('state_ssm_re', [DEPTH, NS, 1024])
        self.st_ssm_im = di('state_ssm_im', [DEPTH, NS, 1024])
        self.st_hgrn = di('state_hgrn', [DEPTH, NS, 4, 64, 64])
        self.st_wkv = di('state_wkv', [DEPTH, NS, 4, 64, 64])
        self.st_shift = di('state_shift', [DEPTH, NS, 1024])
        self.st_pool = di('state_pool', [DEPTH, NS, 15, 256])
        self.P = {}
        for name, shape in PARAM_SHAPES.items():
            self.P[name] = di(name, shape)
        do = self.dram_out
        self.o_y = do('o_y', [NT, D])
        self.o_p_ssm_re = do('o_p_ssm_re', [DEPTH, 1024])
        self.o_p_ssm_im = do('o_p_ssm_im', [DEPTH, 1024])
        self.o_p_hgrn = do('o_p_hgrn', [DEPTH, 4, 64, 64])
        self.o_p_wkv = do('o_p_wkv', [DEPTH, 4, 64, 64])
        self.o_p_shift = do('o_p_shift', [DEPTH, 1024])
        self.o_p_pool = do('o_p_pool', [DEPTH, 15, 256])
        self.o_s_ssm_re = do('o_s_ssm_re', [DEPTH, NS, 1024])
        self.o_s_ssm_im = do('o_s_ssm_im', [DEPTH, NS, 1024])
        self.o_s_hgrn = do('o_s_hgrn', [DEPTH, NS, 4, 64, 64])
        self.o_s_wkv = do('o_s_wkv', [DEPTH, NS, 4, 64, 64])
        self.o_s_shift = do('o_s_shift', [DEPTH, NS, 1024])
        self.o_s_pool = do('o_s_pool', [DEPTH, NS, 15, 256])
        self.outkeys = []
        with contextlib.ExitStack() as st:
            self.st = st
            self.S = Sched(nc, st)
            self.pbanks = [st.enter_context(nc.psum_tensor('pb%d' % i, [128, 512], F32)) for i in range(8)]
            self.pcur = 0
            self.alloc()
            self.consts()
            self.consts2()
            self.load_x()
            for l in range(DEPTH):
                self.layer(l)
            self.final()
            self.S.op('sp', None, reads=list(self.outkeys))
            with nc.allow_non_contiguous_dma(reason='small strided param/state transfers'):
                self.S.emit()
        return nc

    def alloc(self):
        sb = self.sb
        self.hT = sb('hT', [128, 8, NT], F32)
        self.xn = sb('xn', [128, 8, NT], BF16)
        self.ymix = sb('ymix', [128, 2, 512], BF16)
        self.hid = sb('hid', [128, 8, 512], BF16)
        self.wslot = [sb('wslot%d' % i, [128, 10240], BF16) for i in range(2)]
        self.ident = sb('ident', [128, 128], F32)
        self.ones_bf = sb('ones_bf', [128, 128], BF16)
        self.rstd = sb('rstd', [128, 512], F32)
        self.stage = [sb('stage%d' % i, [128, 1024], F32) for i in range(2)]
        self.stage_i = 0
        self.pc = {}
        for name, n in PCOLS.items():
            self.pc[name] = sb('pc_' + name, [128, DEPTH, n // 128], F32)
        self.pc_normf = sb('pc_normf', [128, 8], F32)
        AW = 7400
        cm = self.nc.sbuf_tensor('arena', [128, AW], F32)
        self.arena = cm.__enter__()
        cm.__exit__(None, None, None)
        arena_r = sb('arena_r', [128, AW], F32R)
        del _ARENA_R[:]
        _ARENA_R.append(arena_r)
        ar = self.arena

        def av(off, shape):
            n = int(np.prod(shape[1:]))
            v = ar[0:shape[0], off:off + n]
            if len(shape) == 3:
                v = v.rearrange('p (a b) -> p a b', a=shape[1])
            elif len(shape) == 4:
                v = v.rearrange('p (a b c) -> p a b c', a=shape[1], b=shape[2])
            return v
        self.av = av
        self.pext = av(0, [128, 2, 528])
        self.ps2 = av(1056, [128, 2, 528])
        self.ps4 = av(2112, [128, 2, 528])
        self.pooled = av(3168, [128, 2, 512])
        self.pext_s = av(4192, [128, 2, 16, 20])
        self.ps2_s = av(4832, [128, 2, 16, 20])
        self.ps4_s = av(5472, [128, 2, 16, 20])
        self.u_s = av(6112, [128, 2, 64])
        self.xf = av(0, [128, 8, 512])
        self.poolw = sb('poolw', [128, DEPTH, 2, 128], F32)
        self.pool_tab = sb('pool_tab', [128, 2, 16], F32)
        self.eps_col = sb('eps_col', [128, 1], F32)
        self.shp = sb('shp', [128, 8], F32)
        self.shs = sb('shs', [128, 8, 16], F32)
        self.s5_bbT = sb('s5_bbT', [128, 2, 8, 128], BF16)
        self.s5_cx = sb('s5_cx', [128, 2, 8, 128], BF16)
        self.s5_gluw = sb('s5_gluw', [128, 2, 256], BF16)
        self.idxf = sb('idxf', [128, 129], F32)
        self.ones_f = sb('ones_f', [128, 128], F32)
        self.rmask = sb('rmask', [128, 64], F32)
        self.s5_carry = sb('s5_carry', [128, 8, 2], F32)
        self.s5_hl = sb('s5_hl', [128, 2, 8], F32)
        self.s5_hls = sb('s5_hls', [128, 2, 8, 16], F32)

    def consts(self):
        S = self.S
        ident, ones_bf = self.ident, self.ones_bf
        S.op('pool', lambda e: e.memset(ident[:], 1.0), writes=['ident'])
        S.op('pool', lambda e: e.affine_select(out=ident[:], in_=ident[:], pattern=[[-1, 128]],
                                               compare_op=ALU.is_equal, fill=0.0, base=0, channel_multiplier=1),
             reads=['ident'], writes=['ident'])
        S.op('pool', lambda e: e.memset(ones_bf[:], 1.0), writes=['ones_bf'])
        S.op('pool', lambda e: e.memset(self.ones_f[:], 1.0), writes=['ones_f'])
        S.op('pool', lambda e: e.memset(self.rmask[:], 1.0), writes=['rmask'])
        S.op('pool', lambda e: e.memset(self.rmask[:].rearrange('p (s t) -> p s t', t=4)[:, :, 0:1], 0.0), writes=['rmask'])
        S.op('pool', lambda e: e.iota(self.idxf[:], pattern=[[1, 129]], base=0, channel_multiplier=0,
                                      allow_small_or_imprecise_dtypes=True), writes=['idxf'])
        with self.nc.allow_non_contiguous_dma(reason='tiny param column loads'):
            for name, n in PCOLS.items():
                t = self.pc[name]
                for l in range(DEPTH):
                    src = self.P[name][l].rearrange('(t p) -> p t', p=128)
                    S.op('sp', (lambda e, t=t, l=l, src=src: e.dma_start(out=t[:, l, :], in_=src)),
                         writes=['pc'], dma='pc')
            src = self.P['norm_f_g'].rearrange('(t p) -> p t', p=128)
            S.op('sp', lambda e: e.dma_start(out=self.pc_normf[:], in_=src), writes=['pc'], dma='pc')
        pw = self.poolw
        S.op('pool', lambda e: e.memset(pw[:], 0.0), writes=['poolw'])
        for l in range(DEPTH):
            for g in range(4):
                t, hf = g // 2, g % 2
                S.op('sp', (lambda e, l=l, g=g, t=t, hf=hf: e.dma_start(
                    out=pw[hf * 64:(hf + 1) * 64, l, t, hf * 64:(hf + 1) * 64], in_=self.P['pool_w'][l, g])),
                    reads=[], writes=['poolw'], dma='poolw')
        tab = self.pool_tab
        wins = [2, 4, 8, 16]
        for g in range(4):
            t, hf = g // 2, g % 2
            w = wins[g]
            S.op('pool', (lambda e, t=t, hf=hf, w=w: e.memset(tab[hf * 64:(hf + 1) * 64, t, :], 1.0 / w)),
                 writes=['pool_tab'])
            for c in range(w - 1):
                S.op('pool', (lambda e, t=t, hf=hf, c=c: e.memset(tab[hf * 64:(hf + 1) * 64, t, c:c + 1], 1.0 / (c + 1))),
                     writes=['pool_tab'])

    def consts2(self):
        self.gla_consts()
        self.rwkv_consts()

    def next_stage(self):
        i = self.stage_i
        self.stage_i ^= 1
        return self.stage[i], ('stage', i)

    def load_x(self):
        S = self.S
        hT, ident = self.hT, self.ident
        for i in range(17):
            n = 128 if i < 16 else 64
            c0 = i * 128
            stg, sk = self.next_stage()
            S.op('sp', (lambda e, stg=stg, c0=c0, n=n: e.dma_start(out=stg[:n, :], in_=self.x[c0:c0 + n, :])),
                 writes=[sk], dma=sk)
            for half in range(2):
                pb, pk = self.pn()

                def f(e, stg=stg, pb=pb, n=n, half=half):
                    r = None
                    for j in range(4):
                        ft = half * 4 + j
                        r = e.transpose(out=pb[:, j * 128:j * 128 + n], in_=stg[:n, ft * 128:(ft + 1) * 128],
                                        identity=ident[:n, :n])
                    return r
                S.op('pe', f, reads=[sk, 'ident'], writes=[pk])
                eng = 'act' if half == 0 else 'dve'

                def g(e, pb=pb, n=n, half=half, c0=c0, eng=eng):
                    src = pb[:].rearrange('p (j c) -> p j c', j=4)[:, :, :n]
                    dst = hT[:, half * 4:half * 4 + 4, c0:c0 + n]
                    if eng == 'act':
                        return e.activation(out=dst, in_=src, func=AF.Copy)
                    return e.tensor_copy(out=dst, in_=src)
                S.op(eng, g, reads=[pk], writes=[('h', i // 4 if i < 16 else 4)])

    def rmsnorm(self, ti, gcol, out_fn, hkey_extra=()):
        S = self.S
        c0, n = TTS[ti]
        hT, sq, rstd = self.hT, self.hid, self.rstd
        hk = ('h', ti)
        for kt in range(8):
            S.op('act', (lambda e, kt=kt: e.activation(out=sq[:, kt, :n], in_=hT[:, kt, c0:c0 + n], func=AF.Square)),
                 reads=[hk], writes=[('hid', kt)])
        pb, pk = self.pn()

        def f(e):
            r = None
            for kt in range(8):
                r = e.matmul(pb[:, :n], lhsT=self.ones_bf[:], rhs=sq[:, kt, :n], start=(kt == 0), stop=(kt == 7))
            return r
        S.op('pe', f, reads=[('hid', kt) for kt in range(8)] + ['ones_bf'], writes=[pk])
        S.op('act', lambda e: e.activation(out=rstd[:, :n], in_=pb[:, :n], func=AF.Ln, scale=1.0 / D, bias=self.eps_col[:]),
             reads=[pk, 'eps'], writes=['rstd'])
        S.op('act', lambda e: e.activation(out=rstd[:, :n], in_=rstd[:, :n], func=AF.Exp, scale=-0.5), reads=['rstd'], writes=['rstd'])
        for kt in range(8):
            dst, wk = out_fn(kt, n)
            S.op('dve', (lambda e, kt=kt, dst=dst: e.scalar_tensor_tensor(
                out=dst, in0=hT[:, kt, c0:c0 + n], scalar=gcol(kt), in1=rstd[:, :n], op0=ALU.mult, op1=ALU.mult)),
                reads=[hk, 'rstd', 'pc'], writes=[wk])

    def load_weights(self, slot, parts):
        S = self.S
        for dst, src in parts:
            S.op('pool', (lambda e, dst=dst, src=src: e.dma_start(out=dst, in_=src)),
                 writes=[('w', slot)], dma=('w', slot))

    def issue_weights(self, k):
        if k >= DEPTH * 12 or k in self.wissued:
            return
        l, r = k // 12, k % 12
        slot = k % 2
        if r < 4:
            v = self.mixer_weights(l, r, slot)
        else:
            v = self.mlp_weights(l, r - 4, slot)
        self.wissued[k] = (slot, v)

    def get_weights(self, k):
        self.issue_weights(k)
        self.issue_weights(k + 1)
        return self.wissued[k]

    def mixer_weights(self, l, m, slot):
        c0, nc_ = MIXER_COLS[m]
        ws = self.wslot[slot]
        win = ws[:, 0:8 * nc_].rearrange('p (k c) -> p k c', k=8)
        wout = ws[:, 8192:8192 + 2048].rearrange('p (k c) -> p k c', k=2)
        parts = []
        src = self.P['w_in'][l].rearrange('(k p) c -> p k c', p=128)
        for cc in range(0, nc_, 512):
            w_ = min(512, nc_ - cc)
            parts.append((win[:, :, cc:cc + w_], src[:, :, c0 + cc:c0 + cc + w_]))
        srco = self.P['w_out'][l][m * 256:(m + 1) * 256, :].rearrange('(k p) c -> p k c', p=128)
        parts.append((wout, srco))
        self.load_weights(slot, parts)
        return win, wout

    def mlp_weights(self, l, e8, slot):
        ws = self.wslot[slot]
        wup = ws[:, 0:4096].rearrange('p (k c) -> p k c', k=8)
        wdn = ws[:, 4096:8192].rearrange('p (k c) -> p k c', k=4)
        srcu = self.P['mlp_up'][l].rearrange('(k p) c -> p k c', p=128)[:, :, e8 * 512:(e8 + 1) * 512]
        srcd = self.P['mlp_down'][l][e8 * 512:(e8 + 1) * 512, :].rearrange('(k p) c -> p k c', p=128)
        self.load_weights(slot, [(wup, srcu), (wdn, srcd)])
        return wup, wdn

    def proj(self, win, j, ti, slot):
        S = self.S
        c0, n = TTS[ti]
        pb, pk = self.pn()
        xn = self.xn

        def f(e):
            r = None
            for kt in range(8):
                r = e.matmul(pb[:, :n], lhsT=win[:, kt, j * 128:(j + 1) * 128], rhs=xn[:, kt, c0:c0 + n],
                             start=(kt == 0), stop=(kt == 7))
            return r
        S.op('pe', f, reads=[('w', slot), ('xn', ti)], writes=[pk])
        return pb, pk

    def apply_wout(self, wout, ti, slot):
        S = self.S
        c0, n = TTS[ti]
        hT, ymix = self.hT, self.ymix
        for o in range(8):
            pb, pk = self.pn()

            def f(e, pb=pb, o=o):
                r = None
                for k in range(2):
                    r = e.matmul(pb[:, :n], lhsT=wout[:, k, o * 128:(o + 1) * 128], rhs=ymix[:, k, :n],
                                 start=(k == 0), stop=(k == 1))
                return r
            S.op('pe', f, reads=[('w', slot), 'ymix'], writes=[pk])
            S.op('dve', (lambda e, pb=pb, o=o: e.tensor_tensor(out=hT[:, o, c0:c0 + n], in0=hT[:, o, c0:c0 + n],
                                                               in1=pb[:, :n], op=ALU.add)),
                 reads=[pk, ('h', ti)], writes=[('h', ti)])

    def layer(self, l):
        S = self.S
        for ti in range(5):
            c0, n = TTS[ti]
            self.rmsnorm(ti, lambda kt: self.pc['norm1_g'][:, l, kt:kt + 1],
                         lambda kt, n, c0=c0, ti=ti: (self.xn[:, kt, c0:c0 + n], ('xn', ti)))
        for m in range(4):
            S.barrier()
            slot, (win, wout) = self.get_weights(l * 12 + m)
            self._wout = wout
            for ti in range(5):
                fn = [self.mix_s5, self.mix_hgrn, self.mix_rwkv, self.mix_pool][m]
                if m in (1, 2):
                    fn(l, ti, win, slot)
                else:
                    fn(l, ti, win, slot)
                    self.apply_wout(wout, ti, slot)
        S.barrier()
        for ti in range(5):
            c0, n = TTS[ti]
            self.rmsnorm(ti, lambda kt: self.pc['norm2_g'][:, l, kt:kt + 1],
                         lambda kt, n, c0=c0, ti=ti: (self.xn[:, kt, c0:c0 + n], ('xn', ti)))
        hT, hid, xn = self.hT, self.hid, self.xn
        for e8 in range(8):
            slot, (wup, wdn) = self.get_weights(l * 12 + 4 + e8)
            for ti in range(5):
                c0, n = TTS[ti]
                hb = 4 * (ti % 2)
                for f4 in range(4):
                    pb, pk = self.pn()

                    def f(e, pb=pb, f4=f4, c0=c0, n=n, wup=wup):
                        r = None
                        for kt in range(8):
                            r = e.matmul(pb[:, :n], lhsT=wup[:, kt, f4 * 128:(f4 + 1) * 128], rhs=xn[:, kt, c0:c0 + n],
                                         start=(kt == 0), stop=(kt == 7))
                        return r
                    S.op('pe', f, reads=[('w', slot), ('xn', ti)], writes=[pk])
                    rl = self.av(512 * (f4 % 2), [128, 512])
                    rk = ('mlprl', f4 % 2)
                    S.op('act', (lambda e, pb=pb, n=n, rl=rl: e.activation(out=rl[:, :n], in_=pb[:, :n], func=AF.Relu)),
                         reads=[pk], writes=[rk])
                    S.op('dve', (lambda e, f4=f4, n=n, rl=rl, hb=hb: e.tensor_tensor(out=hid[:, hb + f4, :n], in0=rl[:, :n], in1=rl[:, :n], op=ALU.mult)),
                         reads=[rk], writes=[('hid', hb + f4)])
                for o in range(8):
                    pb, pk = self.pn()

                    def f(e, pb=pb, o=o, n=n, wdn=wdn, hb=hb):
                        r = None
                        for k in range(4):
                            r = e.matmul(pb[:, :n], lhsT=wdn[:, k, o * 128:(o + 1) * 128], rhs=hid[:, hb + k, :n],
                                         start=(k == 0), stop=(k == 3))
                        return r
                    S.op('pe', f, reads=[('w', slot)] + [('hid', hb + k) for k in range(4)], writes=[pk])
                    eng = 'dve' if o % 2 == 0 else 'dve'
                    S.op(eng, (lambda e, pb=pb, o=o, c0=c0, n=n: e.tensor_tensor(
                        out=hT[:, o, c0:c0 + n], in0=hT[:, o, c0:c0 + n], in1=pb[:, :n], op=ALU.add)),
                        reads=[pk, ('h', ti)], writes=[('h', ti)])

    def zero_ymix(self, ti):
        c0, n = TTS[ti]
        self.S.op('pool', lambda e: e.memset(self.ymix[:], 0.0), writes=['ymix'])

    def s5_setup(self, l):
        S, av = self.S, self.av
        T = 4112
        self.Er = av(0, [128, 8, 129])
        self.Ei = av(1032, [128, 8, 129])
        self.Qr = av(2064, [128, 8, 128])
        self.Qi = av(3088, [128, 8, 128])
        self.Xr = av(T, [128, 512])
        self.Xi = av(T + 512, [128, 512])
        self.T1 = av(T + 1024, [128, 512])
        self.T2 = av(T + 1536, [128, 512])
        self.ysb = av(T + 2048, [128, 2, 512])
        Er, Ei, Qr, Qi = self.Er, self.Ei, self.Qr, self.Qi
        T0 = av(T, [128, 8, 129])
        T2s = av(T + 1032, [128, 8, 129])
        small = av(7208, [128, 16, 8])
        self.s5_small = small
        lr, li, ldt, dt, a, th, cr, ci, den, zr, tA, tB = [small[:, i, :] for i in range(12)]
        Bre = av(T + 2064, [128, 8, 16])
        Bim = av(T + 2192, [128, 8, 16])
        bbr = av(T + 2320, [128, 8, 16])
        bbi = av(T + 2448, [128, 8, 16])
        tC = av(T + 2576, [128, 8, 16])
        self.h0r = av(T + 64, [128, 8, 16])
        self.h0i = av(T + 320, [128, 8, 16])
        self.injr = av(T + 576, [128, 8, 16])
        self.inji = av(T + 832, [128, 8, 16])
        self.tC = av(T + 1088, [128, 8, 16])
        BX = av(T, [128, 4, 128])
        CS = av(T + 512, [128, 2, 128])
        K = 's5sc'
        P = self.P
        S.op('sp', lambda e: e.dma_start(out=lr, in_=P['ssm_lambda_re'][l].rearrange('(t g) p -> (g p) t', g=2)), writes=[K], dma='s5ld')
        S.op('sp', lambda e: e.dma_start(out=li, in_=P['ssm_lambda_im'][l].rearrange('(t g) p -> (g p) t', g=2)), writes=[K], dma='s5ld')
        for g2 in range(2):
            S.op('sp', (lambda e, g2=g2: e.dma_start(out=ldt[g2 * 64:(g2 + 1) * 64, :],
                                                     in_=P['ssm_log_dt'][l].rearrange('(t g) -> g t', g=2)[g2].partition_broadcast(64))),
                 writes=[K], dma='s5ld')
        S.op('sp', lambda e: e.dma_start(out=Bre, in_=P['ssm_b_re'][l].rearrange('(t g) p h -> (g p) t h', g=2)), writes=[K], dma='s5ld')
        S.op('sp', lambda e: e.dma_start(out=Bim, in_=P['ssm_b_im'][l].rearrange('(t g) p h -> (g p) t h', g=2)), writes=[K], dma='s5ld')
        S.op('pool', lambda e: e.dma_start(out=self.s5_gluw[:], in_=P['ssm_glu_w'][l].rearrange('(k p) c -> p k c', p=128)),
             writes=['s5gluw'], dma='s5gluw')
        TT = ALU
        bc3 = lambda v: v.unsqueeze(2).to_broadcast([128, 8, 129])
        idx3 = self.idxf[:].unsqueeze(1).to_broadcast([128, 8, 129])
        S.op('act', lambda e: e.activation(out=dt, in_=ldt, func=AF.Exp), reads=[K], writes=[K])
        S.op('dve', lambda e: e.tensor_tensor(out=a, in0=lr, in1=dt, op=TT.mult), reads=[K], writes=[K])
        S.op('dve', lambda e: e.tensor_tensor(out=th, in0=li, in1=dt, op=TT.mult), reads=[K], writes=[K])
        S.op('dve', lambda e: e.tensor_scalar(out=tB, in0=th, scalar1=1.0 / (2 * np.pi), scalar2=None, op0=TT.mult), reads=[K], writes=[K])
        for t8 in range(8):
            S.op('dve', (lambda e, t8=t8: e.tensor_scalar(out=T0[:, t8, :], in0=self.idxf[:, :], scalar1=tB[:, t8:t8 + 1], scalar2=None, op0=TT.mult)),
                 reads=[K, 'idxf'], writes=[K])
        TWO_PI = 6.28318
        def sincos(dst):
            S.op('dve', lambda e: e.tensor_scalar(out=T2s, in0=T0, scalar1=12582912.0, scalar2=None, op0=TT.add), reads=[K], writes=[K])
            S.op('dve', lambda e: e.tensor_scalar(out=T2s, in0=T2s, scalar1=-12582912.0, scalar2=None, op0=TT.add), reads=[K], writes=[K])
            S.op('dve', lambda e: e.tensor_tensor(out=T2s, in0=T0, in1=T2s, op=TT.subtract), reads=[K], writes=[K])
            S.op('act', lambda e: e.activation(out=dst, in_=T2s, func=AF.Sin, scale=TWO_PI), reads=[K], writes=[K])
        sincos(Ei)
        S.op('dve', lambda e: e.tensor_scalar(out=T0, in0=T0, scalar1=0.25, scalar2=None, op0=TT.add), reads=[K], writes=[K])
        sincos(Er)
        for t8 in range(8):
            S.op('dve', (lambda e, t8=t8: e.tensor_scalar(out=T0[:, t8, :], in0=self.idxf[:, :], scalar1=a[:, t8:t8 + 1], scalar2=None, op0=TT.mult)),
                 reads=[K, 'idxf'], writes=[K])
        S.op('act', lambda e: e.activation(out=T2s, in_=T0, func=AF.Exp, scale=-1.0), reads=[K], writes=[K])
        S.op('act', lambda e: e.activation(out=T0, in_=T0, func=AF.Exp), reads=[K], writes=[K])
        S.op('dve', lambda e: e.tensor_tensor(out=Qr, in0=T2s[:, :, 0:128], in1=Er[:, :, 0:128], op=TT.mult), reads=[K], writes=[K])
        S.op('dve', lambda e: e.scalar_tensor_tensor(out=Qi, in0=T2s[:, :, 0:128], scalar=-1.0, in1=Ei[:, :, 0:128], op0=TT.mult, op1=TT.mult),
             reads=[K], writes=[K])
        S.op('dve', lambda e: e.tensor_tensor(out=Er, in0=T0, in1=Er, op=TT.mult), reads=[K], writes=[K])
        S.op('dve', lambda e: e.tensor_tensor(out=Ei, in0=T0, in1=Ei, op=TT.mult), reads=[K], writes=[K])
        E1r, E1i = Er[:, :, 1], Ei[:, :, 1]
        S.op('dve', lambda e: e.tensor_tensor(out=tA, in0=lr, in1=lr, op=TT.mult), reads=[K], writes=[K])
        S.op('dve', lambda e: e.tensor_tensor(out=den, in0=li, in1=li, op=TT.mult), reads=[K], writes=[K])
        S.op('dve', lambda e: e.tensor_tensor(out=den, in0=den, in1=tA, op=TT.add), reads=[K], writes=[K])
        S.op('dve', lambda e: e.reciprocal(out=den, in_=den), reads=[K], writes=[K])
        S.op('dve', lambda e: e.tensor_scalar(out=zr, in0=E1r, scalar1=-1.0, scalar2=None, op0=TT.add), reads=[K], writes=[K])
        S.op('dve', lambda e: e.tensor_tensor(out=tA, in0=zr, in1=lr, op=TT.mult), reads=[K], writes=[K])
        S.op('dve', lambda e: e.tensor_tensor(out=cr, in0=E1i, in1=li, op=TT.mult), reads=[K], writes=[K])
        S.op('dve', lambda e: e.tensor_tensor(out=cr, in0=cr, in1=tA, op=TT.add), reads=[K], writes=[K])
        S.op('dve', lambda e: e.tensor_tensor(out=cr, in0=cr, in1=den, op=TT.mult), reads=[K], writes=[K])
        S.op('dve', lambda e: e.tensor_tensor(out=tA, in0=zr, in1=li, op=TT.mult), reads=[K], writes=[K])
        S.op('dve', lambda e: e.tensor_tensor(out=ci, in0=E1i, in1=lr, op=TT.mult), reads=[K], writes=[K])
        S.op('dve', lambda e: e.tensor_tensor(out=ci, in0=ci, in1=tA, op=TT.subtract), reads=[K], writes=[K])
        S.op('dve', lambda e: e.tensor_tensor(out=ci, in0=ci, in1=den, op=TT.mult), reads=[K], writes=[K])
        b16 = lambda v: v.unsqueeze(2).to_broadcast([128, 8, 16])
        S.op('dve', lambda e: e.tensor_tensor(out=bbr, in0=Bre, in1=b16(cr), op=TT.mult), reads=[K], writes=[K])
        S.op('dve', lambda e: e.tensor_tensor(out=tC, in0=Bim, in1=b16(ci), op=TT.mult), reads=[K], writes=[K])
        S.op('dve', lambda e: e.tensor_tensor(out=bbr, in0=bbr, in1=tC, op=TT.subtract), reads=[K], writes=[K])
        S.op('dve', lambda e: e.tensor_tensor(out=bbi, in0=Bim, in1=b16(cr), op=TT.mult), reads=[K], writes=[K])
        S.op('dve', lambda e: e.tensor_tensor(out=tC, in0=Bre, in1=b16(ci), op=TT.mult), reads=[K], writes=[K])
        S.op('dve', lambda e: e.tensor_tensor(out=bbi, in0=bbi, in1=tC, op=TT.add), reads=[K], writes=[K])
        for reim, bb in enumerate([bbr, bbi]):
            for tg in range(2):
                S.op('pool', lambda e: e.memset(BX, 0.0), reads=[K], writes=[K])
                for t4 in range(4):
                    for g2 in range(2):
                        S.op('pool', (lambda e, t4=t4, g2=g2, bb=bb, tg=tg: e.tensor_copy(
                            out=BX[g2 * 64:(g2 + 1) * 64, t4, 32 * t4 + 16 * g2:32 * t4 + 16 * g2 + 16],
                            in_=bb[g2 * 64:(g2 + 1) * 64, tg * 4 + t4, :])), reads=[K], writes=[K])
                pb, pk = self.pn()

                def f(e, pb=pb):
                    r = None
                    for t4 in range(4):
                        r = e.transpose(out=pb[:, t4 * 128:(t4 + 1) * 128], in_=BX[:, t4, :], identity=self.ident[:, :])
                    return r
                S.op('pe', f, reads=[K, 'ident'], writes=[pk])
                S.op('act', (lambda e, pb=pb, reim=reim, tg=tg: e.activation(
                    out=self.s5_bbT[:, reim, tg * 4:(tg + 1) * 4, :], in_=pb[:].rearrange('p (a b) -> p a b', a=4), func=AF.Copy)),
                    reads=[pk], writes=['s5bbT'])
        S.op('pool', lambda e: e.memset(self.s5_cx[:], 0.0), writes=['s5cx'])
        for reim, nm in enumerate(['ssm_c_re', 'ssm_c_im']):
            S.op('pool', lambda e: e.memset(CS, 0.0), reads=[K], writes=[K])
            for t in range(8):
                for g2 in range(2):
                    r0 = (t % 4) * 32 + g2 * 16
                    S.op('sp', (lambda e, t=t, g2=g2, r0=r0, nm=nm: e.dma_start(
                        out=CS[r0:r0 + 16, t // 4, g2 * 64:(g2 + 1) * 64], in_=P[nm][l, 2 * t + g2])),
                        reads=[], writes=[K], dma='s5ld')
            pb, pk = self.pn()

            def f(e, pb=pb):
                r = None
                for tg in range(2):
                    r = e.transpose(out=pb[:, tg * 128:(tg + 1) * 128], in_=CS[:, tg, :], identity=self.ident[:, :])
                return r
            S.op('pe', f, reads=[K, 'ident'], writes=[pk])
            for tg in range(2):
                for t4 in range(4):
                    S.op('act', (lambda e, pb=pb, tg=tg, t4=t4, reim=reim: e.activation(
                        out=self.s5_cx[:, reim, tg * 4 + t4, 32 * t4:32 * t4 + 32],
                        in_=pb[:, tg * 128 + 32 * t4:tg * 128 + 32 * t4 + 32], func=AF.Copy, scale=(1.0 if reim == 0 else -1.0))),
                        reads=[pk], writes=['s5cx'])
        S.barrier()

    def mix_s5(self, l, ti, win, slot):
        if not ENABLE['s5']:
            return self.zero_ymix(ti)
        S = self.S
        TT = ALU
        if ti == 0:
            self.s5_setup(l)
            S.op('pool', lambda e: e.memset(self.s5_carry[:], 0.0), writes=['carry'])
        c0, n = TTS[ti]
        CW = 128 if ti < 4 else 4
        NCH = n // CW
        hid = self.hid
        Er, Ei, Qr, Qi = self.Er, self.Ei, self.Qr, self.Qi
        Xr, Xi, T1, T2, ysb = self.Xr, self.Xi, self.T1, self.T2, self.ysb
        carry = self.s5_carry
        yacc = [(self.pbanks[6], ('ps', 6)), (self.pbanks[7], ('ps', 7))]
        v3 = lambda ap: ap[:, :n].rearrange('p (c j) -> p c j', j=CW)
        for j in range(2):
            pb, pk = self.proj(win, j, ti, slot)
            S.op('act', (lambda e, pb=pb, j=j: e.activation(out=hid[:, j, :n], in_=pb[:, :n], func=AF.Copy)),
                 reads=[pk], writes=[('hid', j)])
        if ti == 4:
            S.barrier()
            for reim, (src, dst) in enumerate([(self.st_ssm_re, self.h0r), (self.st_ssm_im, self.h0i)]):
                stg, sk = self.next_stage()
                S.op('sp', (lambda e, stg=stg, src=src: e.dma_start(out=stg[:16, :], in_=src[l])), writes=[sk], dma=sk)
                pb, pk = self.pn()

                def f(e, pb=pb, stg=stg):
                    r = None
                    for t in range(8):
                        r = e.transpose(out=pb[:, t * 16:(t + 1) * 16], in_=stg[:16, t * 128:(t + 1) * 128],
                                        identity=self.ident[:16, :16])
                    return r
                S.op('pe', f, reads=[sk, 'ident'], writes=[pk])
                S.op('act', (lambda e, pb=pb, dst=dst: e.activation(out=dst, in_=pb[:, 0:128].rearrange('p (a b) -> p a b', a=8),
                                                                    func=AF.Copy)), reads=[pk], writes=['s5h0'])
            e1r = Er[:, :, 1:2].to_broadcast([128, 8, 16])
            e1i = Ei[:, :, 1:2].to_broadcast([128, 8, 16])
            h0r, h0i, injr, inji, tC = self.h0r, self.h0i, self.injr, self.inji, self.tC
            S.op('pool', lambda e: e.tensor_tensor(out=injr, in0=h0r, in1=e1r, op=TT.mult), reads=['s5h0'], writes=['s5inj'])
            S.op('pool', lambda e: e.tensor_tensor(out=tC, in0=h0i, in1=e1i, op=TT.mult), reads=['s5h0'], writes=['s5tc'])
            S.op('pool', lambda e: e.tensor_tensor(out=injr, in0=injr, in1=tC, op=TT.subtract), reads=['s5tc', 's5inj'], writes=['s5inj'])
            S.op('pool', lambda e: e.tensor_tensor(out=inji, in0=h0i, in1=e1r, op=TT.mult), reads=['s5h0'], writes=['s5inj2'])
            S.op('pool', lambda e: e.tensor_tensor(out=tC, in0=h0r, in1=e1i, op=TT.mult), reads=['s5h0', 's5inj'], writes=['s5tc'])
            S.op('pool', lambda e: e.tensor_tensor(out=inji, in0=inji, in1=tC, op=TT.add), reads=['s5tc', 's5inj2'], writes=['s5inj2'])
        T = 4112
        av = self.av
        sets = []
        for sid in range(2):
            b0 = T + sid * 1024
            sets.append((av(b0, [128, 256]), av(b0 + 256, [128, 256]), av(b0 + 512, [128, 256]), av(b0 + 768, [128, 256])))
        halves = [(0, 256), (256, 256)] if ti < 4 else [(0, 64)]

        def unit(t, col0, nn, sid, lasthalf):
            c = t // 4
            uXr, uXi, uT1, uT2 = sets[sid]
            kXr, kXi, kT1, kT2 = 'Xr%d' % sid, 'Xi%d' % sid, 'T1%d' % sid, 'T2%d' % sid
            hbr, hbi = hid[:, 2 + 4 * sid, :nn], hid[:, 3 + 4 * sid, :nn]
            khr, khi = ('hid', 2 + 4 * sid), ('hid', 3 + 4 * sid)
            nchk = nn // CW
            w3 = lambda ap: ap[:, :nn].rearrange('p (c j) -> p c j', j=CW)
            ub = hid[:, c, col0:col0 + nn]
            pbr, pkr = self.pn()
            pbi, pki = self.pn()
            S.op('pe', (lambda e: e.matmul(pbr[:, :nn], lhsT=self.s5_bbT[:, 0, t, :], rhs=ub, start=True, stop=True)),
                 reads=['s5bbT', ('hid', c)], writes=[pkr])
            S.op('pe', (lambda e: e.matmul(pbi[:, :nn], lhsT=self.s5_bbT[:, 1, t, :], rhs=ub, start=True, stop=True)),
                 reads=['s5bbT', ('hid', c)], writes=[pki])
            yield
            Qrb = Qr[:, t, 0:CW].unsqueeze(1).to_broadcast([128, nchk, CW])
            Qib = Qi[:, t, 0:CW].unsqueeze(1).to_broadcast([128, nchk, CW])
            Erb = Er[:, t, 0:CW].unsqueeze(1).to_broadcast([128, nchk, CW])
            Eib = Ei[:, t, 0:CW].unsqueeze(1).to_broadcast([128, nchk, CW])
            S.op('dve', (lambda e: e.tensor_tensor(out=w3(uT1), in0=w3(pbi), in1=Qib, op=TT.mult)), reads=[pki], writes=[kT1])
            S.op('dve', (lambda e: e.tensor_tensor(out=w3(uXr), in0=w3(pbr), in1=Qrb, op=TT.mult)), reads=[pkr], writes=[kXr])
            S.op('dve', (lambda e: e.tensor_tensor(out=w3(uT2), in0=w3(pbr), in1=Qib, op=TT.mult)), reads=[pkr], writes=[kT2])
            S.op('dve', (lambda e: e.tensor_tensor(out=w3(uXi), in0=w3(pbi), in1=Qrb, op=TT.mult)), reads=[pki], writes=[kXi])
            yield
            S.op('pool', lambda e: e.tensor_tensor(out=uXr[:, :nn], in0=uXr[:, :nn], in1=uT1[:, :nn], op=TT.subtract), reads=[kXr, kT1], writes=[kXr])
            S.op('pool', lambda e: e.tensor_tensor(out=uXi[:, :nn], in0=uXi[:, :nn], in1=uT2[:, :nn], op=TT.add), reads=[kXi, kT2], writes=[kXi])
            if ti == 4:
                S.op('pool', (lambda e: e.tensor_tensor(out=w3(uXr)[:, :, 0], in0=w3(uXr)[:, :, 0], in1=self.injr[:, t, :], op=TT.add)),
                     reads=[kXr, 's5inj'], writes=[kXr])
                S.op('pool', (lambda e: e.tensor_tensor(out=w3(uXi)[:, :, 0], in0=w3(uXi)[:, :, 0], in1=self.inji[:, t, :], op=TT.add)),
                     reads=[kXi, 's5inj2'], writes=[kXi])
                yield
                S.op('dve', lambda e: e.tensor_tensor_scan(out=uT1[:, :nn], data0=self.rmask[:, :nn], data1=uXr[:, :nn], initial=0.0,
                                                           op0=TT.mult, op1=TT.add), reads=[kXr, 'rmask'], writes=[kT1])
                S.op('dve', lambda e: e.tensor_tensor_scan(out=uT2[:, :nn], data0=self.rmask[:, :nn], data1=uXi[:, :nn], initial=0.0,
                                                           op0=TT.mult, op1=TT.add), reads=[kXi, 'rmask'], writes=[kT2])
                yield
            else:
                yield
                ck = ('carry', t)
                tA = self.s5_small[:, 12 + 2 * sid, 0:1]
                tB = self.s5_small[:, 13 + 2 * sid, 0:1]
                kta, ktb = 's5ta%d' % sid, 's5tb%d' % sid
                e128r = Er[:, t, 128:129]
                e128i = Ei[:, t, 128:129]
                for ch in range(nchk):
                    sl = slice(ch * 128, (ch + 1) * 128)
                    S.op('dve', (lambda e, sl=sl: e.tensor_tensor_scan(out=uT1[:, sl], data0=self.ones_f[:, :], data1=uXr[:, sl],
                                                                        initial=carry[:, t, 0:1], op0=TT.mult, op1=TT.add)),
                         reads=[kXr, 'ones_f', ck, 'carry'], writes=[kT1])
                    S.op('dve', (lambda e, sl=sl: e.tensor_tensor_scan(out=uT2[:, sl], data0=self.ones_f[:, :], data1=uXi[:, sl],
                                                                        initial=carry[:, t, 1:2], op0=TT.mult, op1=TT.add)),
                         reads=[kXi, 'ones_f', ck, 'carry'], writes=[kT2])
                    yield
                    last = ch * 128 + 127
                    glr = uT1[:, last:last + 1]
                    gli = uT2[:, last:last + 1]
                    S.op('dve', (lambda e, gli=gli: e.tensor_tensor(out=tA, in0=gli, in1=e128i, op=TT.mult)), reads=[kT2], writes=[kta])
                    S.op('dve', (lambda e, glr=glr: e.tensor_tensor(out=tB, in0=glr, in1=e128i, op=TT.mult)), reads=[kT1], writes=[ktb])
                    yield
                    S.op('dve', (lambda e, glr=glr: e.scalar_tensor_tensor(out=carry[:, t, 0:1], in0=glr, scalar=e128r, in1=tA,
                                                                            op0=TT.mult, op1=TT.subtract)), reads=[kT1, kta], writes=[ck])
                    S.op('dve', (lambda e, gli=gli: e.scalar_tensor_tensor(out=carry[:, t, 1:2], in0=gli, scalar=e128r, in1=tB,
                                                                            op0=TT.mult, op1=TT.add)), reads=[kT2, ktb], writes=[ck])
                    yield
            S.op('pool', (lambda e: e.tensor_tensor(out=w3(uXr), in0=w3(uT1), in1=Erb, op=TT.mult)), reads=[kT1], writes=[kXr])
            S.op('pool', (lambda e: e.tensor_tensor(out=w3(uXi), in0=w3(uT2), in1=Erb, op=TT.mult)), reads=[kT2], writes=[kXi])
            S.op('dve', (lambda e: e.tensor_tensor(out=w3(uT1), in0=w3(uT1), in1=Eib, op=TT.mult)), reads=[kT1, kXr], writes=[kT1])
            S.op('dve', (lambda e: e.tensor_tensor(out=w3(uT2), in0=w3(uT2), in1=Eib, op=TT.mult)), reads=[kT2, kXi], writes=[kT2])
            yield
            S.op('pool', lambda e: e.tensor_tensor(out=uXr[:, :nn], in0=uXr[:, :nn], in1=uT2[:, :nn], op=TT.subtract), reads=[kXr, kT2], writes=[kXr])
            S.op('pool', lambda e: e.tensor_tensor(out=uXi[:, :nn], in0=uXi[:, :nn], in1=uT1[:, :nn], op=TT.add), reads=[kXi, kT1], writes=[kXi])
            yield
            S.op('act', lambda e: e.activation(out=hbr, in_=uXr[:, :nn], func=AF.Copy), reads=[kXr], writes=[khr])
            S.op('act', lambda e: e.activation(out=hbi, in_=uXi[:, :nn], func=AF.Copy), reads=[kXi], writes=[khi])
            if ti == 3 and lasthalf:
                S.op('act', (lambda e: e.activation(out=self.s5_hl[:, 0, t:t + 1], in_=uXr[:, nn - 1:nn], func=AF.Copy)), reads=[kXr], writes=['s5hl'])
                S.op('act', (lambda e: e.activation(out=self.s5_hl[:, 1, t:t + 1], in_=uXi[:, nn - 1:nn], func=AF.Copy)), reads=[kXi], writes=['s5hl'])
            if ti == 4:
                S.op('act', (lambda e: e.activation(out=self.s5_hls[:, 0, t, :], in_=w3(uXr)[:, :, 3], func=AF.Copy)), reads=[kXr], writes=['s5hls'])
                S.op('act', (lambda e: e.activation(out=self.s5_hls[:, 1, t, :], in_=w3(uXi)[:, :, 3], func=AF.Copy)), reads=[kXi], writes=['s5hls'])
            yield
            ya, yk = yacc[c]

            def fy(e):
                e.matmul(ya[:, col0:col0 + nn], lhsT=self.s5_cx[:, 0, t, :], rhs=hbr, start=(t % 4 == 0), stop=False)
                return e.matmul(ya[:, col0:col0 + nn], lhsT=self.s5_cx[:, 1, t, :], rhs=hbi, start=False, stop=(t % 4 == 3))
            S.op('pe', fy, reads=['s5cx', khr, khi], writes=[yk])
            yield

        for hi_, (col0, nn) in enumerate(halves):
            lasthalf = (hi_ == len(halves) - 1)

            def chain(sid, col0=col0, nn=nn, lasthalf=lasthalf):
                for t in range(sid, 8, 2):
                    yield from unit(t, col0, nn, sid, lasthalf)
            alive = [chain(0), chain(1)]
            while alive:
                for g in list(alive):
                    try:
                        next(g)
                    except StopIteration:
                        alive.remove(g)
        for c in range(2):
            ya, yk = yacc[c]
            S.op('dve', (lambda e, ya=ya, c=c: e.scalar_tensor_tensor(out=ysb[:, c, :n], in0=hid[:, c, :n], scalar=self.pc['ssm_d'][:, l, c:c + 1],
                                                                       in1=ya[:, :n], op0=TT.mult, op1=TT.add)),
                 reads=[yk, ('hid', c), 'pc'], writes=[('ysb', c)])
            S.op('act', (lambda e, c=c: e.activation(out=hid[:, 4 + c, :n], in_=ysb[:, c, :n], func=AF.Gelu)),
                 reads=[('ysb', c)], writes=[('hid', 4 + c)])
        S.barrier()
        for c in range(2):
            pb, pk = self.pn()

            def f(e, pb=pb, c=c):
                r = None
                for k in range(2):
                    r = e.matmul(pb[:, :n], lhsT=self.s5_gluw[:, k, c * 128:(c + 1) * 128], rhs=hid[:, 4 + k, :n], start=(k == 0), stop=(k == 1))
                return r
            S.op('pe', f, reads=['s5gluw', ('hid', 4), ('hid', 5)], writes=[pk])
            gate, gk = [(Xr, 'Xr'), (Xi, 'Xi')][c]
            S.op('act', (lambda e, pb=pb, c=c, gate=gate: e.activation(out=gate[:, :n], in_=pb[:, :n], func=AF.Sigmoid,
                                                                       bias=self.pc['ssm_glu_b'][:, l, c:c + 1])),
                 reads=[pk, 'pc'], writes=[gk])
            S.op('dve', (lambda e, c=c, gate=gate: e.tensor_tensor(out=self.ymix[:, c, :n], in0=hid[:, 4 + c, :n], in1=gate[:, :n], op=TT.mult)),
                 reads=[gk, ('hid', 4 + c)], writes=['ymix'])
        if ti == 3:
            for reim, dst in enumerate([self.o_p_ssm_re, self.o_p_ssm_im]):
                okey = 'o_p_ssm%d' % reim
                S.op('sp', (lambda e, reim=reim, dst=dst: e.dma_start(out=dst[l].rearrange('(t p) -> p t', p=128), in_=self.s5_hl[:, reim, :])),
                     reads=['s5hl'], writes=[okey], dma=okey)
                if okey not in self.outkeys:
                    self.outkeys.append(okey)
        if ti == 4:
            for reim, dst in enumerate([self.o_s_ssm_re, self.o_s_ssm_im]):
                for half in range(2):
                    self.store_fm([(self.s5_hls[:, reim, half * 4 + j, :], 's5hls') for j in range(4)], 16,
                                  dst[l][:, half * 512:(half + 1) * 512], 'o_s_ssm%d' % reim)

    def gla_consts(self):
        S = self.S
        sb = self.sb
        self.triU = sb('triU', [64, 64], F32)
        self.blkU = sb('blkU', [64, 64], F32)
        self.segm = sb('segm', [64, 16], F32)
        self.segT = sb('segT', [16, 64], F32)
        self.rm64 = sb('rm64', [64, 256], F32)
        self.zsh = sb('zsh', [64, 192], F32R)
        self.ones64 = sb('ones64', [64, 64], F32R)
        self.eps5 = sb('eps5', [64, 1], F32)
        self.pc64 = {}
        for name in PC64:
            self.pc64[name] = sb('p64_' + name, [64, DEPTH, 4], F32)
        self.lbv = sb('lbv', [64, DEPTH, 4], F32)
        self.omlv = sb('omlv', [64, DEPTH, 4], F32)
        triU, blkU, segm, segT, rm64, zsh = self.triU, self.blkU, self.segm, self.segT, self.rm64, self.zsh
        K = 'glac'
        S.op('pool', lambda e: e.memset(triU[:], 1.0), writes=[K])
        S.op('pool', lambda e: e.affine_select(out=triU[:], in_=triU[:], pattern=[[1, 64]], compare_op=ALU.is_ge, fill=0.0,
                                               base=0, channel_multiplier=-1), reads=[K], writes=[K])
        S.op('pool', lambda e: e.memset(segm[:], 1.0), reads=[K], writes=[K])
        S.op('pool', lambda e: e.affine_select(out=segm[:], in_=segm[:], pattern=[[-4, 16]], compare_op=ALU.is_ge, fill=0.0,
                                               base=0, channel_multiplier=1), reads=[K], writes=[K])
        S.op('pool', lambda e: e.affine_select(out=segm[:], in_=segm[:], pattern=[[4, 16]], compare_op=ALU.is_ge, fill=0.0,
                                               base=3, channel_multiplier=-1), reads=[K], writes=[K])
        S.op('pool', lambda e: e.memset(segT[:], 1.0), reads=[K], writes=[K])
        S.op('pool', lambda e: e.affine_select(out=segT[:], in_=segT[:], pattern=[[1, 64]], compare_op=ALU.is_ge, fill=0.0,
                                               base=0, channel_multiplier=-4), reads=[K], writes=[K])
        S.op('pool', lambda e: e.affine_select(out=segT[:], in_=segT[:], pattern=[[-1, 64]], compare_op=ALU.is_ge, fill=0.0,
                                               base=3, channel_multiplier=4), reads=[K], writes=[K])
        pb, pk = self.pn()
        S.op('pe', lambda e: e.matmul(pb[:64, :64], lhsT=segT[:, :], rhs=segT[:, :], start=True, stop=True), reads=[K], writes=[pk])
        S.op('dve', lambda e: e.tensor_tensor(out=blkU[:], in0=pb[:64, :64], in1=triU[:], op=ALU.mult), reads=[pk, K], writes=[K])
        S.op('pool', lambda e: e.memset(rm64[:], 1.0), reads=[K], writes=[K])
        S.op('pool', lambda e: e.memset(rm64[:].rearrange('p (c j) -> p c j', j=64)[:, :, 0:1], 0.0), reads=[K], writes=[K])
        S.op('dve', lambda e: e.tensor_scalar(out=zsh[:], in0=rm64[:, 0:192], scalar1=0.0, scalar2=None, op0=ALU.mult), reads=[K], writes=[K])
        S.op('dve', lambda e: e.tensor_copy(out=zsh[:, 64:128], in_=self.ident[0:64, 0:64]), reads=[K, 'ident'], writes=[K])
        S.op('dve', lambda e: e.tensor_scalar(out=self.ones64[:], in0=rm64[:, 0:64], scalar1=0.0, scalar2=1.0, op0=ALU.mult, op1=ALU.add), reads=[K], writes=[K])
        S.op('pool', lambda e: e.memset(self.eps5[:], 1e-5), reads=[K], writes=[K])
        for name in PC64:
            t = self.pc64[name]
            for l in range(DEPTH):
                S.op('sp', (lambda e, t=t, l=l, name=name: e.dma_start(out=t[:, l, :], in_=self.P[name][l].rearrange('(h k) -> k h', k=64))),
                     writes=['pc'], dma='pc')
        lg = self.pc64['hgrn_lb_logits']
        lbv, omlv = self.lbv, self.omlv
        S.op('pool', lambda e: e.memset(lbv[:], 0.0), reads=[K], writes=[K])
        S.op('dve', lambda e: e.tensor_tensor(out=lbv[:, 1, :], in0=lg[:, 1, :], in1=lg[:, 0, :], op=ALU.subtract), reads=['pc', K], writes=[K])
        S.op('act', lambda e: e.activation(out=lbv[:, 1, :], in_=lbv[:, 1, :], func=AF.Sigmoid), reads=[K], writes=[K])
        S.op('dve', lambda e: e.tensor_scalar(out=omlv[:], in0=lbv[:], scalar1=-1.0, scalar2=1.0, op0=ALU.mult, op1=ALU.add), reads=[K], writes=[K])

    def proj_heads(self, win, colbase, sub, slot, evac):
        S = self.S
        c0, n = sub
        xn = self.xn
        for hh in range(2):
            pb, pk = self.pn()

            def f(e, pb=pb, hh=hh):
                r = None
                for j in range(2):
                    h = hh * 2 + j
                    for kt in range(8):
                        r = e.matmul(pb[:64, j * n:(j + 1) * n], lhsT=win[:, kt, colbase + h * 64:colbase + (h + 1) * 64],
                                     rhs=xn[:, kt, c0:c0 + n], start=(kt == 0), stop=(kt == 7))
                return r
            S.op('pe', f, reads=[('w', slot), ('xn', 0), ('xn', 1), ('xn', 2), ('xn', 3), ('xn', 4)], writes=[pk])
            evac(pb, pk, hh)

    def heads_to_ymix(self, src, skey, n):
        S = self.S
        zsh = self.zsh
        for t in range(2):
            pb, pk = self.pn()

            def f(e, pb=pb, t=t):
                e.matmul(pb[:, :n], lhsT=zsh[:, 64:192], rhs=R(src[:, 2 * t, :]), start=True, stop=False)
                return e.matmul(pb[:, :n], lhsT=zsh[:, 0:128], rhs=R(src[:, 2 * t + 1, :]), start=False, stop=True)
            S.op('pe', f, reads=[skey, 'glac'], writes=[pk])
            S.op('act', (lambda e, pb=pb, t=t: e.activation(out=self.ymix[:, t, :n], in_=pb[:, :n], func=AF.Copy)),
                 reads=[pk], writes=['ymix'])

    def gla_chunks(self, l, kind, n, sample, A_r, A_k, A_v, A_o, GL, Sst, off, S0=None, A_a=None, A_b=None):
        S = self.S
        av = self.av
        L = 64
        nch = n // L
        mask = self.blkU if sample else self.triU
        M4 = av(off, [64, 4, 64])
        Vtok = av(off + 256, [64, 4, 64])
        Ktok = av(off + 512, [64, 4, 64])
        Stmp = av(off + 768, [64, 4, 64])
        K4 = lambda nm: (kind + nm)
        Sr = av(off + 1344, [64, 4, 64])
        if not sample:
            S.op('act', lambda e: e.activation(out=R(Sr), in_=Sst, func=AF.Copy), reads=[K4('S')], writes=[K4('Sr')])
        for c in range(nch):
            cs = slice(c * L, (c + 1) * L)
            pbA, pkA = self.pn()

            def fA(e, pbA=pbA, cs=cs):
                r = None
                for h in range(4):
                    r = e.matmul(pbA[:64, h * 64:(h + 1) * 64], lhsT=R(A_k[:, h, cs]), rhs=R(A_r[:, h, cs]), start=True, stop=True)
                return r
            S.op('pe', fA, reads=[K4('k'), K4('r')], writes=[pkA])
            S.op('dve', (lambda e, pbA=pbA: e.tensor_tensor(out=R(M4), in0=pbA[:64, 0:256].rearrange('p (h t) -> p h t', h=4),
                                                            in1=mask[:, :].unsqueeze(1).to_broadcast([64, 4, 64]), op=ALU.mult)),
                 reads=[pkA, 'glac'], writes=[K4('M4')])
            for src, skey, dst, dkey, eng in ((A_k, K4('k'), Ktok, K4('Ktok'), 'act'), (A_v, K4('v'), Vtok, K4('Vtok'), 'act')):
                pbT, pkT = self.pn()

                def fT(e, pbT=pbT, src=src, cs=cs):
                    r = None
                    for h in range(4):
                        r = e.transpose(out=pbT[:64, h * 64:(h + 1) * 64], in_=src[:, h, cs], identity=self.ident[0:64, 0:64])
                    return r
                S.op('pe', fT, reads=[skey, 'ident'], writes=[pkT])
                S.op(eng, (lambda e, pbT=pbT, dst=dst: e.activation(out=R(dst), in_=pbT[:64, 0:256].rearrange('p (h t) -> p h t', h=4), func=AF.Copy)),
                     reads=[pkT], writes=[dkey])
            pbY, pkY = self.pn()

            def fY(e, pbY=pbY, cs=cs, c=c):
                r = None
                for h in range(4):
                    out = pbY[:64, h * 64:(h + 1) * 64]
                    if not sample:
                        e.matmul(out, lhsT=R(Sr[:, h, :]), rhs=R(A_r[:, h, cs]), start=True, stop=False)
                        r = e.matmul(out, lhsT=R(Vtok[:, h, :]), rhs=R(M4[:, h, :]), start=False, stop=True)
                    else:
                        e.matmul(out, lhsT=R(Vtok[:, h, :]), rhs=R(M4[:, h, :]), start=True, stop=False)
                        for j in range(NS):
                            r = e.matmul(pbY[:64, h * 64 + 4 * j:h * 64 + 4 * j + 4], lhsT=S0[:, j, h, :], rhs=A_r[:, h, 4 * j:4 * j + 4],
                                         start=False, stop=(j == NS - 1))
                return r
            S.op('pe', fY, reads=[K4('S'), K4('Sr'), K4('r'), K4('Vtok'), K4('M4')], writes=[pkY])
            S.op('act', (lambda e, pbY=pbY, cs=cs: e.activation(out=A_o[:, :, cs], in_=pbY[:64, 0:256].rearrange('p (h t) -> p h t', h=4), func=AF.Copy)),
                 reads=[pkY], writes=[K4('o')])
            if not sample:
                pbS, pkS = self.pn()

                def fS(e, pbS=pbS):
                    r = None
                    for h in range(4):
                        r = e.matmul(pbS[:64, h * 64:(h + 1) * 64], lhsT=R(Ktok[:, h, :]), rhs=R(Vtok[:, h, :]), start=True, stop=True)
                    return r
                S.op('pe', fS, reads=[K4('Ktok'), K4('Vtok')], writes=[pkS])
                S.op('dve', (lambda e, pbS=pbS: e.tensor_tensor(out=Stmp, in0=pbS[:64, 0:256].rearrange('p (h v) -> p h v', h=4), in1=Sst, op=ALU.add)),
                     reads=[pkS, K4('S')], writes=[K4('Stmp')])
                S.op('dve', (lambda e, c=c: e.tensor_tensor(out=Sst, in0=Stmp, in1=GL[:, :, c:c + 1].to_broadcast([64, 4, 64]), op=ALU.mult)),
                     reads=[K4('Stmp'), K4('GL')], writes=[K4('S')])
                S.op('act', lambda e: e.activation(out=R(Sr), in_=Sst, func=AF.Copy), reads=[K4('S')], writes=[K4('Sr')])
            else:
                hidf = self.hid[:].rearrange('p a b -> p (a b)').bitcast(F32)
                Vexp = hidf[0:64, 1024:2048].rearrange('p (j v) -> p j v', j=16)
                for h in range(4):
                    S.op('dve', (lambda e, h=h: e.tensor_tensor(out=Vexp, in0=Vtok[:, h, :].unsqueeze(1).to_broadcast([64, 16, 64]),
                                                                in1=self.segm[:, :].unsqueeze(2).to_broadcast([64, 16, 64]), op=ALU.mult)),
                         reads=[K4('Vtok'), 'glac'], writes=[K4('Vexp')])
                    for jj in range(2):
                        pbS, pkS = self.pn()
                        S.op('pe', (lambda e, pbS=pbS, h=h, jj=jj: e.matmul(
                            pbS[:64, :], lhsT=Ktok[:, h, :], rhs=Vexp[:, 8 * jj:8 * jj + 8, :], start=True, stop=True)),
                            reads=[K4('Ktok'), K4('Vexp')], writes=[pkS])
                        S.op('dve', (lambda e, pbS=pbS, h=h, jj=jj: e.tensor_tensor(
                            out=S0[:, 8 * jj:8 * jj + 8, h, :], in0=pbS[:64, :].rearrange('p (j v) -> p j v', j=8),
                            in1=S0[:, 8 * jj:8 * jj + 8, h, :], op=ALU.add)), reads=[pkS, K4('S')], writes=[K4('S')])
                        S.op('dve', (lambda e, h=h, jj=jj: e.tensor_tensor(
                            out=S0[:, 8 * jj:8 * jj + 8, h, :], in0=S0[:, 8 * jj:8 * jj + 8, h, :],
                            in1=GL[:, h, 8 * jj:8 * jj + 8].unsqueeze(2).to_broadcast([64, 8, 64]), op=ALU.mult)),
                            reads=[K4('S'), K4('GL')], writes=[K4('S')])

    def mix_hgrn(self, l, ti, win, slot):
        if not ENABLE['hgrn']:
            self.zero_ymix(ti)
            return self.apply_wout(self._wout, ti, slot)
        if ti < 4:
            for half in range(2):
                self.hgrn_sub(l, ti * 2 + half, win, slot)
                self.apply_wout_sub(self._wout, (TTS[ti][0] + half * 256, 256), ti, slot)
        else:
            self.S.barrier()
            self.hgrn_sub(l, 8, win, slot)
            self.apply_wout_sub(self._wout, TTS[4], ti, slot)

    def apply_wout_sub(self, wout, sub, ti, slot):
        S = self.S
        c0, n = sub
        hT, ymix = self.hT, self.ymix
        for o in range(8):
            pb, pk = self.pn()

            def f(e, pb=pb, o=o):
                r = None
                for k in range(2):
                    r = e.matmul(pb[:, :n], lhsT=wout[:, k, o * 128:(o + 1) * 128], rhs=ymix[:, k, :n], start=(k == 0), stop=(k == 1))
                return r
            S.op('pe', f, reads=[('w', slot), 'ymix'], writes=[pk])
            S.op('dve', (lambda e, pb=pb, o=o: e.tensor_tensor(out=hT[:, o, c0:c0 + n], in0=hT[:, o, c0:c0 + n], in1=pb[:, :n], op=ALU.add)),
                 reads=[pk, ('h', ti)], writes=[('h', ti)])

    def hgrn_sub(self, l, si, win, slot):
        S = self.S
        av = self.av
        TT = ALU
        sample = (si == 8)
        c0, n = (si * 256, 256) if not sample else (2048, 64)
        W = n
        A_q = av(0, [64, 4, W])
        A_k = av(1024, [64, 4, W])
        A_b = av(2048, [64, 4, W])
        A_t = av(3072, [64, 4, W])
        A_v = av(4096, [64, 4, W])
        A_o = A_b
        hidf = self.hid[:].rearrange('p a b -> p (a b)').bitcast(F32)
        A_g = hidf[0:64, 0:4 * W].rearrange('p (h t) -> p h t', h=4)
        off = 5120
        Sst = av(off + 1024, [64, 4, 64])
        nch = 1 if sample else 4
        GL = av(off + 1280, [64, 4, 16])
        S0 = av(1280, [64, 16, 4, 64]) if sample else None
        if sample:
            A_q = av(0, [64, 4, W])
            A_k = av(256, [64, 4, W])
            A_b = av(512, [64, 4, W])
            A_t = av(768, [64, 4, W])
            A_v = av(1024, [64, 4, W])
            A_o = A_b
            off = 5376
            GL = av(off + 1280, [64, 4, 16])
        lbc = self.lbv[:, l, :].unsqueeze(2).to_broadcast([64, 4, W])
        omlc = self.omlv[:, l, :].unsqueeze(2).to_broadcast([64, 4, W])
        ngc = self.pc64['hgrn_norm_g'][:, l, :].unsqueeze(2).to_broadcast([64, 4, W])
        sub = (c0, n)
        v3 = lambda pb: pb[:64, 0:2 * n].rearrange('p (j t) -> p j t', j=2)
        if si == 0:
            S.op('pool', lambda e: e.memset(Sst, 0.0), writes=['hS'])
        if sample:
            for q in range(4):
                S.op('sp', (lambda e, q=q: e.dma_start(out=S0[:, 4 * q:4 * q + 4, :, :],
                                                       in_=self.st_hgrn[l, 4 * q:4 * q + 4].rearrange('j h k v -> k j h v'))),
                     writes=['hS'], dma='hS0')
        self.proj_heads(win, 0, sub, slot, lambda pb, pk, hh: S.op(
            'act', (lambda e: e.activation(out=A_q[:, 2 * hh:2 * hh + 2, :], in_=v3(pb), func=AF.Silu)), reads=[pk], writes=['hr']))
        self.proj_heads(win, 256, sub, slot, lambda pb, pk, hh: S.op(
            'act', (lambda e: e.activation(out=A_b[:, 2 * hh:2 * hh + 2, :], in_=v3(pb), func=AF.Sigmoid)), reads=[pk], writes=['hb', 'ho']))
        self.proj_heads(win, 512, sub, slot, lambda pb, pk, hh: S.op(
            'act', (lambda e: e.activation(out=A_v[:, 2 * hh:2 * hh + 2, :], in_=v3(pb), func=AF.Copy)), reads=[pk], writes=['hv']))
        self.proj_heads(win, 768, sub, slot, lambda pb, pk, hh: S.op(
            'act', (lambda e: e.activation(out=A_g[:, 2 * hh:2 * hh + 2, :], in_=v3(pb), func=AF.Silu)), reads=[pk], writes=['hg']))
        S.op('dve', lambda e: e.tensor_tensor(out=A_b, in0=A_b, in1=omlc, op=TT.mult), reads=['hb', 'glac'], writes=['hb'])
        S.op('dve', lambda e: e.tensor_tensor(out=A_b, in0=A_b, in1=lbc, op=TT.add), reads=['hb', 'glac'], writes=['hb'])
        S.op('dve', lambda e: e.tensor_scalar(out=A_k, in0=A_b, scalar1=-1.0, scalar2=1.0, op0=TT.mult, op1=TT.add), reads=['hb'], writes=['hk'])
        S.op('act', lambda e: e.activation(out=A_b, in_=A_b, func=AF.Ln), reads=['hb', 'hk'], writes=['hb'])
        rmk = self.rmask[0:64, 0:64] if sample else self.rm64[:, :]
        for h in range(4):
            S.op('dve', (lambda e, h=h: e.tensor_tensor_scan(out=A_t[:, h, :], data0=rmk, data1=A_b[:, h, :], initial=0.0, op0=TT.mult, op1=TT.add)),
                 reads=['hb', 'glac', 'rmask'], writes=['ht'])
        S.op('act', lambda e: e.activation(out=A_b, in_=A_t, func=AF.Exp), reads=['ht'], writes=['hb'])
        S.op('dve', lambda e: e.tensor_tensor(out=R(A_q), in0=A_q, in1=A_b, op=TT.mult), reads=['hr', 'hb'], writes=['hr'])
        if sample:
            S.op('pool', lambda e: e.tensor_copy(out=GL, in_=A_b[:, :, :].rearrange('p h (j t) -> p h j t', t=4)[:, :, :, 3]), reads=['hb'], writes=['hGL'])
        else:
            S.op('pool', lambda e: e.tensor_copy(out=GL[:, :, 0:4], in_=A_b[:, :, :].rearrange('p h (c t) -> p h c t', t=64)[:, :, :, 63]),
                 reads=['hb'], writes=['hGL'])
        S.op('act', lambda e: e.activation(out=A_t, in_=A_t, func=AF.Exp, scale=-1.0), reads=['ht'], writes=['ht'])
        S.op('dve', lambda e: e.tensor_tensor(out=R(A_k), in0=A_k, in1=A_t, op=TT.mult), reads=['hk', 'ht'], writes=['hk'])
        S.op('pool', lambda e: e.tensor_copy(out=GL[:, 0, 15:16], in_=GL[:, 0, 15:16]), reads=['hGL', 'hr', 'hb'], writes=['ho', 'hGL'])
        self.gla_chunks(l, 'h', n, sample, A_q, A_k, A_v, A_o, GL, Sst, off, S0=S0)
        S.op('dve', lambda e: e.tensor_tensor(out=R(A_t), in0=A_o, in1=A_o, op=TT.mult), reads=['ho', 'hk'], writes=['ht'])
        flat = lambda a: a[:, :, :].rearrange('p h t -> p (h t)')
        tot = 4 * n
        for s0 in range(0, tot, 512):
            w = min(512, tot - s0)
            pb, pk = self.pn()
            S.op('pe', (lambda e, pb=pb, s0=s0, w=w: e.matmul(pb[:64, :w], lhsT=self.ones64[:, :], rhs=R(flat(A_t)[:, s0:s0 + w]), start=True, stop=True)),
                 reads=['ht', 'glac'], writes=[pk])
            S.op('act', (lambda e, pb=pb, s0=s0, w=w: e.activation(out=flat(A_v)[:, s0:s0 + w], in_=pb[:64, :w], func=AF.Ln, scale=1.0 / 64,
                                                                    bias=self.eps5[:])), reads=[pk, 'glac', 'hVtok'], writes=['hv'])
        S.op('act', lambda e: e.activation(out=A_v, in_=A_v, func=AF.Exp, scale=-0.5), reads=['hv'], writes=['hv'])
        S.op('dve', lambda e: e.tensor_tensor(out=A_o, in0=A_o, in1=A_v, op=TT.mult), reads=['ho', 'hv'], writes=['ho'])
        S.op('dve', lambda e: e.tensor_tensor(out=A_o, in0=A_o, in1=ngc, op=TT.mult), reads=['ho', 'pc'], writes=['ho'])
        S.op('dve', lambda e: e.tensor_tensor(out=R(A_o), in0=A_o, in1=A_g, op=TT.mult), reads=['ho', 'hg'], writes=['ho'])
        self.heads_to_ymix(A_o, 'ho', n)
        if si == 7:
            S.op('sp', lambda e: e.dma_start(out=self.o_p_hgrn[l].rearrange('h k v -> k h v'), in_=Sst), reads=['hS'], writes=['o_p_hgrn'], dma='o_p_hgrn')
            if 'o_p_hgrn' not in self.outkeys:
                self.outkeys.append('o_p_hgrn')
        if sample:
            for q in range(4):
                S.op('sp', (lambda e, q=q: e.dma_start(out=self.o_s_hgrn[l, 4 * q:4 * q + 4].rearrange('j h k v -> k j h v'),
                                                       in_=S0[:, 4 * q:4 * q + 4, :, :])), reads=['hS'], writes=['o_s_hgrn'], dma='o_s_hgrn')
            if 'o_s_hgrn' not in self.outkeys:
                self.outkeys.append('o_s_hgrn')

    C0 = 0.6065306597126334

    def rwkv_consts(self):
        S = self.S
        sb = self.sb
        self.sU = sb('sU', [64, 64], F32)
        self.sL = sb('sL', [64, 64], F32)
        self.bsU = sb('bsU', [64, 64], F32)
        self.bsL = sb('bsL', [64, 64], F32)
        self.mu64 = sb('mu64', [64, DEPTH, 16], F32)
        self.shlast = sb('shlast', [64, 16], F32)
        self.gn_eps = sb('gn_eps', [64, 1], F32)
        K = 'glac'
        id64 = self.ident[0:64, 0:64]
        S.op('dve', lambda e: e.tensor_tensor(out=self.sU[:], in0=self.triU[:], in1=id64, op=ALU.subtract), reads=[K, 'ident'], writes=[K])
        S.op('dve', lambda e: e.tensor_tensor(out=self.bsU[:], in0=self.blkU[:], in1=id64, op=ALU.subtract), reads=[K, 'ident'], writes=[K])
        S.op('pool', lambda e: e.memset(self.sL[:], 1.0), reads=[K], writes=[K])
        S.op('pool', lambda e: e.affine_select(out=self.sL[:], in_=self.sL[:], pattern=[[-1, 64]], compare_op=ALU.is_gt, fill=0.0,
                                               base=0, channel_multiplier=1), reads=[K], writes=[K])
        pb, pk = self.pn()
        S.op('pe', lambda e: e.matmul(pb[:64, :64], lhsT=self.segT[:, :], rhs=self.segT[:, :], start=True, stop=True), reads=[K], writes=[pk])
        S.op('dve', lambda e: e.tensor_tensor(out=self.bsL[:], in0=pb[:64, :64], in1=self.sL[:], op=ALU.mult), reads=[pk, K], writes=[K])
        S.op('pool', lambda e: e.memset(self.gn_eps[:], 64e-5), reads=[K], writes=[K])
        for l in range(DEPTH):
            S.op('sp', (lambda e, l=l: e.dma_start(out=self.mu64[:, l, :], in_=self.P['rwkv_mu'][l].rearrange('(c k) -> k c', k=64))),
                 writes=['pc'], dma='pc')

    def mix_rwkv(self, l, ti, win, slot):
        if not ENABLE['rwkv']:
            S = self.S
            self.zero_ymix(ti)
            if ti >= 3:
                for j in range(8):
                    pb, pk = self.proj(win, j, ti, slot)
                    self.shift_capture(l, ti, j, pb, pk)
                self.shift_store(l, ti)
            self.apply_wout(self._wout, ti, slot)
            return
        if ti == 0:
            S = self.S
            lw = self.rstd[:].bitcast(BF16)
            self.w2b = lw[0:64, 0:256]
            self.a2b = lw[0:64, 256:512]
            self.g2b = lw[0:64, 512:1024].rearrange('p (a c) -> p a c', a=2)
            S.op('pool', lambda e: e.dma_start(out=self.w2b, in_=self.P['rwkv_w2'][l]), writes=['rlw'], dma='rlw')
            S.op('pool', lambda e: e.dma_start(out=self.a2b, in_=self.P['rwkv_a2'][l]), writes=['rlw'], dma='rlw')
            S.op('pool', lambda e: e.dma_start(out=self.g2b, in_=self.P['rwkv_g2'][l].rearrange('(a p) c -> p a c', p=64)), writes=['rlw'], dma='rlw')
        if ti < 4:
            for q in range(4):
                si = ti * 4 + q
                self.rwkv_sub(l, si, win, slot)
                self.apply_wout_sub(self._wout, (si * 128, 128), ti, slot)
        else:
            self.rwkv_sub(l, 16, win, slot)
            self.apply_wout_sub(self._wout, TTS[4], ti, slot)

    def rwkv_sub(self, l, si, win, slot):
        S = self.S
        av = self.av
        TT = ALU
        C0 = self.C0
        sample = (si == 16)
        c0, n = (si * 128, 128) if not sample else (2048, 64)
        W = 4 * n
        S.barrier()
        sl = lambda i: av(i * W, [64, 4, n])
        S0a, S1a, S2a, S3a, S4a, S5a = [sl(i) for i in range(6)]
        TB = 6 * W
        tsl = lambda i: av(TB + i * W, [64, 4, n])
        T0, T1, T2, T3, T4, T5, T6 = [tsl(i) for i in range(7)]
        hidf = self.hid[:].rearrange('p a b -> p (a b)').bitcast(F32)
        Gg = hidf[0:64, 0:W].rearrange('p (h t) -> p h t', h=4)
        BON = hidf[0:64, W:2 * W].rearrange('p (h t) -> p h t', h=4)
        LB = self.hid[0:64, 7, 0:W].rearrange('p (h t) -> p h t', h=4) if not sample else self.hid[0:64, 2, 0:W].rearrange('p (h t) -> p h t', h=4)
        Sst = av(6912, [64, 4, 64])
        GL = av(7168, [64, 4, 16])
        sub = (c0, n)
        P64 = self.pc64
        bc = lambda name: P64[name][:, l, :].unsqueeze(2).to_broadcast([64, 4, n])
        if si == 0:
            S.op('pool', lambda e: e.memset(Sst, 0.0), writes=['rS'])
            S.op('pool', lambda e: e.memset(self.shlast[:], 0.0), writes=['shlast'])
        if not sample:
            pass
        if not sample:
            PF = av(TB, [64, 16, n + 1])
            S.op('pool', lambda e: e.tensor_copy(out=PF[:, :, 0], in_=self.shlast[:, :]), reads=['shlast'], writes=['PF'])
        else:
            PF = av(TB, [64, 16, 16, 5])
            stg, sk = self.next_stage()
            S.op('sp', lambda e: e.dma_start(out=stg[:16, :], in_=self.st_shift[l]), writes=[sk], dma=sk)
            pbx, pkx = self.pn()

            def fx(e):
                r = None
                for c in range(16):
                    r = e.transpose(out=pbx[:64, c * 16:(c + 1) * 16], in_=stg[:16, c * 64:(c + 1) * 64], identity=self.ident[:16, :16])
                return r
            S.op('pe', fx, reads=[sk, 'ident'], writes=[pkx])
            S.op('act', lambda e: e.activation(out=PF[:, :, :, 0], in_=pbx[:64, 0:256].rearrange('p (c j) -> p c j', c=16), func=AF.Copy),
                 reads=[pkx], writes=['PF'])
        xn = self.xn
        for g4 in range(4):
            pb, pk = self.pn()

            def f(e, pb=pb, g4=g4):
                r = None
                for j in range(4):
                    c = g4 * 4 + j
                    for kt in range(8):
                        r = e.matmul(pb[:64, j * n:(j + 1) * n], lhsT=win[:, kt, c * 64:(c + 1) * 64], rhs=xn[:, kt, c0:c0 + n],
                                     start=(kt == 0), stop=(kt == 7))
                return r
            S.op('pe', f, reads=[('w', slot), ('xn', 0), ('xn', 1), ('xn', 2), ('xn', 3), ('xn', 4)], writes=[pk])
            if not sample:
                S.op('act', (lambda e, pb=pb, g4=g4: e.activation(out=PF[:, g4 * 4:g4 * 4 + 4, 1:n + 1], in_=pb[:64, 0:4 * n].rearrange('p (c t) -> p c t', c=4),
                                                                  func=AF.Copy)), reads=[pk], writes=['PF'])
            else:
                for j in range(4):
                    S.op('act', (lambda e, pb=pb, g4=g4, j=j: e.activation(out=PF[:, g4 * 4 + j, :, 1:5],
                                                                           in_=pb[:64, j * n:(j + 1) * n].rearrange('p (s t) -> p s t', t=4), func=AF.Copy)),
                         reads=[pk], writes=['PF'])
        if not sample:
            S.op('pool', lambda e: e.tensor_copy(out=self.shlast[:, :], in_=PF[:, :, n]), reads=['PF'], writes=['shlast'])
            if si == 15:
                S.op('sp', lambda e: e.dma_start(out=self.o_p_shift[l].rearrange('(c k) -> k c', k=64), in_=self.shlast[:, :]),
                     reads=['shlast'], writes=['o_p_shift'], dma='o_p_shift')
                if 'o_p_shift' not in self.outkeys:
                    self.outkeys.append('o_p_shift')
        else:
            shs = av(TB + 3 * W + 1536, [64, 16, 16])
            S.op('pool', lambda e: e.tensor_copy(out=shs, in_=PF[:, :, :, 4]), reads=['PF'], writes=['shs2'])
            for half in range(2):
                pbs, pks = self.pn()

                def fs(e, pbs=pbs, half=half):
                    r = None
                    for c in range(8):
                        r = e.transpose(out=pbs[:16, c * 64:(c + 1) * 64], in_=shs[:, half * 8 + c, :], identity=self.ident[0:64, 0:64])
                    return r
                S.op('pe', fs, reads=['shs2', 'ident'], writes=[pks])
                stg, sk = self.next_stage()
                S.op('act', (lambda e, pbs=pbs, stg=stg: e.activation(out=stg[:16, 0:512], in_=pbs[:16, 0:512], func=AF.Copy)), reads=[pks], writes=[sk])
                S.op('sp', (lambda e, stg=stg, half=half: e.dma_start(out=self.o_s_shift[l][:, half * 512:(half + 1) * 512], in_=stg[:16, 0:512])),
                     reads=[sk], writes=['o_s_shift'], dma='o_s_shift')
            if 'o_s_shift' not in self.outkeys:
                self.outkeys.append('o_s_shift')
        XS = av(0, [64, 16, n]) if not sample else av(0, [64, 16, 16, 4])
        if not sample:
            prev, cur = PF[:, :, 0:n], PF[:, :, 1:n + 1]
            mub = self.mu64[:, l, :].unsqueeze(2).to_broadcast([64, 16, n])
        else:
            prev, cur = PF[:, :, :, 0:4], PF[:, :, :, 1:5]
            mub = self.mu64[:, l, :].unsqueeze(2).unsqueeze(3).to_broadcast([64, 16, 16, 4])
        S.op('dve', lambda e: e.tensor_tensor(out=XS, in0=prev, in1=cur, op=TT.subtract), reads=['PF'], writes=['XS'])
        S.op('dve', lambda e: e.tensor_tensor(out=XS, in0=XS, in1=mub, op=TT.mult), reads=['XS', 'pc'], writes=['XS'])
        S.op('dve', lambda e: e.tensor_tensor(out=XS, in0=XS, in1=cur, op=TT.add), reads=['XS', 'PF'], writes=['XS'])
        S.barrier()
        S.op('act', lambda e: e.activation(out=LB[:, 0, :], in_=S3a[:, 0, :], func=AF.Tanh), writes=['LB'])
        S.op('act', lambda e: e.activation(out=LB[:, 1, :], in_=S3a[:, 1, :], func=AF.Copy), writes=['LB'])
        S.op('act', lambda e: e.activation(out=LB[:, 2:4, :], in_=S3a[:, 2:4, :], func=AF.Sigmoid), writes=['LB'])
        pbw, pkw = self.pn()
        pba, pka = self.pn()
        pbg, pkg = self.pn()

        def flw(e):
            r = None
            for h in range(4):
                hs = slice(h * 64, (h + 1) * 64)
                r = e.matmul(pbw[:64, h * n:(h + 1) * n], lhsT=self.w2b[:, hs], rhs=LB[:, 0, :], start=True, stop=True)
            return r

        def fla(e):
            r = None
            for h in range(4):
                hs = slice(h * 64, (h + 1) * 64)
                r = e.matmul(pba[:64, h * n:(h + 1) * n], lhsT=self.a2b[:, hs], rhs=LB[:, 1, :], start=True, stop=True)
            return r

        def flg(e):
            r = None
            for h in range(4):
                hs = slice(h * 64, (h + 1) * 64)
                e.matmul(pbg[:64, h * n:(h + 1) * n], lhsT=self.g2b[:, 0, hs], rhs=LB[:, 2, :], start=True, stop=False)
                r = e.matmul(pbg[:64, h * n:(h + 1) * n], lhsT=self.g2b[:, 1, hs], rhs=LB[:, 3, :], start=False, stop=True)
            return r
        S.op('pe', flw, reads=['LB', 'rlw'], writes=[pkw])
        S.op('pe', fla, reads=['LB', 'rlw'], writes=[pka])
        S.op('pe', flg, reads=['LB', 'rlw'], writes=[pkg])
        for h in range(4):
            S.op('act', (lambda e, h=h: e.activation(out=T0[:, h, :], in_=pbw[:64, h * n:(h + 1) * n], func=AF.Sigmoid,
                                                     bias=P64['rwkv_w0'][:, l, h:h + 1])), reads=[pkw, 'pc'], writes=['T0'])
            S.op('act', (lambda e, h=h: e.activation(out=T1[:, h, :], in_=pba[:64, h * n:(h + 1) * n], func=AF.Sigmoid,
                                                     bias=P64['rwkv_a0'][:, l, h:h + 1])), reads=[pka, 'pc'], writes=['T1'])
        S.op('act', lambda e: e.activation(out=Gg, in_=pbg[:64, 0:W].rearrange('p (h t) -> p h t', h=4), func=AF.Copy), reads=[pkg], writes=['Gg'])
        rmk = self.rmask[0:64, 0:64] if sample else self.rm64[:, 0:n]
        for h in range(4):
            S.op('dve', (lambda e, h=h: e.tensor_tensor_scan(out=T2[:, h, :], data0=rmk, data1=T0[:, h, :], initial=0.0, op0=TT.mult, op1=TT.add)),
                 reads=['T0', 'glac', 'rmask'], writes=['T2'])
        S.op('act', lambda e: e.activation(out=T3, in_=T2, func=AF.Exp, scale=-C0), reads=['T2'], writes=['T3'])
        S.op('act', lambda e: e.activation(out=T4, in_=T2, func=AF.Exp, scale=C0), reads=['T2'], writes=['T4'])
        S.op('pool', lambda e: e.tensor_tensor(out=T6, in0=T2, in1=T0, op=TT.subtract), reads=['T2', 'T0'], writes=['T6'])
        S.op('act', lambda e: e.activation(out=T0, in_=T6, func=AF.Exp, scale=-C0), reads=['T6'], writes=['T0'])
        if sample:
            S.op('pool', lambda e: e.tensor_copy(out=GL, in_=T3[:, :, :].rearrange('p h (j t) -> p h j t', t=4)[:, :, :, 3]), reads=['T3'], writes=['rGL'])
        else:
            S.op('pool', lambda e: e.tensor_copy(out=GL[:, :, 0:2], in_=T3[:, :, :].rearrange('p h (c t) -> p h c t', t=64)[:, :, :, 63]),
                 reads=['T3'], writes=['rGL'])
        S.op('dve', lambda e: e.tensor_tensor(out=T5, in0=S1a, in1=bc('rwkv_k_k'), op=TT.mult), reads=['XS', 'pc'], writes=['T5'])
        S.op('dve', lambda e: e.tensor_tensor(out=R(T6), in0=T5, in1=T5, op=TT.mult), reads=['T5', 'T0'], writes=['T6'])
        flat = lambda a: a[:, :, :].rearrange('p h t -> p (h t)')
        pb1, pk1 = self.pn()
        S.op('pe', lambda e: e.matmul(pb1[:64, :W], lhsT=self.ones64[:, :], rhs=R(flat(T6)), start=True, stop=True), reads=['T6', 'glac'], writes=[pk1])
        S.op('act', lambda e: e.activation(out=flat(T6), in_=pb1[:64, :W], func=AF.Sqrt), reads=[pk1], writes=['T6'])
        S.op('dve', lambda e: e.tensor_scalar(out=T6, in0=T6, scalar1=1e-12, scalar2=None, op0=TT.max), reads=['T6'], writes=['T6'])
        S.op('dve', lambda e: e.reciprocal(out=T6, in_=T6), reads=['T6'], writes=['T6'])
        S.op('dve', lambda e: e.tensor_tensor(out=T5, in0=T5, in1=T6, op=TT.mult), reads=['T5', 'T6'], writes=['T5'])
        S.op('dve', lambda e: e.scalar_tensor_tensor(out=T6, in0=T1, scalar=-1.0, in1=bc('rwkv_k_a'), op0=TT.add, op1=TT.mult),
             reads=['T1', 'T5', 'pc'], writes=['T6'])
        S.op('dve', lambda e: e.scalar_tensor_tensor(out=S1a, in0=T6, scalar=1.0, in1=S1a, op0=TT.add, op1=TT.mult), reads=['T6', 'XS'], writes=['XS'])
        S.op('dve', lambda e: e.tensor_tensor(out=T6, in0=S0a, in1=S1a, op=TT.mult), reads=['XS'], writes=['T6'])
        S.op('dve', lambda e: e.tensor_tensor(out=R(T6), in0=T6, in1=bc('rwkv_r_k'), op=TT.mult), reads=['T6', 'pc'], writes=['T6'])
        pb2, pk2 = self.pn()
        S.op('pe', lambda e: e.matmul(pb2[:64, :W], lhsT=self.ones64[:, :], rhs=R(flat(T6)), start=True, stop=True), reads=['T6', 'glac'], writes=[pk2])
        S.op('dve', lambda e: e.tensor_tensor(out=BON, in0=pb2[:64, 0:W].rearrange('p (h t) -> p h t', h=4), in1=S2a, op=TT.mult),
             reads=[pk2, 'XS'], writes=['BON'])
        S.op('dve', lambda e: e.scalar_tensor_tensor(out=R(S3a), in0=T5, scalar=-1.0, in1=T0, op0=TT.mult, op1=TT.mult),
             reads=['T5', 'T0', 'LB'], writes=['XS'])
        S.op('pool', lambda e: e.tensor_tensor(out=T6, in0=T5, in1=T1, op=TT.mult), reads=['T5', 'T1', pk2], writes=['T6'])
        S.op('dve', lambda e: e.tensor_tensor(out=R(S4a), in0=T6, in1=T4, op=TT.mult), reads=['T6', 'T4'], writes=['XS'])
        S.op('dve', lambda e: e.tensor_tensor(out=R(S1a), in0=S1a, in1=T4, op=TT.mult), reads=['XS', 'T4'], writes=['XS'])
        S.op('dve', lambda e: e.tensor_tensor(out=R(S0a), in0=S0a, in1=T3, op=TT.mult), reads=['XS', 'T3'], writes=['XS'])
        S.barrier()
        if sample:
            self.rwkv_chunks(l, n, sample, S0a, S1a, S2a, S3a, S4a, S5a, GL, Sst, TB)
        else:
            self.rwkv_chunks_prompt(n, S0a, S1a, S2a, S3a, S4a, S5a, GL, Sst, TB)
        S.barrier()
        O = S5a
        pbm, pkm = self.pn()
        S.op('pe', lambda e: e.matmul(pbm[:64, :W], lhsT=self.ones64[:, :], rhs=R(flat(O)), start=True, stop=True), reads=['glac'], writes=[pkm])
        S.op('dve', lambda e: e.scalar_tensor_tensor(out=T0, in0=pbm[:64, 0:W].rearrange('p (h t) -> p h t', h=4), scalar=-1.0 / 64, in1=O,
                                                     op0=TT.mult, op1=TT.add), reads=[pkm], writes=['T0'])
        S.op('dve', lambda e: e.tensor_tensor(out=R(T1), in0=T0, in1=T0, op=TT.mult), reads=['T0'], writes=['T1'])
        pbv, pkv = self.pn()
        S.op('pe', lambda e: e.matmul(pbv[:64, :W], lhsT=self.ones64[:, :], rhs=R(flat(T1)), start=True, stop=True), reads=['T1', 'glac'], writes=[pkv])
        S.op('act', lambda e: e.activation(out=flat(T1), in_=pbv[:64, :W], func=AF.Ln, scale=1.0 / 64, bias=self.gn_eps[:]),
             reads=[pkv, 'glac'], writes=['T1'])
        S.op('act', lambda e: e.activation(out=T1, in_=T1, func=AF.Exp, scale=-0.5), reads=['T1'], writes=['T1'])
        S.op('dve', lambda e: e.tensor_tensor(out=T0, in0=T0, in1=T1, op=TT.mult), reads=['T0', 'T1'], writes=['T0'])
        S.op('dve', lambda e: e.tensor_tensor(out=T0, in0=T0, in1=bc('rwkv_ln_g'), op=TT.mult), reads=['T0', 'pc'], writes=['T0'])
        S.op('dve', lambda e: e.tensor_tensor(out=T0, in0=T0, in1=bc('rwkv_ln_b'), op=TT.add), reads=['T0', 'pc'], writes=['T0'])
        S.op('dve', lambda e: e.tensor_tensor(out=T0, in0=T0, in1=BON, op=TT.add), reads=['T0', 'BON'], writes=['T0'])
        S.op('dve', lambda e: e.tensor_tensor(out=R(T0), in0=T0, in1=Gg, op=TT.mult), reads=['T0', 'Gg'], writes=['T0'])
        self.heads_to_ymix(T0, 'T0', n)
        if si == 15:
            pbo, pko = self.pn()

            def fo(e):
                r = None
                for h in range(4):
                    r = e.transpose(out=pbo[:64, h * 64:(h + 1) * 64], in_=Sst[:, h, :], identity=self.ident[0:64, 0:64])
                return r
            S.op('pe', fo, reads=['rS', 'ident'], writes=[pko])
            stg, sk = self.next_stage()
            S.op('act', lambda e: e.activation(out=stg[:64, 0:256], in_=pbo[:64, 0:256], func=AF.Copy), reads=[pko], writes=[sk])
            S.op('sp', lambda e: e.dma_start(out=self.o_p_wkv[l].rearrange('h v k -> v h k'), in_=stg[:64, 0:256].rearrange('p (h k) -> p h k', h=4)),
                 reads=[sk], writes=['o_p_wkv'], dma='o_p_wkv')
            if 'o_p_wkv' not in self.outkeys:
                self.outkeys.append('o_p_wkv')

    def rwkv_unit(self, c, hs, sid, RT, KT, XV, AT, BT, O, GL, Sst, Sr, TB):
        S = self.S
        av = self.av
        L = 64
        nh = len(hs)
        h0 = hs[0]
        cs = slice(c * L, (c + 1) * L)
        W2 = nh * 64
        tv = lambda i: av(TB + sid * 1792 + i * 128, [64, nh, 64])
        M3, M2, M4, Q, QT, Qn, QTn, Tm, Vtok, Ktok, Btok, W0T, Utok, Stmp = [tv(i) for i in range(14)]
        mU, msU, msL = self.triU, self.sU, self.sL
        b4 = lambda m: m[:, :].unsqueeze(1).to_broadcast([64, nh, 64])
        p4 = lambda pb: pb[:64, 0:W2].rearrange('p (h t) -> p h t', h=nh)
        id4 = self.ident[0:64, 0:64].unsqueeze(1).to_broadcast([64, nh, 64])
        k = lambda nm: 'u%d%s' % (sid, nm)
        XSK = 'XS'
        SK = 'rS%d' % sid
        SRK = 'rSr%d' % sid

        def mm(lhs_fn, rhs_fn, reads):
            pb, pk = self.pn()

            def f(e, pb=pb):
                r = None
                for j in range(nh):
                    r = e.matmul(pb[:64, j * 64:(j + 1) * 64], lhsT=R(lhs_fn(j)), rhs=R(rhs_fn(j)), start=True, stop=True)
                return r
            S.op('pe', f, reads=reads, writes=[pk])
            return pb, pk
        A = lambda arr: (lambda j: arr[:, h0 + j, cs])
        Tt = lambda buf: (lambda j: buf[:, j, :])
        pb, pk = mm(A(BT), A(AT), [XSK])
        S.op('dve', (lambda e, pb=pb: e.tensor_tensor(out=R(Q), in0=p4(pb), in1=b4(msU), op=ALU.mult)), reads=[pk, 'glac'], writes=[k('Q')])
        pb, pk = mm(A(AT), A(BT), [XSK])
        S.op('dve', (lambda e, pb=pb: e.tensor_tensor(out=R(QT), in0=p4(pb), in1=b4(msL), op=ALU.mult)), reads=[pk, 'glac'], writes=[k('QT')])
        yield
        pb, pk = mm(A(BT), A(RT), [XSK])
        S.op('dve', (lambda e, pb=pb: e.tensor_tensor(out=R(M3), in0=p4(pb), in1=b4(mU), op=ALU.mult)), reads=[pk, 'glac'], writes=[k('M3')])
        pb, pk = mm(A(KT), A(AT), [XSK])
        S.op('dve', (lambda e, pb=pb: e.tensor_tensor(out=R(M2), in0=p4(pb), in1=b4(msU), op=ALU.mult)), reads=[pk, 'glac'], writes=[k('M2')])
        pb, pk = mm(A(KT), A(RT), [XSK])
        S.op('dve', (lambda e, pb=pb: e.tensor_tensor(out=R(M4), in0=p4(pb), in1=b4(mU), op=ALU.mult)), reads=[pk, 'glac'], writes=[k('M4')])
        S.op('dve', lambda e: e.tensor_tensor(out=R(Tm), in0=Q, in1=id4, op=ALU.add), reads=[k('Q'), 'ident'], writes=[k('T')])
        yield
        q, qt, qn, qtn = Q, QT, Qn, QTn
        kq, kqt, kqn, kqtn = k('Q'), k('QT'), k('Qn'), k('QTn')
        nlev = 5
        for lev in range(nlev):
            last = (lev == nlev - 1)
            pbq2, pkq2 = mm(Tt(q), Tt(qt), [kq, kqt])
            S.op('act', (lambda e, pb=pbq2, qtn=qtn: e.activation(out=R(qtn), in_=p4(pb), func=AF.Copy)), reads=[pkq2], writes=[kqtn])
            if not last:
                pbq1, pkq1 = mm(Tt(qt), Tt(q), [kq, kqt])
                S.op('dve', (lambda e, pb=pbq1, qn=qn: e.tensor_copy(out=R(qn), in_=p4(pb))), reads=[pkq1], writes=[kqn])
            yield
            pbt, pkt = mm(Tt(qtn), Tt(Tm), [kqtn, k('T')])
            S.op('dve', (lambda e, pb=pbt: e.tensor_tensor(out=R(Tm), in0=p4(pb), in1=Tm, op=ALU.add)), reads=[pkt, k('T')], writes=[k('T')])
            q, qt, qn, qtn = qn, qtn, q, qt
            kq, kqt, kqn, kqtn = kqn, kqtn, kq, kqt
            if lev == 1:
                for src, dst, dkey in ((XV, Vtok, k('Vtok')), (KT, Ktok, k('Ktok')), (BT, Btok, k('Btok'))):
                    pbT, pkT = self.pn()

                    def fT(e, pbT=pbT, src=src):
                        r = None
                        for j in range(nh):
                            r = e.transpose(out=pbT[:64, j * 64:(j + 1) * 64], in_=src[:, h0 + j, cs], identity=self.ident[0:64, 0:64])
                        return r
                    S.op('pe', fT, reads=[XSK, 'ident'], writes=[pkT])
                    S.op('act', (lambda e, pbT=pbT, dst=dst: e.activation(out=R(dst), in_=p4(pbT), func=AF.Copy)), reads=[pkT], writes=[dkey])
            yield
        pbw, pkw = self.pn()

        def fW(e, pbw=pbw):
            r = None
            for j in range(nh):
                o_ = pbw[:64, j * 64:(j + 1) * 64]
                e.matmul(o_, lhsT=R(AT[:, h0 + j, cs]), rhs=R(Sr[:, h0 + j, :]), start=True, stop=False)
                r = e.matmul(o_, lhsT=R(M2[:, j, :]), rhs=R(Vtok[:, j, :]), start=False, stop=True)
            return r
        S.op('pe', fW, reads=[XSK, SRK, k('M2'), k('Vtok')], writes=[pkw])
        S.op('act', (lambda e, pbw=pbw: e.activation(out=R(W0T), in_=p4(pbw), func=AF.Copy)), reads=[pkw], writes=[k('W0T')])
        yield
        pbu, pku = mm(Tt(Tm), Tt(W0T), [k('T'), k('W0T')])
        S.op('act', (lambda e, pbu=pbu: e.activation(out=R(Utok), in_=p4(pbu), func=AF.Copy)), reads=[pku], writes=[k('Utok')])
        yield
        pby, pky = self.pn()

        def fY(e, pby=pby):
            r = None
            for j in range(nh):
                o_ = pby[:64, j * 64:(j + 1) * 64]
                e.matmul(o_, lhsT=R(Sr[:, h0 + j, :]), rhs=R(RT[:, h0 + j, cs]), start=True, stop=False)
                e.matmul(o_, lhsT=R(Utok[:, j, :]), rhs=R(M3[:, j, :]), start=False, stop=False)
                r = e.matmul(o_, lhsT=R(Vtok[:, j, :]), rhs=R(M4[:, j, :]), start=False, stop=True)
            return r
        S.op('pe', fY, reads=[XSK, SRK, k('Utok'), k('M3'), k('Vtok'), k('M4')], writes=[pky])
        pbs, pks = self.pn()

        def fS(e, pbs=pbs):
            r = None
            for j in range(nh):
                o_ = pbs[:64, j * 64:(j + 1) * 64]
                e.matmul(o_, lhsT=R(Btok[:, j, :]), rhs=R(Utok[:, j, :]), start=True, stop=False)
                r = e.matmul(o_, lhsT=R(Ktok[:, j, :]), rhs=R(Vtok[:, j, :]), start=False, stop=True)
            return r
        S.op('pe', fS, reads=[k('Btok'), k('Utok'), k('Ktok'), k('Vtok')], writes=[pks])
        S.op('act', (lambda e, pby=pby: e.activation(out=R(O[:, h0:h0 + nh, cs]), in_=p4(pby), func=AF.Copy)), reads=[pky], writes=['rO%d' % sid])
        S.op('dve', (lambda e, pbs=pbs: e.tensor_tensor(out=Stmp, in0=p4(pbs), in1=Sst[:, h0:h0 + nh, :], op=ALU.add)), reads=[pks, SK], writes=[k('Stmp')])
        S.op('dve', lambda e: e.tensor_tensor(out=Sst[:, h0:h0 + nh, :], in0=Stmp, in1=GL[:, h0:h0 + nh, c:c + 1].to_broadcast([64, nh, 64]), op=ALU.mult),
             reads=[k('Stmp'), 'rGL'], writes=[SK])
        S.op('act', lambda e: e.activation(out=R(Sr[:, h0:h0 + nh, :]), in_=Sst[:, h0:h0 + nh, :], func=AF.Copy), reads=[SK], writes=[SRK])
        yield

    def rwkv_chunks_prompt(self, n, RT, KT, XV, AT, BT, O, GL, Sst, TB):
        S = self.S
        av = self.av
        nch = n // 64
        Sr = av(TB + 14 * 256, [64, 4, 64])
        chains = []
        for sid, hs in enumerate(([0, 1], [2, 3])):
            S.op('act', (lambda e, hs=hs: e.activation(out=R(Sr[:, hs[0]:hs[0] + 2, :]), in_=Sst[:, hs[0]:hs[0] + 2, :], func=AF.Copy)),
                 reads=['rS', 'rS%d' % sid], writes=['rSr%d' % sid])

            def chain(sid=sid, hs=hs):
                for c in range(nch):
                    yield from self.rwkv_unit(c, hs, sid, RT, KT, XV, AT, BT, O, GL, Sst, Sr, TB)
            chains.append(chain())
        alive = list(chains)
        while alive:
            for g in list(alive):
                try:
                    next(g)
                except StopIteration:
                    alive.remove(g)

    def rwkv_chunks(self, l, n, sample, RT, KT, XV, AT, BT, O, GL, Sst, TB):
        S = self.S
        av = self.av
        L = 64
        nch = n // L
        tv = lambda i: av(TB + i * 256, [64, 4, 64])
        M3, M2, M4, Q, QT, Qn, QTn, Tm, Vtok, Ktok, Btok, W0T, Utok, Stmp = [tv(i) for i in range(14)]
        Sr = av(TB + 14 * 256, [64, 4, 64])
        mU, msU, msL = (self.blkU, self.bsU, self.bsL) if sample else (self.triU, self.sU, self.sL)
        nlev = 1 if sample else 5
        b4 = lambda m: m[:, :].unsqueeze(1).to_broadcast([64, 4, 64])
        p4 = lambda pb: pb[:64, 0:256].rearrange('p (h t) -> p h t', h=4)
        id4 = self.ident[0:64, 0:64].unsqueeze(1).to_broadcast([64, 4, 64])
        KK = 'rc'

        def mm4(lhs_fn, rhs_fn, reads):
            pb, pk = self.pn()

            def f(e, pb=pb):
                r = None
                for h in range(4):
                    r = e.matmul(pb[:64, h * 64:(h + 1) * 64], lhsT=R(lhs_fn(h)), rhs=R(rhs_fn(h)), start=True, stop=True)
                return r
            S.op('pe', f, reads=reads, writes=[pk])
            return pb, pk

        if sample:
            S0h = av(TB + 14 * 256, [64, 16, 64])
            Uexp = av(TB + 14 * 256 + 1024, [64, 16, 64])
            hidf = self.hid[:].rearrange('p a b -> p (a b)').bitcast(F32)
            Vexp = hidf[0:64, 1024:2048].rearrange('p (j v) -> p j v', j=16)
        if not sample:
            S.op('act', lambda e: e.activation(out=R(Sr), in_=Sst, func=AF.Copy), reads=['rS'], writes=['rSr'])
        for c in range(nch):
            cs = slice(c * L, (c + 1) * L)
            pb, pk = mm4(lambda h, cs=cs: BT[:, h, cs], lambda h, cs=cs: AT[:, h, cs], ['XS'])
            S.op('dve', (lambda e, pb=pb: e.tensor_tensor(out=R(Q), in0=p4(pb), in1=b4(msU), op=ALU.mult)), reads=[pk, 'glac'], writes=['rQ'])
            pb, pk = mm4(lambda h, cs=cs: AT[:, h, cs], lambda h, cs=cs: BT[:, h, cs], ['XS'])
            S.op('dve', (lambda e, pb=pb: e.tensor_tensor(out=R(QT), in0=p4(pb), in1=b4(msL), op=ALU.mult)), reads=[pk, 'glac'], writes=['rQT'])
            pb, pk = mm4(lambda h, cs=cs: BT[:, h, cs], lambda h, cs=cs: RT[:, h, cs], ['XS'])
            S.op('dve', (lambda e, pb=pb: e.tensor_tensor(out=R(M3), in0=p4(pb), in1=b4(mU), op=ALU.mult)), reads=[pk, 'glac'], writes=['rM3'])
            pb, pk = mm4(lambda h, cs=cs: KT[:, h, cs], lambda h, cs=cs: AT[:, h, cs], ['XS'])
            S.op('dve', (lambda e, pb=pb: e.tensor_tensor(out=R(M2), in0=p4(pb), in1=b4(msU), op=ALU.mult)), reads=[pk, 'glac'], writes=['rM2'])
            pb, pk = mm4(lambda h, cs=cs: KT[:, h, cs], lambda h, cs=cs: RT[:, h, cs], ['XS'])
            S.op('dve', (lambda e, pb=pb: e.tensor_tensor(out=R(M4), in0=p4(pb), in1=b4(mU), op=ALU.mult)), reads=[pk, 'glac'], writes=['rM4'])
            S.op('dve', lambda e: e.tensor_tensor(out=R(Tm), in0=Q, in1=id4, op=ALU.add), reads=['rQ', 'ident'], writes=['rT'])
            q, qt, qn, qtn = Q, QT, Qn, QTn
            kq, kqt, kqn, kqtn = 'rQ', 'rQT', 'rQn', 'rQTn'
            for lev in range(nlev):
                last = (lev == nlev - 1)
                pbq2, pkq2 = mm4(lambda h, q=q: q[:, h, :], lambda h, qt=qt: qt[:, h, :], [kq, kqt])
                S.op('act', (lambda e, pb=pbq2, qtn=qtn: e.activation(out=R(qtn), in_=p4(pb), func=AF.Copy)), reads=[pkq2], writes=[kqtn])
                if not last:
                    pbq1, pkq1 = mm4(lambda h, qt=qt: qt[:, h, :], lambda h, q=q: q[:, h, :], [kq, kqt])
                    S.op('dve', (lambda e, pb=pbq1, qn=qn: e.tensor_copy(out=R(qn), in_=p4(pb))), reads=[pkq1], writes=[kqn])
                pbt, pkt = mm4(lambda h, qtn=qtn: qtn[:, h, :], lambda h: Tm[:, h, :], [kqtn, 'rT'])
                S.op('dve', (lambda e, pb=pbt: e.tensor_tensor(out=R(Tm), in0=p4(pb), in1=Tm, op=ALU.add)), reads=[pkt, 'rT'], writes=['rT'])
                q, qt, qn, qtn = qn, qtn, q, qt
                kq, kqt, kqn, kqtn = kqn, kqtn, kq, kqt
            for src, dst, dkey in ((XV, Vtok, 'rVtok'), (KT, Ktok, 'rKtok'), (BT, Btok, 'rBtok')):
                pbT, pkT = self.pn()

                def fT(e, pbT=pbT, src=src, cs=cs):
                    r = None
                    for h in range(4):
                        r = e.transpose(out=pbT[:64, h * 64:(h + 1) * 64], in_=src[:, h, cs], identity=self.ident[0:64, 0:64])
                    return r
                S.op('pe', fT, reads=['XS', 'ident'], writes=[pkT])
                S.op('act', (lambda e, pbT=pbT, dst=dst: e.activation(out=R(dst), in_=p4(pbT), func=AF.Copy)), reads=[pkT], writes=[dkey])
            if not sample:
                pbw, pkw = self.pn()

                def fW(e, pbw=pbw, cs=cs):
                    r = None
                    for h in range(4):
                        o_ = pbw[:64, h * 64:(h + 1) * 64]
                        e.matmul(o_, lhsT=R(AT[:, h, cs]), rhs=R(Sr[:, h, :]), start=True, stop=False)
                        r = e.matmul(o_, lhsT=R(M2[:, h, :]), rhs=R(Vtok[:, h, :]), start=False, stop=True)
                    return r
                S.op('pe', fW, reads=['XS', 'rSr', 'rM2', 'rVtok'], writes=[pkw])
                S.op('act', (lambda e, pbw=pbw: e.activation(out=R(W0T), in_=p4(pbw), func=AF.Copy)), reads=[pkw], writes=['rW0T'])
                pbu, pku = mm4(lambda h: Tm[:, h, :], lambda h: W0T[:, h, :], ['rT', 'rW0T'])
                S.op('act', (lambda e, pbu=pbu: e.activation(out=R(Utok), in_=p4(pbu), func=AF.Copy)), reads=[pku], writes=['rUtok'])
                pby, pky = self.pn()

                def fY(e, pby=pby, cs=cs):
                    r = None
                    for h in range(4):
                        o_ = pby[:64, h * 64:(h + 1) * 64]
                        e.matmul(o_, lhsT=R(Sr[:, h, :]), rhs=R(RT[:, h, cs]), start=True, stop=False)
                        e.matmul(o_, lhsT=R(Utok[:, h, :]), rhs=R(M3[:, h, :]), start=False, stop=False)
                        r = e.matmul(o_, lhsT=R(Vtok[:, h, :]), rhs=R(M4[:, h, :]), start=False, stop=True)
                    return r
                S.op('pe', fY, reads=['XS', 'rSr', 'rUtok', 'rM3', 'rVtok', 'rM4'], writes=[pky])
                S.op('act', (lambda e, pby=pby, cs=cs: e.activation(out=O[:, :, cs], in_=p4(pby), func=AF.Copy)), reads=[pky], writes=['rO'])
                pbs, pks = self.pn()

                def fS(e, pbs=pbs):
                    r = None
                    for h in range(4):
                        o_ = pbs[:64, h * 64:(h + 1) * 64]
                        e.matmul(o_, lhsT=R(Btok[:, h, :]), rhs=R(Utok[:, h, :]), start=True, stop=False)
                        r = e.matmul(o_, lhsT=R(Ktok[:, h, :]), rhs=R(Vtok[:, h, :]), start=False, stop=True)
                    return r
                S.op('pe', fS, reads=['rBtok', 'rUtok', 'rKtok', 'rVtok'], writes=[pks])
                S.op('dve', (lambda e, pbs=pbs: e.tensor_tensor(out=Stmp, in0=p4(pbs), in1=Sst, op=ALU.add)), reads=[pks, 'rS'], writes=['rStmp'])
                S.op('dve', (lambda e, c=c: e.tensor_tensor(out=Sst, in0=Stmp, in1=GL[:, :, c:c + 1].to_broadcast([64, 4, 64]), op=ALU.mult)),
                     reads=['rStmp', 'rGL'], writes=['rS'])
                S.op('act', lambda e: e.activation(out=R(Sr), in_=Sst, func=AF.Copy), reads=['rS'], writes=['rSr'])
            else:
                for h in range(4):
                    stg, sk = self.next_stage()
                    for jj in range(2):
                        S.op('sp', (lambda e, stg=stg, h=h, jj=jj: e.dma_start(
                            out=stg[jj * 64:(jj + 1) * 64, 0:512].rearrange('p (a k) -> p a k', a=8),
                            in_=self.st_wkv[l, :, h].rearrange('(jp jj) v k -> jj v jp k', jj=2)[jj])), writes=[sk], dma=sk)
                    for half in range(2):
                        pbl, pkl = self.pn()

                        def fl(e, pbl=pbl, stg=stg, half=half):
                            r = None
                            for a in range(4):
                                jp = half * 4 + a
                                r = e.transpose(out=pbl[:64, a * 128:(a + 1) * 128], in_=stg[:, jp * 64:(jp + 1) * 64], identity=self.ident[:, :])
                            return r
                        S.op('pe', fl, reads=[sk, 'ident'], writes=[pkl])
                        S.op('act', (lambda e, pbl=pbl, half=half: e.activation(out=R(S0h[:, half * 8:(half + 1) * 8, :]),
                                                                               in_=pbl[:64, :].rearrange('p (j v) -> p j v', j=8), func=AF.Copy)),
                             reads=[pkl], writes=['rS0h'])
                    pbw, pkw = self.pn()

                    def fW(e, pbw=pbw, h=h):
                        e.matmul(pbw[:64, 0:64], lhsT=R(Vtok[:, h, :]), rhs=R(M2[:, h, :]), start=True, stop=False)
                        r = None
                        for j in range(NS):
                            r = e.matmul(pbw[:64, 4 * j:4 * j + 4], lhsT=R(S0h[:, j, :]), rhs=R(AT[:, h, 4 * j:4 * j + 4]), start=False, stop=(j == NS - 1))
                        return r
                    S.op('pe', fW, reads=['XS', 'rS0h', 'rM2', 'rVtok'], writes=[pkw])
                    S.op('act', (lambda e, pbw=pbw, h=h: e.activation(out=Stmp[:, h, :], in_=pbw[:64, 0:64], func=AF.Copy)), reads=[pkw], writes=['rStmp'])
                    pbx, pkx = self.pn()
                    S.op('pe', (lambda e, pbx=pbx, h=h: e.transpose(out=pbx[:64, 0:64], in_=Stmp[:, h, :], identity=self.ident[0:64, 0:64])),
                         reads=['rStmp', 'ident'], writes=[pkx])
                    S.op('act', (lambda e, pbx=pbx, h=h: e.activation(out=R(W0T[:, h, :]), in_=pbx[:64, 0:64], func=AF.Copy)), reads=[pkx], writes=['rW0T'])
                    pbu, pku = self.pn()
                    S.op('pe', (lambda e, pbu=pbu, h=h: e.matmul(pbu[:64, 0:64], lhsT=R(Tm[:, h, :]), rhs=R(W0T[:, h, :]), start=True, stop=True)),
                         reads=['rT', 'rW0T'], writes=[pku])
                    S.op('act', (lambda e, pbu=pbu, h=h: e.activation(out=R(Utok[:, h, :]), in_=pbu[:64, 0:64], func=AF.Copy)), reads=[pku], writes=['rUtok'])
                    pby, pky = self.pn()

                    def fY(e, pby=pby, h=h):
                        e.matmul(pby[:64, 0:64], lhsT=R(Utok[:, h, :]), rhs=R(M3[:, h, :]), start=True, stop=False)
                        e.matmul(pby[:64, 0:64], lhsT=R(Vtok[:, h, :]), rhs=R(M4[:, h, :]), start=False, stop=False)
                        r = None
                        for j in range(NS):
                            r = e.matmul(pby[:64, 4 * j:4 * j + 4], lhsT=R(S0h[:, j, :]), rhs=R(RT[:, h, 4 * j:4 * j + 4]), start=False, stop=(j == NS - 1))
                        return r
                    S.op('pe', fY, reads=['XS', 'rS0h', 'rUtok', 'rM3', 'rVtok', 'rM4'], writes=[pky])
                    S.op('act', (lambda e, pby=pby, h=h: e.activation(out=R(O[:, h, :]), in_=pby[:64, 0:64], func=AF.Copy)), reads=[pky], writes=['rO'])
                    segb = self.segm[:, :].unsqueeze(2).to_broadcast([64, 16, 64])
                    S.op('dve', (lambda e, h=h: e.tensor_tensor(out=R(Uexp), in0=Utok[:, h, :].unsqueeze(1).to_broadcast([64, 16, 64]), in1=segb, op=ALU.mult)),
                         reads=['rUtok', 'glac'], writes=['rUexp'])
                    S.op('dve', (lambda e, h=h: e.tensor_tensor(out=Vexp, in0=Vtok[:, h, :].unsqueeze(1).to_broadcast([64, 16, 64]), in1=segb, op=ALU.mult)),
                         reads=['rVtok', 'glac'], writes=['rVexp'])
                    for jj in range(2):
                        js = slice(8 * jj, 8 * jj + 8)
                        pbs, pks = self.pn()

                        def fS(e, pbs=pbs, h=h, js=js):
                            e.matmul(pbs[:64, :], lhsT=R(Btok[:, h, :]), rhs=R(Uexp[:, js, :]), start=True, stop=False)
                            return e.matmul(pbs[:64, :], lhsT=Ktok[:, h, :], rhs=Vexp[:, js, :], start=False, stop=True)
                        S.op('pe', fS, reads=['rBtok', 'rUexp', 'rKtok', 'rVexp'], writes=[pks])
                        S.op('dve', (lambda e, pbs=pbs, js=js: e.tensor_tensor(out=S0h[:, js, :], in0=pbs[:64, :].rearrange('p (j v) -> p j v', j=8),
                                                                               in1=S0h[:, js, :], op=ALU.add)), reads=[pks, 'rS0h'], writes=['rS0h'])
                        S.op('dve', (lambda e, h=h, js=js: e.tensor_tensor(out=S0h[:, js, :], in0=S0h[:, js, :],
                                                                           in1=GL[:, h, js].unsqueeze(2).to_broadcast([64, 8, 64]), op=ALU.mult)),
                             reads=['rS0h', 'rGL'], writes=['rS0h'])
                    pbo, pko = self.pn()

                    def fo(e, pbo=pbo):
                        r = None
                        for jp in range(8):
                            r = e.transpose(out=pbo[:, jp * 64:(jp + 1) * 64], in_=S0h[:, 2 * jp:2 * jp + 2, :].rearrange('p j v -> p (j v)'),
                                            identity=self.ident[0:64, 0:64])
                        return r
                    S.op('pe', fo, reads=['rS0h', 'ident'], writes=[pko])
                    stg2, sk2 = self.next_stage()
                    S.op('act', (lambda e, pbo=pbo, stg2=stg2: e.activation(out=stg2[:, 0:512], in_=pbo[:, 0:512], func=AF.Copy)), reads=[pko], writes=[sk2])
                    for jj in range(2):
                        S.op('sp', (lambda e, stg2=stg2, h=h, jj=jj: e.dma_start(
                            out=self.o_s_wkv[l, :, h].rearrange('(jp jj) v k -> jj v jp k', jj=2)[jj],
                            in_=stg2[jj * 64:(jj + 1) * 64, 0:512].rearrange('p (a k) -> p a k', a=8))),
                            reads=[sk2], writes=['o_s_wkv'], dma='o_s_wkv')
                if 'o_s_wkv' not in self.outkeys:
                    self.outkeys.append('o_s_wkv')

    def shift_capture(self, l, ti, j, pb, pk):
        S = self.S
        if ti == 3:
            S.op('act', (lambda e: e.activation(out=self.shp[:, j:j + 1], in_=pb[:, 511:512], func=AF.Copy)), reads=[pk], writes=['shp'])
        elif ti == 4:
            S.op('act', (lambda e: e.activation(out=self.shs[:, j, :], in_=pb[:, 0:64].rearrange('p (s t) -> p s t', t=4)[:, :, 3], func=AF.Copy)),
                 reads=[pk], writes=['shs'])

    def shift_store(self, l, ti):
        S = self.S
        if ti == 3:
            S.op('sp', lambda e: e.dma_start(out=self.o_p_shift[l].rearrange('(t p) -> p t', p=128), in_=self.shp[:, :]),
                 reads=['shp'], writes=['o_p_shift'], dma='o_p_shift')
            if 'o_p_shift' not in self.outkeys:
                self.outkeys.append('o_p_shift')
        elif ti == 4:
            for half in range(2):
                self.store_fm([(self.shs[:, half * 4 + j, :], 'shs') for j in range(4)], 16,
                              self.o_s_shift[l][:, half * 512:(half + 1) * 512], 'o_s_shift')

    def pool_sample(self, l, win, slot):
        S = self.S
        TT = ALU
        ti = 4
        c0, n = TTS[ti]
        pext, ps2, ps4, pooled, u_s = self.pext_s, self.ps2_s, self.ps4_s, self.pooled, self.u_s
        wins = [2, 4, 8, 16]
        for blk in range(2):
            stg, sk = self.next_stage()
            S.op('sp', (lambda e, stg=stg, blk=blk: e.dma_start(
                out=stg[:120, 0:256], in_=self.st_pool[l, blk * 8:(blk + 1) * 8].rearrange('j r f -> (j r) f'))), writes=[sk], dma=sk)
            pb, pk = self.pn()

            def f(e, pb=pb, stg=stg):
                r = None
                for t in range(2):
                    r = e.transpose(out=pb[:, t * 128:t * 128 + 120], in_=stg[:120, t * 128:(t + 1) * 128], identity=self.ident[:120, :120])
                return r
            S.op('pe', f, reads=[sk, 'ident'], writes=[pk])
            for t in range(2):
                S.op('act', (lambda e, pb=pb, t=t, blk=blk: e.activation(
                    out=pext[:, t, blk * 8:(blk + 1) * 8, 1:16], in_=pb[:, t * 128:t * 128 + 120].rearrange('p (j r) -> p j r', j=8), func=AF.Copy)),
                    reads=[pk], writes=['pext_s'])
        for j in range(2):
            pb, pk = self.proj(win, j, ti, slot)
            S.op('act', (lambda e, pb=pb, j=j: e.activation(out=u_s[:, j, :], in_=pb[:, :64], func=AF.Copy)), reads=[pk], writes=['u_s'])
            S.op('pool', (lambda e, j=j: e.tensor_copy(out=pext[:, j, :, 16:20], in_=u_s[:, j, :].rearrange('p (s t) -> p s t', t=4))),
                 reads=['u_s'], writes=['pext_s'])
        W = 20
        for t in range(2):
            S.op('pool', (lambda e, t=t: e.tensor_tensor(out=ps2[:, t, :, 1:W], in0=pext[:, t, :, 1:W], in1=pext[:, t, :, 0:W - 1], op=TT.add)),
                 reads=['pext_s'], writes=[('ps2_s', t)])
            S.op('pool', (lambda e, t=t: e.tensor_tensor(out=ps4[:, t, :, 3:W], in0=ps2[:, t, :, 3:W], in1=ps2[:, t, :, 1:W - 2], op=TT.add)),
                 reads=[('ps2_s', t)], writes=[('ps4_s', t)])

        def grp(e, src, g):
            t, hf = g // 2, g % 2
            sl = slice(hf * 64, (hf + 1) * 64)
            return e.scalar_tensor_tensor(out=pooled[sl, t, 0:64].rearrange('p (s t) -> p s t', t=4), in0=src[sl, t, :, 16:20], scalar=1.0 / wins[g],
                                          in1=pext[sl, t, :, 16:20], op0=TT.mult, op1=TT.subtract)
        S.op('dve', lambda e: grp(e, ps2, 0), reads=[('ps2_s', 0), 'pext_s'], writes=['pooled'])
        S.op('dve', lambda e: grp(e, ps4, 1), reads=[('ps4_s', 0), 'pext_s'], writes=['pooled'])
        t = 1
        S.op('pool', lambda e: e.tensor_tensor(out=ps2[:, t, :, 7:W], in0=ps4[:, t, :, 7:W], in1=ps4[:, t, :, 3:W - 4], op=TT.add),
             reads=[('ps4_s', 1)], writes=[('ps2_s', 1)])
        S.op('dve', lambda e: grp(e, ps2, 2), reads=[('ps2_s', 1), 'pext_s'], writes=['pooled'])
        S.op('pool', lambda e: e.tensor_tensor(out=ps4[:, t, :, 15:W], in0=ps2[:, t, :, 15:W], in1=ps2[:, t, :, 7:W - 8], op=TT.add),
             reads=[('ps2_s', 1), 'pooled'], writes=[('ps4_s', 1)])
        S.op('dve', lambda e: grp(e, ps4, 3), reads=[('ps4_s', 1), 'pext_s'], writes=['pooled'])
        self.pool_out(l, ti)
        S.op('sp', lambda e: e.dma_start(out=self.o_s_pool[l][:, 0:11, :], in_=self.st_pool[l][:, 4:15, :]), writes=['o_s_pool'], dma='o_s_pool')
        stg, sk = self.next_stage()
        pb, pk = self.pn()

        def f(e):
            r = None
            for t in range(2):
                r = e.transpose(out=pb[:64, t * 128:(t + 1) * 128], in_=u_s[:, t, :], identity=self.ident[:, :])
            return r
        S.op('pe', f, reads=['u_s', 'ident'], writes=[pk])
        S.op('act', lambda e: e.activation(out=stg[:64, 0:256], in_=pb[:64, 0:256], func=AF.Copy), reads=[pk], writes=[sk])
        for j in range(NS):
            S.op('sp', (lambda e, j=j: e.dma_start(out=self.o_s_pool[l][j, 11:15, :], in_=stg[4 * j:4 * j + 4, 0:256])),
                 reads=[sk], writes=['o_s_pool'], dma='o_s_pool')
        if 'o_s_pool' not in self.outkeys:
            self.outkeys.append('o_s_pool')

    def mix_pool(self, l, ti, win, slot):
        if not ENABLE['pool']:
            return self.zero_ymix(ti)
        S = self.S
        c0, n = TTS[ti]
        pext, ps2, ps4, pooled = self.pext, self.ps2, self.ps4, self.pooled
        wins = [2, 4, 8, 16]
        if ti < 4:
            if ti == 0:
                S.op('pool', lambda e: e.memset(pext[:, :, 0:16], 0.0), writes=['pext'])
            else:
                S.op('pool', lambda e: e.tensor_copy(out=pext[:, :, 0:16], in_=pext[:, :, 512:528]),
                     reads=['pext'], writes=['pext'])
            for j in range(2):
                pb, pk = self.proj(win, j, ti, slot)
                S.op('act', (lambda e, pb=pb, j=j: e.activation(out=pext[:, j, 16:16 + n], in_=pb[:, :n], func=AF.Copy)),
                     reads=[pk], writes=['pext'])
            W = 16 + n
            x3 = lambda a, b: pext[:, :, a:b]
            S.op('pool', lambda e: e.tensor_tensor(out=ps2[:, :, 1:W], in0=pext[:, :, 1:W], in1=pext[:, :, 0:W - 1], op=ALU.add),
                 reads=['pext'], writes=['ps2'])
            S.op('pool', lambda e: e.tensor_tensor(out=ps4[:, :, 3:W], in0=ps2[:, :, 3:W], in1=ps2[:, :, 1:W - 2], op=ALU.add),
                 reads=['ps2'], writes=['ps4'])
            def grp(e, src, g):
                t, hf = g // 2, g % 2
                sl = slice(hf * 64, (hf + 1) * 64)
                return e.scalar_tensor_tensor(out=pooled[sl, t, :n], in0=src[sl, t, 16:16 + n], scalar=1.0 / wins[g],
                                              in1=pext[sl, t, 16:16 + n], op0=ALU.mult, op1=ALU.subtract)
            S.op('dve', lambda e: grp(e, ps2, 0), reads=['ps2', 'pext'], writes=['pooled'])
            S.op('dve', lambda e: grp(e, ps4, 1), reads=['ps4', 'pext'], writes=['pooled'])
            if ti == 0:
                self.pool_fix(ps2, 0)
                self.pool_fix(ps4, 1)
            S.op('pool', lambda e: e.tensor_tensor(out=ps2[:, :, 7:W], in0=ps4[:, :, 7:W], in1=ps4[:, :, 3:W - 4], op=ALU.add),
                 reads=['ps4', 'pooled'], writes=['ps2'])
            S.op('dve', lambda e: grp(e, ps2, 2), reads=['ps2', 'pext'], writes=['pooled'])
            if ti == 0:
                self.pool_fix(ps2, 2)
            S.op('pool', lambda e: e.tensor_tensor(out=ps4[:, :, 15:W], in0=ps2[:, :, 15:W], in1=ps2[:, :, 7:W - 8], op=ALU.add),
                 reads=['ps2', 'pooled'], writes=['ps4'])
            S.op('dve', lambda e: grp(e, ps4, 3), reads=['ps4', 'pext'], writes=['pooled'])
            if ti == 0:
                self.pool_fix(ps4, 3)
            self.pool_out(l, ti)
            if ti == 3:
                self.store_fm([(pext[:, t, 16 + 512 - 15:16 + 512], 'pext') for t in range(2)], 15,
                              self.o_p_pool[l], 'o_p_pool')
        else:
            self.pool_sample(l, win, slot)

    def pool_fix(self, src, g):
        S = self.S
        t, hf = g // 2, g % 2
        sl = slice(hf * 64, (hf + 1) * 64)
        pooled, pext, tab = self.pooled, self.pext, self.pool_tab
        S.op('dve', lambda e: e.tensor_tensor(out=pooled[sl, t, 0:16], in0=src[sl, t, 16:32], in1=tab[sl, t, :], op=ALU.mult),
             reads=['pool_tab'], writes=['pooled'])
        S.op('dve', lambda e: e.tensor_tensor(out=pooled[sl, t, 0:16], in0=pooled[sl, t, 0:16], in1=pext[sl, t, 16:32], op=ALU.subtract),
             reads=['pooled', 'pext'], writes=['pooled'])

    def pool_out(self, l, ti):
        S = self.S
        c0, n = TTS[ti]
        pooled, ymix, pw = self.pooled, self.ymix, self.poolw
        for t in range(2):
            pb, pk = self.pn()
            S.op('pe', (lambda e, pb=pb, t=t: e.matmul(pb[:, :n], lhsT=pw[:, l, t, :], rhs=pooled[:, t, :n], start=True, stop=True)),
                 reads=['pooled', 'poolw'], writes=[pk])
            S.op('dve', (lambda e, pb=pb, t=t: e.tensor_scalar(out=ymix[:, t, :n], in0=pb[:, :n],
                                                               scalar1=self.pc['pool_scale'][:, l, t:t + 1], scalar2=None, op0=ALU.mult)),
                 reads=[pk, 'pc'], writes=['ymix'])

    def store_fm(self, blocks, ncols, dram_ap, okey):
        S = self.S
        stg, sk = self.next_stage()
        pb, pk = self.pn()
        nb = len(blocks)
        assert nb <= 4

        def f(e):
            r = None
            for j, (ap, k) in enumerate(blocks):
                r = e.transpose(out=pb[:ncols, j * 128:(j + 1) * 128], in_=ap, identity=self.ident[:, :])
            return r
        S.op('pe', f, reads=[k for _, k in blocks] + ['ident'], writes=[pk])
        S.op('act', lambda e: e.activation(out=stg[:ncols, :nb * 128], in_=pb[:ncols, :nb * 128], func=AF.Copy),
             reads=[pk], writes=[sk])
        S.op('sp', lambda e: e.dma_start(out=dram_ap, in_=stg[:ncols, :nb * 128]), reads=[sk], writes=[okey], dma=okey)
        if okey not in self.outkeys:
            self.outkeys.append(okey)

    def final(self):
        S = self.S
        xf = self.xf
        S.barrier()
        for ti in range(5):
            c0, n = TTS[ti]
            self.rmsnorm(ti, lambda kt: self.pc_normf[:, kt:kt + 1],
                         lambda kt, n: (xf[:, kt, :n], ('xf', kt)))
            for s0 in range(0, n, 128):
                w = min(128, n - s0)
                for half in range(2):
                    stg, sk = self.next_stage()
                    pb, pk = self.pn()

                    def f(e, pb=pb, half=half, s0=s0, w=w):
                        r = None
                        for j in range(4):
                            r = e.transpose(out=pb[:w, j * 128:(j + 1) * 128], in_=xf[:, half * 4 + j, s0:s0 + w],
                                            identity=self.ident[:, :])
                        return r
                    S.op('pe', f, reads=[('xf', half * 4 + j) for j in range(4)] + ['ident'], writes=[pk])
                    S.op('act', (lambda e, pb=pb, stg=stg, w=w: e.activation(out=stg[:w, :512], in_=pb[:w, :], func=AF.Copy)),
                         reads=[pk], writes=[sk])
                    S.op('sp', (lambda e, stg=stg, w=w, half=half, r0=c0 + s0: e.dma_start(
                        out=self.o_y[r0:r0 + w, half * 512:(half + 1) * 512], in_=stg[:w, :512])),
                        reads=[sk], writes=['o_y'], dma='o_y')
        self.outkeys.append('o_y')


PARAM_SHAPES = {
    'norm1_g': [DEPTH, 1024], 'w_in': [DEPTH, 1024, PROJ],
    'ssm_lambda_re': [DEPTH, 16, 64], 'ssm_lambda_im': [DEPTH, 16, 64], 'ssm_log_dt': [DEPTH, 16],
    'ssm_b_re': [DEPTH, 16, 64, 16], 'ssm_b_im': [DEPTH, 16, 64, 16],
    'ssm_c_re': [DEPTH, 16, 16, 64], 'ssm_c_im': [DEPTH, 16, 16, 64],
    'ssm_d': [DEPTH, 256], 'ssm_glu_w': [DEPTH, 256, 256], 'ssm_glu_b': [DEPTH, 256],
    'hgrn_lb_logits': [DEPTH, 256], 'hgrn_norm_g': [DEPTH, 256],
    'rwkv_mu': [DEPTH, 1024], 'rwkv_w0': [DEPTH, 256], 'rwkv_w2': [DEPTH, 64, 256],
    'rwkv_a0': [DEPTH, 256], 'rwkv_a2': [DEPTH, 64, 256], 'rwkv_g2': [DEPTH, 128, 256],
    'rwkv_k_k': [DEPTH, 256], 'rwkv_k_a': [DEPTH, 256], 'rwkv_r_k': [DEPTH, 256],
    'rwkv_ln_g': [DEPTH, 256], 'rwkv_ln_b': [DEPTH, 256],
    'pool_w': [DEPTH, 4, 64, 64], 'pool_scale': [DEPTH, 256],
    'w_out': [DEPTH, 1024, 1024], 'norm2_g': [DEPTH, 1024],
    'mlp_up': [DEPTH, 1024, DFF], 'mlp_down': [DEPTH, DFF, 1024], 'norm_f_g': [1024],
}
PCOLS = {'norm1_g': 1024, 'norm2_g': 1024, 'ssm_d': 256, 'ssm_glu_b': 256, 'hgrn_lb_logits': 256, 'hgrn_norm_g': 256,
         'rwkv_mu': 1024, 'rwkv_w0': 256, 'rwkv_a0': 256, 'rwkv_k_k': 256, 'rwkv_k_a': 256, 'rwkv_r_k': 256,
         'rwkv_ln_g': 256, 'rwkv_ln_b': 256, 'pool_scale': 256}
PC64 = ['hgrn_lb_logits', 'hgrn_norm_g', 'rwkv_w0', 'rwkv_a0', 'rwkv_k_k', 'rwkv_k_a', 'rwkv_r_k', 'rwkv_ln_g', 'rwkv_ln_b']
MIXER_COLS = [(0, 256), (256, 1024), (1280, 1024), (2304, 256)]

_NC_CACHE = {}
_RUN_KW = {}


def _build():
    if 'nc' not in _NC_CACHE:
        b = Builder()
        b.wcount = 0
        b.wissued = {}
        _orig_alloc = b.alloc

        def alloc2():
            _orig_alloc()
            b.S.op('pool', lambda e: e.memset(b.eps_col[:], NORM_EPS), writes=['eps'])
        b.alloc = alloc2
        _NC_CACHE['nc'] = b.build()
        _NC_CACHE['nops'] = b.S.nops
    return _NC_CACHE['nc']


def kernel(**inputs):
    inp = {k: np.ascontiguousarray(np.asarray(v, dtype=np.float32)) for k, v in inputs.items()}
    nc = _build()
    in_maps = []
    xs = inp['x_sample'].reshape(128 * TS, D)
    for c in range(NCORES):
        m = {}
        m['x'] = np.concatenate([inp['x_prompt'][c], xs[c * 64:(c + 1) * 64]], axis=0)
        sl = slice(c * NS, (c + 1) * NS)
        m['state_ssm_re'] = inp['state_ssm_re'][:, sl].reshape(DEPTH, NS, 1024)
        m['state_ssm_im'] = inp['state_ssm_im'][:, sl].reshape(DEPTH, NS, 1024)
        m['state_hgrn'] = inp['state_hgrn'][:, sl]
        m['state_wkv'] = inp['state_wkv'][:, sl]
        m['state_shift'] = inp['state_shift'][:, sl].reshape(DEPTH, NS, 1024)
        m['state_pool'] = inp['state_pool'][:, sl]
        for name in PARAM_SHAPES:
            m[name] = inp[name]
        in_maps.append({k: np.ascontiguousarray(v) for k, v in m.items()})
    res = run_bass_kernel_spmd(nc, in_maps, core_ids=list(range(NCORES)), **_RUN_KW)
    _NC_CACHE['res'] = res
    R = res.results
    cat = lambda name, ax: np.concatenate([np.expand_dims(r[name], ax) if False else r[name] for r in R], axis=ax)
    y_prompt = np.stack([r['o_y'][:SEQ] for r in R], axis=0)
    y_sample = np.concatenate([r['o_y'][SEQ:].reshape(NS, TS, D) for r in R], axis=0)
    pst = lambda name, shp: np.stack([r[name] for r in R], axis=1).reshape(shp)
    sst = lambda name, shp: np.concatenate([r[name] for r in R], axis=1).reshape(shp)
    outs = (
        y_prompt, y_sample,
        pst('o_p_ssm_re', (DEPTH, 8, 16, 64)), pst('o_p_ssm_im', (DEPTH, 8, 16, 64)),
        pst('o_p_hgrn', (DEPTH, 8, 4, 64, 64)), pst('o_p_wkv', (DEPTH, 8, 4, 64, 64)),
        pst('o_p_shift', (DEPTH, 8, 1, 1024)), pst('o_p_pool', (DEPTH, 8, 15, 256)),
        sst('o_s_ssm_re', (DEPTH, 128, 16, 64)), sst('o_s_ssm_im', (DEPTH, 128, 16, 64)),
        sst('o_s_hgrn', (DEPTH, 128, 4, 64, 64)), sst('o_s_wkv', (DEPTH, 128, 4, 64, 64)),
        sst('o_s_shift', (DEPTH, 128, 1, 1024)), sst('o_s_pool', (DEPTH, 128, 15, 256)),
    )
    return tuple(np.ascontiguousarray(o.astype(np.float32)) for o in outs)
```

```python
import contextlib
import numpy as np
import concourse.bass as bass
import concourse.mybir as mybir
from concourse.bass_utils import run_bass_kernel_spmd

F32 = mybir.dt.float32
BF16 = mybir.dt.bfloat16
I32 = mybir.dt.int32
F32R = mybir.dt.float32r


_ARENA_R = []


def R(ap):
    assert ap.tensor.name == 'arena', ap.tensor.name
    return bass.AP(tensor=_ARENA_R[0], offset=ap.offset, ap=list(ap.ap))
AF = mybir.ActivationFunctionType
ALU = mybir.AluOpType

NCORES = 8
D = 1024
SEQ = 2048
NS = 16
TS = 4
NT = SEQ + NS * TS
DEPTH = 2
DFF = 4096
PROJ = 2560
TTS = [(0, 512), (512, 512), (1024, 512), (1536, 512), (2048, 64)]
NORM_EPS = 1e-6

ENABLE = dict(s5=True, hgrn=True, rwkv=True, pool=True)


class Sched:
    ENG = ['pe', 'act', 'dve', 'pool', 'sp']

    def __init__(self, nc, stack):
        self.nc = nc
        self.stack = stack
        self.ops = {e: [] for e in self.ENG}
        self.ecount = {e: 0 for e in self.ENG}
        self.dsem = {}
        self.last_w = {}
        self.readers = {}
        self.waited = {e: {} for e in self.ENG}
        self.nops = 0

    def op(self, eng, fn, reads=(), writes=(), dma=None):
        deps = []
        for k in reads:
            if k in self.last_w:
                deps.append(self.last_w[k])
        for k in writes:
            if k in self.last_w:
                deps.append(self.last_w[k])
            deps += self.readers.get(k, [])
        if dma is None:
            self.ecount[eng] += 1
            tok = (('e', eng), self.ecount[eng])
        else:
            d = self.dsem.setdefault(dma, [0])
            d[0] += 16
            tok = (('d', dma), d[0])
        need = {}
        for (s, v) in deps:
            if s[0] == 'd' and s != tok[0]:
                v = self.dsem[s[1]][0]
            if s == tok[0]:
                if s[0] == 'd':
                    continue
            need[s] = max(need.get(s, 0), v)
        waits = []
        for s, v in need.items():
            if self.waited[eng].get(s, 0) >= v:
                continue
            self.waited[eng][s] = v
            waits.append((s, v))
        self.ops[eng].append((fn, waits, tok))
        for k in reads:
            self.readers.setdefault(k, []).append(tok)
        for k in writes:
            self.last_w[k] = tok
            self.readers[k] = []
        self.nops += 1

    def barrier(self):
        snap_e = dict(self.ecount)
        snap_d = {k: v[0] for k, v in self.dsem.items()}
        for e in self.ENG:
            waits = []
            for f in self.ENG:
                v = snap_e[f]
                if v > 0 and self.waited[e].get(('e', f), 0) < v:
                    self.waited[e][('e', f)] = v
                    waits.append((('e', f), v))
            for k, v in snap_d.items():
                if v > 0 and self.waited[e].get(('d', k), 0) < v:
                    self.waited[e][('d', k)] = v
                    waits.append((('d', k), v))
            self.ecount[e] += 1
            self.ops[e].append((None, waits, (('e', e), self.ecount[e])))

    def emit(self):
        nc = self.nc
        sems = {}
        for e in self.ENG:
            sems[('e', e)] = self.stack.enter_context(nc.semaphore('s_' + e))
        for i, k in enumerate(self.dsem):
            sems[('d', k)] = self.stack.enter_context(nc.semaphore('d%d' % i))
        engobj = {'pe': nc.tensor, 'act': nc.scalar, 'dve': nc.vector, 'pool': nc.gpsimd, 'sp': nc.sync}

        def run(e):
            eng = engobj[e]
            for fn, waits, tok in self.ops[e]:
                for s, v in waits:
                    eng.wait_ge(sems[s], v)
                if fn is None:
                    if tok[0][0] == 'e':
                        eng.sem_inc(sems[tok[0]], 1)
                    continue
                inst = fn(eng)
                if tok[0][0] == 'e':
                    inst.then_inc(sems[tok[0]], 1)
                else:
                    inst.then_inc(sems[tok[0]], 16)

        with nc.Block() as block:
            @block.tensor
            def _(x):
                run('pe')

            @block.scalar
            def _(x):
                run('act')

            @block.vector
            def _(x):
                run('dve')

            @block.gpsimd
            def _(x):
                run('pool')

            @block.sync
            def _(x):
                run('sp')


class Builder:
    def __init__(self):
        self.nc = bass.Bass('TRN2', target_bir_lowering=False)
        self.uid = 0

    def dram_in(self, name, shape):
        return self.nc.dram_tensor(name, list(shape), F32, kind='ExternalInput').ap()

    def dram_out(self, name, shape):
        return self.nc.dram_tensor(name, list(shape), F32, kind='ExternalOutput').ap()

    def sb(self, name, shape, dt=F32):
        return self.st.enter_context(self.nc.sbuf_tensor(name, list(shape), dt))

    def newkey(self, base):
        self.uid += 1
        return (base, self.uid)

    def pn(self):
        mod = getattr(self, 'pmod', 6)
        i = self.pcur % mod
        self.pcur = (i + 1) % mod
        return self.pbanks[i], ('ps', i)

    def build(self):
        nc = self.nc
        di = self.dram_in
        self.x = di('x', [NT, D])
        self.st_ssm_re = di('state_ssm_re', [DEPTH, NS, 1024])
        self.st_ssm_im = di('state_ssm_im', [DEPTH, NS, 1024])
        self.st_hgrn = di('state_hgrn', [DEPTH, NS, 4, 64, 64])
        self.st_wkv = di('state_wkv', [DEPTH, NS, 4, 64, 64])
        self.st_shift = di('state_shift', [DEPTH, NS, 1024])
        self.st_pool = di('state_pool', [DEPTH, NS, 15, 256])
        self.P = {}
        for name, shape in PARAM_SHAPES.items():
            self.P[name] = di(name, shape)
        do = self.dram_out
        self.o_y = do('o_y', [NT, D])
        self.o_p_ssm_re = do('o_p_ssm_re', [DEPTH, 1024])
        self.o_p_ssm_im = do('o_p_ssm_im', [DEPTH, 1024])
        self.o_p_hgrn = do('o_p_hgrn', [DEPTH, 4, 64, 64])
        self.o_p_wkv = do('o_p_wkv', [DEPTH, 4, 64, 64])
        self.o_p_shift = do('o_p_shift', [DEPTH, 1024])
        self.o_p_pool = do('o_p_pool', [DEPTH, 15, 256])
        self.o_s_ssm_re = do('o_s_ssm_re', [DEPTH, NS, 1024])
        self.o_s_ssm_im = do('o_s_ssm_im', [DEPTH, NS, 1024])
        self.o_s_hgrn = do('o_s_hgrn', [DEPTH, NS, 4, 64, 64])
        self.o_s_wkv = do('o_s_wkv', [DEPTH, NS, 4, 64, 64])
        self.o_s_shift = do('o_s_shift', [DEPTH, NS, 1024])
        self.o_s_pool = do('o_s_pool', [DEPTH, NS, 15, 256])
        self.outkeys = []
        with contextlib.ExitStack() as st:
            self.st = st
            self.S = Sched(nc, st)
            self.pbanks = [st.enter_context(nc.psum_tensor('pb%d' % i, [128, 512], F32)) for i in range(8)]
            self.pcur = 0
            self.alloc()
            self.consts()
            self.consts2()
            self.load_x()
            for l in range(DEPTH):
                self.layer(l)
            self.final()
            self.S.op('sp', None, reads=list(self.outkeys))
            with nc.allow_non_contiguous_dma(reason='small strided param/state transfers'):
                self.S.emit()
        return nc

    def alloc(self):
        sb = self.sb
        self.hT = sb('hT', [128, 8, NT], F32)
        self.xn = sb('xn', [128, 8, NT], BF16)
        self.ymix = sb('ymix', [128, 2, 512], BF16)
        self.hid = sb('hid', [128, 8, 512], BF16)
        self.wslot = [sb('wslot%d' % i, [128, 10240], BF16) for i in range(2)]
        self.ident = sb('ident', [128, 128], F32)
        self.ones_bf = sb('ones_bf', [128, 128], BF16)
        self.rstd = sb('rstd', [128, 512], F32)
        self.stage = [sb('stage%d' % i, [128, 1024], F32) for i in range(2)]
        self.stage_i = 0
        self.pc = {}
        for name, n in PCOLS.items():
            self.pc[name] = sb('pc_' + name, [128, DEPTH, n // 128], F32)
        self.pc_normf = sb('pc_normf', [128, 8], F32)
        AW = 7400
        cm = self.nc.sbuf_tensor('arena', [128, AW], F32)
        self.arena = cm.__enter__()
        cm.__exit__(None, None, None)
        arena_r = sb('arena_r', [128, AW], F32R)
        del _ARENA_R[:]
        _ARENA_R.append(arena_r)
        ar = self.arena

        def av(off, shape):
            n = int(np.prod(shape[1:]))
            v = ar[0:shape[0], off:off + n]
            if len(shape) == 3:
                v = v.rearrange('p (a b) -> p a b', a=shape[1])
            elif len(shape) == 4:
                v = v.rearrange('p (a b c) -> p a b c', a=shape[1], b=shape[2])
            return v
        self.av = av
        self.pext = av(0, [128, 2, 528])
        self.ps2 = av(1056, [128, 2, 528])
        self.ps4 = av(2112, [128, 2, 528])
        self.pooled = av(3168, [128, 2, 512])
        self.pext_s = av(4192, [128, 2, 16, 20])
        self.ps2_s = av(4832, [128, 2, 16, 20])
        self.ps4_s = av(5472, [128, 2, 16, 20])
        self.u_s = av(6112, [128, 2, 64])
        self.xf = av(0, [128, 8, 512])
        self.poolw = sb('poolw', [128, DEPTH, 2, 128], F32)
        self.pool_tab = sb('pool_tab', [128, 2, 16], F32)
        self.eps_col = sb('eps_col', [128, 1], F32)
        self.shp = sb('shp', [128, 8], F32)
        self.shs = sb('shs', [128, 8, 16], F32)
        self.s5_bbT = sb('s5_bbT', [128, 2, 8, 128], BF16)
        self.s5_cx = sb('s5_cx', [128, 2, 8, 128], BF16)
        self.s5_gluw = sb('s5_gluw', [128, 2, 256], BF16)
        self.idxf = sb('idxf', [128, 129], F32)
        self.ones_f = sb('ones_f', [128, 128], F32)
        self.rmask = sb('rmask', [128, 64], F32)
        self.s5_carry = sb('s5_carry', [128, 8, 2], F32)
        self.s5_hl = sb('s5_hl', [128, 2, 8], F32)
        self.s5_hls = sb('s5_hls', [128, 2, 8, 16], F32)

    def consts(self):
        S = self.S
        ident, ones_bf = self.ident, self.ones_bf
        S.op('pool', lambda e: e.memset(ident[:], 1.0), writes=['ident'])
        S.op('pool', lambda e: e.affine_select(out=ident[:], in_=ident[:], pattern=[[-1, 128]],
                                               compare_op=ALU.is_equal, fill=0.0, base=0, channel_multiplier=1),
             reads=['ident'], writes=['ident'])
        S.op('pool', lambda e: e.memset(ones_bf[:], 1.0), writes=['ones_bf'])
        S.op('pool', lambda e: e.memset(self.ones_f[:], 1.0), writes=['ones_f'])
        S.op('pool', lambda e: e.memset(self.rmask[:], 1.0), writes=['rmask'])
        S.op('pool', lambda e: e.memset(self.rmask[:].rearrange('p (s t) -> p s t', t=4)[:, :, 0:1], 0.0), writes=['rmask'])
        S.op('pool', lambda e: e.iota(self.idxf[:], pattern=[[1, 129]], base=0, channel_multiplier=0,
                                      allow_small_or_imprecise_dtypes=True), writes=['idxf'])
        with self.nc.allow_non_contiguous_dma(reason='tiny param column loads'):
            for name, n in PCOLS.items():
                t = self.pc[name]
                for l in range(DEPTH):
                    src = self.P[name][l].rearrange('(t p) -> p t', p=128)
                    S.op('sp', (lambda e, t=t, l=l, src=src: e.dma_start(out=t[:, l, :], in_=src)),
                         writes=['pc'], dma='pc')
            src = self.P['norm_f_g'].rearrange('(t p) -> p t', p=128)
            S.op('sp', lambda e: e.dma_start(out=self.pc_normf[:], in_=src), writes=['pc'], dma='pc')
        pw = self.poolw
        S.op('pool', lambda e: e.memset(pw[:], 0.0), writes=['poolw'])
        for l in range(DEPTH):
            for g in range(4):
                t, hf = g // 2, g % 2
                S.op('sp', (lambda e, l=l, g=g, t=t, hf=hf: e.dma_start(
                    out=pw[hf * 64:(hf + 1) * 64, l, t, hf * 64:(hf + 1) * 64], in_=self.P['pool_w'][l, g])),
                    reads=[], writes=['poolw'], dma='poolw')
        tab = self.pool_tab
        wins = [2, 4, 8, 16]
        for g in range(4):
            t, hf = g // 2, g % 2
            w = wins[g]
            S.op('pool', (lambda e, t=t, hf=hf, w=w: e.memset(tab[hf * 64:(hf + 1) * 64, t, :], 1.0 / w)),
                 writes=['pool_tab'])
            for c in range(w - 1):
                S.op('pool', (lambda e, t=t, hf=hf, c=c: e.memset(tab[hf * 64:(hf + 1) * 64, t, c:c + 1], 1.0 / (c + 1))),
                     writes=['pool_tab'])

    def consts2(self):
        self.gla_consts()
        self.rwkv_consts()

    def next_stage(self):
        i = self.stage_i
        self.stage_i ^= 1
        return self.stage[i], ('stage', i)

    def load_x(self):
        S = self.S
        hT, ident = self.hT, self.ident
        for i in range(17):
            n = 128 if i < 16 else 64
            c0 = i * 128
            stg, sk = self.next_stage()
            S.op('sp', (lambda e, stg=stg, c0=c0, n=n: e.dma_start(out=stg[:n, :], in_=self.x[c0:c0 + n, :])),
                 writes=[sk], dma=sk)
            for half in range(2):
                pb, pk = self.pn()

                def f(e, stg=stg, pb=pb, n=n, half=half):
                    r = None
                    for j in range(4):
                        ft = half * 4 + j
                        r = e.transpose(out=pb[:, j * 128:j * 128 + n], in_=stg[:n, ft * 128:(ft + 1) * 128],
                                        identity=ident[:n, :n])
                    return r
                S.op('pe', f, reads=[sk, 'ident'], writes=[pk])
                eng = 'act' if half == 0 else 'dve'

                def g(e, pb=pb, n=n, half=half, c0=c0, eng=eng):
                    src = pb[:].rearrange('p (j c) -> p j c', j=4)[:, :, :n]
                    dst = hT[:, half * 4:half * 4 + 4, c0:c0 + n]
                    if eng == 'act':
                        return e.activation(out=dst, in_=src, func=AF.Copy)
                    return e.tensor_copy(out=dst, in_=src)
                S.op(eng, g, reads=[pk], writes=[('h', i // 4 if i < 16 else 4)])

    def rmsnorm(self, ti, gcol, out_fn, hkey_extra=()):
        S = self.S
        c0, n = TTS[ti]
        hT, sq, rstd = self.hT, self.hid, self.rstd
        hk = ('h', ti)
        for kt in range(8):
            S.op('act', (lambda e, kt=kt: e.activation(out=sq[:, kt, :n], in_=hT[:, kt, c0:c0 + n], func=AF.Square)),
                 reads=[hk], writes=[('hid', kt)])
        pb, pk = self.pn()

        def f(e):
            r = None
            for kt in range(8):
                r = e.matmul(pb[:, :n], lhsT=self.ones_bf[:], rhs=sq[:, kt, :n], start=(kt == 0), stop=(kt == 7))
            return r
        S.op('pe', f, reads=[('hid', kt) for kt in range(8)] + ['ones_bf'], writes=[pk])
        S.op('act', lambda e: e.activation(out=rstd[:, :n], in_=pb[:, :n], func=AF.Ln, scale=1.0 / D, bias=self.eps_col[:]),
             reads=[pk, 'eps'], writes=['rstd'])
        S.op('act', lambda e: e.activation(out=rstd[:, :n], in_=rstd[:, :n], func=AF.Exp, scale=-0.5), reads=['rstd'], writes=['rstd'])
        for kt in range(8):
            dst, wk = out_fn(kt, n)
            S.op('dve', (lambda e, kt=kt, dst=dst: e.scalar_tensor_tensor(
                out=dst, in0=hT[:, kt, c0:c0 + n], scalar=gcol(kt), in1=rstd[:, :n], op0=ALU.mult, op1=ALU.mult)),
                reads=[hk, 'rstd', 'pc'], writes=[wk])

    def load_weights(self, slot, parts):
        S = self.S
        for dst, src in parts:
            S.op('pool', (lambda e, dst=dst, src=src: e.dma_start(out=dst, in_=src)),
                 writes=[('w', slot)], dma=('w', slot))

    def issue_weights(self, k):
        if k >= DEPTH * 12 or k in self.wissued:
            return
        l, r = k // 12, k % 12
        slot = k % 2
        if r < 4:
            v = self.mixer_weights(l, r, slot)
        else:
            v = self.mlp_weights(l, r - 4, slot)
        self.wissued[k] = (slot, v)

    def get_weights(self, k):
        self.issue_weights(k)
        self.issue_weights(k + 1)
        return self.wissued[k]

    def mixer_weights(self, l, m, slot):
        c0, nc_ = MIXER_COLS[m]
        ws = self.wslot[slot]
        win = ws[:, 0:8 * nc_].rearrange('p (k c) -> p k c', k=8)
        wout = ws[:, 8192:8192 + 2048].rearrange('p (k c) -> p k c', k=2)
        parts = []
        src = self.P['w_in'][l].rearrange('(k p) c -> p k c', p=128)
        for cc in range(0, nc_, 512):
            w_ = min(512, nc_ - cc)
            parts.append((win[:, :, cc:cc + w_], src[:, :, c0 + cc:c0 + cc + w_]))
        srco = self.P['w_out'][l][m * 256:(m + 1) * 256, :].rearrange('(k p) c -> p k c', p=128)
        parts.append((wout, srco))
        self.load_weights(slot, parts)
        return win, wout

    def mlp_weights(self, l, e8, slot):
        ws = self.wslot[slot]
        wup = ws[:, 0:4096].rearrange('p (k c) -> p k c', k=8)
        wdn = ws[:, 4096:8192].rearrange('p (k c) -> p k c', k=4)
        srcu = self.P['mlp_up'][l].rearrange('(k p) c -> p k c', p=128)[:, :, e8 * 512:(e8 + 1) * 512]
        srcd = self.P['mlp_down'][l][e8 * 512:(e8 + 1) * 512, :].rearrange('(k p) c -> p k c', p=128)
        self.load_weights(slot, [(wup, srcu), (wdn, srcd)])
        return wup, wdn

    def proj(self, win, j, ti, slot):
        S = self.S
        c0, n = TTS[ti]
        pb, pk = self.pn()
        xn = self.xn

        def f(e):
            r = None
            for kt in range(8):
                r = e.matmul(pb[:, :n], lhsT=win[:, kt, j * 128:(j + 1) * 128], rhs=xn[:, kt, c0:c0 + n],
                             start=(kt == 0), stop=(kt == 7))
            return r
        S.op('pe', f, reads=[('w', slot), ('xn', ti)], writes=[pk])
        return pb, pk

    def apply_wout(self, wout, ti, slot):
        S = self.S
        c0, n = TTS[ti]
        hT, ymix = self.hT, self.ymix
        for o in range(8):
            pb, pk = self.pn()

            def f(e, pb=pb, o=o):
                r = None
                for k in range(2):
                    r = e.matmul(pb[:, :n], lhsT=wout[:, k, o * 128:(o + 1) * 128], rhs=ymix[:, k, :n],
                                 start=(k == 0), stop=(k == 1))
                return r
            S.op('pe', f, reads=[('w', slot), 'ymix'], writes=[pk])
            S.op('dve', (lambda e, pb=pb, o=o: e.tensor_tensor(out=hT[:, o, c0:c0 + n], in0=hT[:, o, c0:c0 + n],
                                                               in1=pb[:, :n], op=ALU.add)),
                 reads=[pk, ('h', ti)], writes=[('h', ti)])

    def layer(self, l):
        S = self.S
        for ti in range(5):
            c0, n = TTS[ti]
            self.rmsnorm(ti, lambda kt: self.pc['norm1_g'][:, l, kt:kt + 1],
                         lambda kt, n, c0=c0, ti=ti: (self.xn[:, kt, c0:c0 + n], ('xn', ti)))
        for m in range(4):
            S.barrier()
            slot, (win, wout) = self.get_weights(l * 12 + m)
            self._wout = wout
            self.pmod = 6 if m == 0 else 8
            for ti in range(5):
                fn = [self.mix_s5, self.mix_hgrn, self.mix_rwkv, self.mix_pool][m]
                if m in (1, 2):
                    fn(l, ti, win, slot)
                else:
                    fn(l, ti, win, slot)
                    self.apply_wout(wout, ti, slot)
        S.barrier()
        for ti in range(5):
            c0, n = TTS[ti]
            self.rmsnorm(ti, lambda kt: self.pc['norm2_g'][:, l, kt:kt + 1],
                         lambda kt, n, c0=c0, ti=ti: (self.xn[:, kt, c0:c0 + n], ('xn', ti)))
        hT, hid, xn = self.hT, self.hid, self.xn
        for e8 in range(8):
            slot, (wup, wdn) = self.get_weights(l * 12 + 4 + e8)
            for ti in range(5):
                c0, n = TTS[ti]
                for f4 in range(4):
                    pb, pk = self.pn()

                    def f(e, pb=pb, f4=f4, c0=c0, n=n, wup=wup):
                        r = None
                        for kt in range(8):
                            r = e.matmul(pb[:, :n], lhsT=wup[:, kt, f4 * 128:(f4 + 1) * 128], rhs=xn[:, kt, c0:c0 + n],
                                         start=(kt == 0), stop=(kt == 7))
                        return r
                    S.op('pe', f, reads=[('w', slot), ('xn', ti)], writes=[pk])
                    rl = self.av(512 * (f4 % 2), [128, 512])
                    rk = ('mlprl', f4 % 2)
                    S.op('act', (lambda e, pb=pb, n=n, rl=rl: e.activation(out=rl[:, :n], in_=pb[:, :n], func=AF.Relu)),
                         reads=[pk], writes=[rk])
                    S.op('dve', (lambda e, f4=f4, n=n, rl=rl: e.tensor_tensor(out=hid[:, f4, :n], in0=rl[:, :n], in1=rl[:, :n], op=ALU.mult)),
                         reads=[rk], writes=[('hid', f4)])
                for o in range(8):
                    pb, pk = self.pn()

                    def f(e, pb=pb, o=o, n=n, wdn=wdn):
                        r = None
                        for k in range(4):
                            r = e.matmul(pb[:, :n], lhsT=wdn[:, k, o * 128:(o + 1) * 128], rhs=hid[:, k, :n],
                                         start=(k == 0), stop=(k == 3))
                        return r
                    S.op('pe', f, reads=[('w', slot)] + [('hid', k) for k in range(4)], writes=[pk])
                    eng = 'dve' if o % 2 == 0 else 'dve'
                    S.op(eng, (lambda e, pb=pb, o=o, c0=c0, n=n: e.tensor_tensor(
                        out=hT[:, o, c0:c0 + n], in0=hT[:, o, c0:c0 + n], in1=pb[:, :n], op=ALU.add)),
                        reads=[pk, ('h', ti)], writes=[('h', ti)])

    def zero_ymix(self, ti):
        c0, n = TTS[ti]
        self.S.op('pool', lambda e: e.memset(self.ymix[:], 0.0), writes=['ymix'])

    def s5_setup(self, l):
        S, av = self.S, self.av
        T = 4112
        self.Er = av(0, [128, 8, 129])
        self.Ei = av(1032, [128, 8, 129])
        self.Qr = av(2064, [128, 8, 128])
        self.Qi = av(3088, [128, 8, 128])
        self.Xr = av(T, [128, 512])
        self.Xi = av(T + 512, [128, 512])
        self.T1 = av(T + 1024, [128, 512])
        self.T2 = av(T + 1536, [128, 512])
        self.ysb = av(T + 2048, [128, 2, 512])
        Er, Ei, Qr, Qi = self.Er, self.Ei, self.Qr, self.Qi
        T0 = av(T, [128, 8, 129])
        T2s = av(T + 1032, [128, 8, 129])
        small = av(7208, [128, 16, 8])
        self.s5_small = small
        lr, li, ldt, dt, a, th, cr, ci, den, zr, tA, tB = [small[:, i, :] for i in range(12)]
        Bre = av(T + 2064, [128, 8, 16])
        Bim = av(T + 2192, [128, 8, 16])
        bbr = av(T + 2320, [128, 8, 16])
        bbi = av(T + 2448, [128, 8, 16])
        tC = av(T + 2576, [128, 8, 16])
        self.h0r = av(T + 64, [128, 8, 16])
        self.h0i = av(T + 320, [128, 8, 16])
        self.injr = av(T + 576, [128, 8, 16])
        self.inji = av(T + 832, [128, 8, 16])
        self.tC = av(T + 1088, [128, 8, 16])
        BX = av(T, [128, 4, 128])
        CS = av(T + 512, [128, 2, 128])
        K = 's5sc'
        P = self.P
        S.op('sp', lambda e: e.dma_start(out=lr, in_=P['ssm_lambda_re'][l].rearrange('(t g) p -> (g p) t', g=2)), writes=[K], dma='s5ld')
        S.op('sp', lambda e: e.dma_start(out=li, in_=P['ssm_lambda_im'][l].rearrange('(t g) p -> (g p) t', g=2)), writes=[K], dma='s5ld')
        for g2 in range(2):
            S.op('sp', (lambda e, g2=g2: e.dma_start(out=ldt[g2 * 64:(g2 + 1) * 64, :],
                                                     in_=P['ssm_log_dt'][l].rearrange('(t g) -> g t', g=2)[g2].partition_broadcast(64))),
                 writes=[K], dma='s5ld')
        S.op('sp', lambda e: e.dma_start(out=Bre, in_=P['ssm_b_re'][l].rearrange('(t g) p h -> (g p) t h', g=2)), writes=[K], dma='s5ld')
        S.op('sp', lambda e: e.dma_start(out=Bim, in_=P['ssm_b_im'][l].rearrange('(t g) p h -> (g p) t h', g=2)), writes=[K], dma='s5ld')
        S.op('pool', lambda e: e.dma_start(out=self.s5_gluw[:], in_=P['ssm_glu_w'][l].rearrange('(k p) c -> p k c', p=128)),
             writes=['s5gluw'], dma='s5gluw')
        TT = ALU
        bc3 = lambda v: v.unsqueeze(2).to_broadcast([128, 8, 129])
        idx3 = self.idxf[:].unsqueeze(1).to_broadcast([128, 8, 129])
        S.op('act', lambda e: e.activation(out=dt, in_=ldt, func=AF.Exp), reads=[K], writes=[K])
        S.op('dve', lambda e: e.tensor_tensor(out=a, in0=lr, in1=dt, op=TT.mult), reads=[K], writes=[K])
        S.op('dve', lambda e: e.tensor_tensor(out=th, in0=li, in1=dt, op=TT.mult), reads=[K], writes=[K])
        S.op('dve', lambda e: e.tensor_scalar(out=tB, in0=th, scalar1=1.0 / (2 * np.pi), scalar2=None, op0=TT.mult), reads=[K], writes=[K])
        for t8 in range(8):
            S.op('dve', (lambda e, t8=t8: e.tensor_scalar(out=T0[:, t8, :], in0=self.idxf[:, :], scalar1=tB[:, t8:t8 + 1], scalar2=None, op0=TT.mult)),
                 reads=[K, 'idxf'], writes=[K])
        TWO_PI = 6.28318
        def sincos(dst):
            S.op('dve', lambda e: e.tensor_scalar(out=T2s, in0=T0, scalar1=12582912.0, scalar2=None, op0=TT.add), reads=[K], writes=[K])
            S.op('dve', lambda e: e.tensor_scalar(out=T2s, in0=T2s, scalar1=-12582912.0, scalar2=None, op0=TT.add), reads=[K], writes=[K])
            S.op('dve', lambda e: e.tensor_tensor(out=T2s, in0=T0, in1=T2s, op=TT.subtract), reads=[K], writes=[K])
            S.op('act', lambda e: e.activation(out=dst, in_=T2s, func=AF.Sin, scale=TWO_PI), reads=[K], writes=[K])
        sincos(Ei)
        S.op('dve', lambda e: e.tensor_scalar(out=T0, in0=T0, scalar1=0.25, scalar2=None, op0=TT.add), reads=[K], writes=[K])
        sincos(Er)
        for t8 in range(8):
            S.op('dve', (lambda e, t8=t8: e.tensor_scalar(out=T0[:, t8, :], in0=self.idxf[:, :], scalar1=a[:, t8:t8 + 1], scalar2=None, op0=TT.mult)),
                 reads=[K, 'idxf'], writes=[K])
        S.op('act', lambda e: e.activation(out=T2s, in_=T0, func=AF.Exp, scale=-1.0), reads=[K], writes=[K])
        S.op('act', lambda e: e.activation(out=T0, in_=T0, func=AF.Exp), reads=[K], writes=[K])
        S.op('dve', lambda e: e.tensor_tensor(out=Qr, in0=T2s[:, :, 0:128], in1=Er[:, :, 0:128], op=TT.mult), reads=[K], writes=[K])
        S.op('dve', lambda e: e.scalar_tensor_tensor(out=Qi, in0=T2s[:, :, 0:128], scalar=-1.0, in1=Ei[:, :, 0:128], op0=TT.mult, op1=TT.mult),
             reads=[K], writes=[K])
        S.op('dve', lambda e: e.tensor_tensor(out=Er, in0=T0, in1=Er, op=TT.mult), reads=[K], writes=[K])
        S.op('dve', lambda e: e.tensor_tensor(out=Ei, in0=T0, in1=Ei, op=TT.mult), reads=[K], writes=[K])
        E1r, E1i = Er[:, :, 1], Ei[:, :, 1]
        S.op('dve', lambda e: e.tensor_tensor(out=tA, in0=lr, in1=lr, op=TT.mult), reads=[K], writes=[K])
        S.op('dve', lambda e: e.tensor_tensor(out=den, in0=li, in1=li, op=TT.mult), reads=[K], writes=[K])
        S.op('dve', lambda e: e.tensor_tensor(out=den, in0=den, in1=tA, op=TT.add), reads=[K], writes=[K])
        S.op('dve', lambda e: e.reciprocal(out=den, in_=den), reads=[K], writes=[K])
        S.op('dve', lambda e: e.tensor_scalar(out=zr, in0=E1r, scalar1=-1.0, scalar2=None, op0=TT.add), reads=[K], writes=[K])
        S.op('dve', lambda e: e.tensor_tensor(out=tA, in0=zr, in1=lr, op=TT.mult), reads=[K], writes=[K])
        S.op('dve', lambda e: e.tensor_tensor(out=cr, in0=E1i, in1=li, op=TT.mult), reads=[K], writes=[K])
        S.op('dve', lambda e: e.tensor_tensor(out=cr, in0=cr, in1=tA, op=TT.add), reads=[K], writes=[K])
        S.op('dve', lambda e: e.tensor_tensor(out=cr, in0=cr, in1=den, op=TT.mult), reads=[K], writes=[K])
        S.op('dve', lambda e: e.tensor_tensor(out=tA, in0=zr, in1=li, op=TT.mult), reads=[K], writes=[K])
        S.op('dve', lambda e: e.tensor_tensor(out=ci, in0=E1i, in1=lr, op=TT.mult), reads=[K], writes=[K])
        S.op('dve', lambda e: e.tensor_tensor(out=ci, in0=ci, in1=tA, op=TT.subtract), reads=[K], writes=[K])
        S.op('dve', lambda e: e.tensor_tensor(out=ci, in0=ci, in1=den, op=TT.mult), reads=[K], writes=[K])
        b16 = lambda v: v.unsqueeze(2).to_broadcast([128, 8, 16])
        S.op('dve', lambda e: e.tensor_tensor(out=bbr, in0=Bre, in1=b16(cr), op=TT.mult), reads=[K], writes=[K])
        S.op('dve', lambda e: e.tensor_tensor(out=tC, in0=Bim, in1=b16(ci), op=TT.mult), reads=[K], writes=[K])
        S.op('dve', lambda e: e.tensor_tensor(out=bbr, in0=bbr, in1=tC, op=TT.subtract), reads=[K], writes=[K])
        S.op('dve', lambda e: e.tensor_tensor(out=bbi, in0=Bim, in1=b16(cr), op=TT.mult), reads=[K], writes=[K])
        S.op('dve', lambda e: e.tensor_tensor(out=tC, in0=Bre, in1=b16(ci), op=TT.mult), reads=[K], writes=[K])
        S.op('dve', lambda e: e.tensor_tensor(out=bbi, in0=bbi, in1=tC, op=TT.add), reads=[K], writes=[K])
        for reim, bb in enumerate([bbr, bbi]):
            for tg in range(2):
                S.op('pool', lambda e: e.memset(BX, 0.0), reads=[K], writes=[K])
                for t4 in range(4):
                    for g2 in range(2):
                        S.op('pool', (lambda e, t4=t4, g2=g2, bb=bb, tg=tg: e.tensor_copy(
                            out=BX[g2 * 64:(g2 + 1) * 64, t4, 32 * t4 + 16 * g2:32 * t4 + 16 * g2 + 16],
                            in_=bb[g2 * 64:(g2 + 1) * 64, tg * 4 + t4, :])), reads=[K], writes=[K])
                pb, pk = self.pn()

                def f(e, pb=pb):
                    r = None
                    for t4 in range(4):
                        r = e.transpose(out=pb[:, t4 * 128:(t4 + 1) * 128], in_=BX[:, t4, :], identity=self.ident[:, :])
                    return r
                S.op('pe', f, reads=[K, 'ident'], writes=[pk])
                S.op('act', (lambda e, pb=pb, reim=reim, tg=tg: e.activation(
                    out=self.s5_bbT[:, reim, tg * 4:(tg + 1) * 4, :], in_=pb[:].rearrange('p (a b) -> p a b', a=4), func=AF.Copy)),
                    reads=[pk], writes=['s5bbT'])
        S.op('pool', lambda e: e.memset(self.s5_cx[:], 0.0), writes=['s5cx'])
        for reim, nm in enumerate(['ssm_c_re', 'ssm_c_im']):
            S.op('pool', lambda e: e.memset(CS, 0.0), reads=[K], writes=[K])
            for t in range(8):
                for g2 in range(2):
                    r0 = (t % 4) * 32 + g2 * 16
                    S.op('sp', (lambda e, t=t, g2=g2, r0=r0, nm=nm: e.dma_start(
                        out=CS[r0:r0 + 16, t // 4, g2 * 64:(g2 + 1) * 64], in_=P[nm][l, 2 * t + g2])),
                        reads=[], writes=[K], dma='s5ld')
            pb, pk = self.pn()

            def f(e, pb=pb):
                r = None
                for tg in range(2):
                    r = e.transpose(out=pb[:, tg * 128:(tg + 1) * 128], in_=CS[:, tg, :], identity=self.ident[:, :])
                return r
            S.op('pe', f, reads=[K, 'ident'], writes=[pk])
            for tg in range(2):
                for t4 in range(4):
                    S.op('act', (lambda e, pb=pb, tg=tg, t4=t4, reim=reim: e.activation(
                        out=self.s5_cx[:, reim, tg * 4 + t4, 32 * t4:32 * t4 + 32],
                        in_=pb[:, tg * 128 + 32 * t4:tg * 128 + 32 * t4 + 32], func=AF.Copy, scale=(1.0 if reim == 0 else -1.0))),
                        reads=[pk], writes=['s5cx'])
        S.barrier()

    def mix_s5(self, l, ti, win, slot):
        if not ENABLE['s5']:
            return self.zero_ymix(ti)
        S = self.S
        TT = ALU
        if ti == 0:
            self.s5_setup(l)
            S.op('pool', lambda e: e.memset(self.s5_carry[:], 0.0), writes=['carry'])
        c0, n = TTS[ti]
        CW = 128 if ti < 4 else 4
        NCH = n // CW
        hid = self.hid
        Er, Ei, Qr, Qi = self.Er, self.Ei, self.Qr, self.Qi
        Xr, Xi, T1, T2, ysb = self.Xr, self.Xi, self.T1, self.T2, self.ysb
        carry = self.s5_carry
        yacc = [(self.pbanks[6], ('ps', 6)), (self.pbanks[7], ('ps', 7))]
        v3 = lambda ap: ap[:, :n].rearrange('p (c j) -> p c j', j=CW)
        for j in range(2):
            pb, pk = self.proj(win, j, ti, slot)
            S.op('act', (lambda e, pb=pb, j=j: e.activation(out=hid[:, j, :n], in_=pb[:, :n], func=AF.Copy)),
                 reads=[pk], writes=[('hid', j)])
        if ti == 4:
            S.barrier()
            for reim, (src, dst) in enumerate([(self.st_ssm_re, self.h0r), (self.st_ssm_im, self.h0i)]):
                stg, sk = self.next_stage()
                S.op('sp', (lambda e, stg=stg, src=src: e.dma_start(out=stg[:16, :], in_=src[l])), writes=[sk], dma=sk)
                pb, pk = self.pn()

                def f(e, pb=pb, stg=stg):
                    r = None
                    for t in range(8):
                        r = e.transpose(out=pb[:, t * 16:(t + 1) * 16], in_=stg[:16, t * 128:(t + 1) * 128],
                                        identity=self.ident[:16, :16])
                    return r
                S.op('pe', f, reads=[sk, 'ident'], writes=[pk])
                S.op('act', (lambda e, pb=pb, dst=dst: e.activation(out=dst, in_=pb[:, 0:128].rearrange('p (a b) -> p a b', a=8),
                                                                    func=AF.Copy)), reads=[pk], writes=['s5h0'])
            e1r = Er[:, :, 1:2].to_broadcast([128, 8, 16])
            e1i = Ei[:, :, 1:2].to_broadcast([128, 8, 16])
            h0r, h0i, injr, inji, tC = self.h0r, self.h0i, self.injr, self.inji, self.tC
            S.op('pool', lambda e: e.tensor_tensor(out=injr, in0=h0r, in1=e1r, op=TT.mult), reads=['s5h0'], writes=['s5inj'])
            S.op('pool', lambda e: e.tensor_tensor(out=tC, in0=h0i, in1=e1i, op=TT.mult), reads=['s5h0'], writes=['s5tc'])
            S.op('pool', lambda e: e.tensor_tensor(out=injr, in0=injr, in1=tC, op=TT.subtract), reads=['s5tc', 's5inj'], writes=['s5inj'])
            S.op('pool', lambda e: e.tensor_tensor(out=inji, in0=h0i, in1=e1r, op=TT.mult), reads=['s5h0'], writes=['s5inj2'])
            S.op('pool', lambda e: e.tensor_tensor(out=tC, in0=h0r, in1=e1i, op=TT.mult), reads=['s5h0', 's5inj'], writes=['s5tc'])
            S.op('pool', lambda e: e.tensor_tensor(out=inji, in0=inji, in1=tC, op=TT.add), reads=['s5tc', 's5inj2'], writes=['s5inj2'])
        T = 4112
        av = self.av
        sets = []
        for sid in range(2):
            b0 = T + sid * 1024
            sets.append((av(b0, [128, 256]), av(b0 + 256, [128, 256]), av(b0 + 512, [128, 256]), av(b0 + 768, [128, 256])))
        halves = [(0, 256), (256, 256)] if ti < 4 else [(0, 64)]

        def unit(t, col0, nn, sid, lasthalf):
            c = t // 4
            uXr, uXi, uT1, uT2 = sets[sid]
            kXr, kXi, kT1, kT2 = 'Xr%d' % sid, 'Xi%d' % sid, 'T1%d' % sid, 'T2%d' % sid
            hbr, hbi = hid[:, 2 + 4 * sid, :nn], hid[:, 3 + 4 * sid, :nn]
            khr, khi = ('hid', 2 + 4 * sid), ('hid', 3 + 4 * sid)
            nchk = nn // CW
            w3 = lambda ap: ap[:, :nn].rearrange('p (c j) -> p c j', j=CW)
            ub = hid[:, c, col0:col0 + nn]
            pbr, pkr = self.pn()
            pbi, pki = self.pn()
            S.op('pe', (lambda e: e.matmul(pbr[:, :nn], lhsT=self.s5_bbT[:, 0, t, :], rhs=ub, start=True, stop=True)),
                 reads=['s5bbT', ('hid', c)], writes=[pkr])
            S.op('pe', (lambda e: e.matmul(pbi[:, :nn], lhsT=self.s5_bbT[:, 1, t, :], rhs=ub, start=True, stop=True)),
                 reads=['s5bbT', ('hid', c)], writes=[pki])
            yield
            Qrb = Qr[:, t, 0:CW].unsqueeze(1).to_broadcast([128, nchk, CW])
            Qib = Qi[:, t, 0:CW].unsqueeze(1).to_broadcast([128, nchk, CW])
            Erb = Er[:, t, 0:CW].unsqueeze(1).to_broadcast([128, nchk, CW])
            Eib = Ei[:, t, 0:CW].unsqueeze(1).to_broadcast([128, nchk, CW])
            S.op('dve', (lambda e: e.tensor_tensor(out=w3(uT1), in0=w3(pbi), in1=Qib, op=TT.mult)), reads=[pki], writes=[kT1])
            S.op('dve', (lambda e: e.tensor_tensor(out=w3(uXr), in0=w3(pbr), in1=Qrb, op=TT.mult)), reads=[pkr], writes=[kXr])
            S.op('dve', (lambda e: e.tensor_tensor(out=w3(uT2), in0=w3(pbr), in1=Qib, op=TT.mult)), reads=[pkr], writes=[kT2])
            S.op('dve', (lambda e: e.tensor_tensor(out=w3(uXi), in0=w3(pbi), in1=Qrb, op=TT.mult)), reads=[pki], writes=[kXi])
            yield
            S.op('pool', lambda e: e.tensor_tensor(out=uXr[:, :nn], in0=uXr[:, :nn], in1=uT1[:, :nn], op=TT.subtract), reads=[kXr, kT1], writes=[kXr])
            S.op('pool', lambda e: e.tensor_tensor(out=uXi[:, :nn], in0=uXi[:, :nn], in1=uT2[:, :nn], op=TT.add), reads=[kXi, kT2], writes=[kXi])
            if ti == 4:
                S.op('pool', (lambda e: e.tensor_tensor(out=w3(uXr)[:, :, 0], in0=w3(uXr)[:, :, 0], in1=self.injr[:, t, :], op=TT.add)),
                     reads=[kXr, 's5inj'], writes=[kXr])
                S.op('pool', (lambda e: e.tensor_tensor(out=w3(uXi)[:, :, 0], in0=w3(uXi)[:, :, 0], in1=self.inji[:, t, :], op=TT.add)),
                     reads=[kXi, 's5inj2'], writes=[kXi])
                yield
                S.op('dve', lambda e: e.tensor_tensor_scan(out=uT1[:, :nn], data0=self.rmask[:, :nn], data1=uXr[:, :nn], initial=0.0,
                                                           op0=TT.mult, op1=TT.add), reads=[kXr, 'rmask'], writes=[kT1])
                S.op('dve', lambda e: e.tensor_tensor_scan(out=uT2[:, :nn], data0=self.rmask[:, :nn], data1=uXi[:, :nn], initial=0.0,
                                                           op0=TT.mult, op1=TT.add), reads=[kXi, 'rmask'], writes=[kT2])
                yield
            else:
                yield
                ck = ('carry', t)
                tA = self.s5_small[:, 12 + 2 * sid, 0:1]
                tB = self.s5_small[:, 13 + 2 * sid, 0:1]
                kta, ktb = 's5ta%d' % sid, 's5tb%d' % sid
                e128r = Er[:, t, 128:129]
                e128i = Ei[:, t, 128:129]
                for ch in range(nchk):
                    sl = slice(ch * 128, (ch + 1) * 128)
                    S.op('dve', (lambda e, sl=sl: e.tensor_tensor_scan(out=uT1[:, sl], data0=self.ones_f[:, :], data1=uXr[:, sl],
                                                                        initial=carry[:, t, 0:1], op0=TT.mult, op1=TT.add)),
                         reads=[kXr, 'ones_f', ck, 'carry'], writes=[kT1])
                    S.op('dve', (lambda e, sl=sl: e.tensor_tensor_scan(out=uT2[:, sl], data0=self.ones_f[:, :], data1=uXi[:, sl],
                                                                        initial=carry[:, t, 1:2], op0=TT.mult, op1=TT.add)),
                         reads=[kXi, 'ones_f', ck, 'carry'], writes=[kT2])
                    yield
                    last = ch * 128 + 127
                    glr = uT1[:, last:last + 1]
                    gli = uT2[:, last:last + 1]
                    S.op('dve', (lambda e, gli=gli: e.tensor_tensor(out=tA, in0=gli, in1=e128i, op=TT.mult)), reads=[kT2], writes=[kta])
                    S.op('dve', (lambda e, glr=glr: e.tensor_tensor(out=tB, in0=glr, in1=e128i, op=TT.mult)), reads=[kT1], writes=[ktb])
                    yield
                    S.op('dve', (lambda e, glr=glr: e.scalar_tensor_tensor(out=carry[:, t, 0:1], in0=glr, scalar=e128r, in1=tA,
                                                                            op0=TT.mult, op1=TT.subtract)), reads=[kT1, kta], writes=[ck])
                    S.op('dve', (lambda e, gli=gli: e.scalar_tensor_tensor(out=carry[:, t, 1:2], in0=gli, scalar=e128r, in1=tB,
                                                                            op0=TT.mult, op1=TT.add)), reads=[kT2, ktb], writes=[ck])
                    yield
            S.op('pool', (lambda e: e.tensor_tensor(out=w3(uXr), in0=w3(uT1), in1=Erb, op=TT.mult)), reads=[kT1], writes=[kXr])
            S.op('pool', (lambda e: e.tensor_tensor(out=w3(uXi), in0=w3(uT2), in1=Erb, op=TT.mult)), reads=[kT2], writes=[kXi])
            S.op('dve', (lambda e: e.tensor_tensor(out=w3(uT1), in0=w3(uT1), in1=Eib, op=TT.mult)), reads=[kT1, kXr], writes=[kT1])
            S.op('dve', (lambda e: e.tensor_tensor(out=w3(uT2), in0=w3(uT2), in1=Eib, op=TT.mult)), reads=[kT2, kXi], writes=[kT2])
            yield
            S.op('pool', lambda e: e.tensor_tensor(out=uXr[:, :nn], in0=uXr[:, :nn], in1=uT2[:, :nn], op=TT.subtract), reads=[kXr, kT2], writes=[kXr])
            S.op('pool', lambda e: e.tensor_tensor(out=uXi[:, :nn], in0=uXi[:, :nn], in1=uT1[:, :nn], op=TT.add), reads=[kXi, kT1], writes=[kXi])
            yield
            S.op('act', lambda e: e.activation(out=hbr, in_=uXr[:, :nn], func=AF.Copy), reads=[kXr], writes=[khr])
            S.op('act', lambda e: e.activation(out=hbi, in_=uXi[:, :nn], func=AF.Copy), reads=[kXi], writes=[khi])
            if ti == 3 and lasthalf:
                S.op('act', (lambda e: e.activation(out=self.s5_hl[:, 0, t:t + 1], in_=uXr[:, nn - 1:nn], func=AF.Copy)), reads=[kXr], writes=['s5hl'])
                S.op('act', (lambda e: e.activation(out=self.s5_hl[:, 1, t:t + 1], in_=uXi[:, nn - 1:nn], func=AF.Copy)), reads=[kXi], writes=['s5hl'])
            if ti == 4:
                S.op('act', (lambda e: e.activation(out=self.s5_hls[:, 0, t, :], in_=w3(uXr)[:, :, 3], func=AF.Copy)), reads=[kXr], writes=['s5hls'])
                S.op('act', (lambda e: e.activation(out=self.s5_hls[:, 1, t, :], in_=w3(uXi)[:, :, 3], func=AF.Copy)), reads=[kXi], writes=['s5hls'])
            yield
            ya, yk = yacc[c]

            def fy(e):
                e.matmul(ya[:, col0:col0 + nn], lhsT=self.s5_cx[:, 0, t, :], rhs=hbr, start=(t % 4 == 0), stop=False)
                return e.matmul(ya[:, col0:col0 + nn], lhsT=self.s5_cx[:, 1, t, :], rhs=hbi, start=False, stop=(t % 4 == 3))
            S.op('pe', fy, reads=['s5cx', khr, khi], writes=[yk])
            yield

        for hi_, (col0, nn) in enumerate(halves):
            lasthalf = (hi_ == len(halves) - 1)

            def chain(sid, col0=col0, nn=nn, lasthalf=lasthalf):
                for t in range(sid, 8, 2):
                    yield from unit(t, col0, nn, sid, lasthalf)
            alive = [chain(0), chain(1)]
            while alive:
                for g in list(alive):
                    try:
                        next(g)
                    except StopIteration:
                        alive.remove(g)
        for c in range(2):
            ya, yk = yacc[c]
            S.op('dve', (lambda e, ya=ya, c=c: e.scalar_tensor_tensor(out=ysb[:, c, :n], in0=hid[:, c, :n], scalar=self.pc['ssm_d'][:, l, c:c + 1],
                                                                       in1=ya[:, :n], op0=TT.mult, op1=TT.add)),
                 reads=[yk, ('hid', c), 'pc'], writes=[('ysb', c)])
            S.op('act', (lambda e, c=c: e.activation(out=hid[:, 4 + c, :n], in_=ysb[:, c, :n], func=AF.Gelu)),
                 reads=[('ysb', c)], writes=[('hid', 4 + c)])
        S.barrier()
        for c in range(2):
            pb, pk = self.pn()

            def f(e, pb=pb, c=c):
                r = None
                for k in range(2):
                    r = e.matmul(pb[:, :n], lhsT=self.s5_gluw[:, k, c * 128:(c + 1) * 128], rhs=hid[:, 4 + k, :n], start=(k == 0), stop=(k == 1))
                return r
            S.op('pe', f, reads=['s5gluw', ('hid', 4), ('hid', 5)], writes=[pk])
            gate, gk = [(Xr, 'Xr'), (Xi, 'Xi')][c]
            S.op('act', (lambda e, pb=pb, c=c, gate=gate: e.activation(out=gate[:, :n], in_=pb[:, :n], func=AF.Sigmoid,
                                                                       bias=self.pc['ssm_glu_b'][:, l, c:c + 1])),
                 reads=[pk, 'pc'], writes=[gk])
            S.op('dve', (lambda e, c=c, gate=gate: e.tensor_tensor(out=self.ymix[:, c, :n], in0=hid[:, 4 + c, :n], in1=gate[:, :n], op=TT.mult)),
                 reads=[gk, ('hid', 4 + c)], writes=['ymix'])
        if ti == 3:
            for reim, dst in enumerate([self.o_p_ssm_re, self.o_p_ssm_im]):
                okey = 'o_p_ssm%d' % reim
                S.op('sp', (lambda e, reim=reim, dst=dst: e.dma_start(out=dst[l].rearrange('(t p) -> p t', p=128), in_=self.s5_hl[:, reim, :])),
                     reads=['s5hl'], writes=[okey], dma=okey)
                if okey not in self.outkeys:
                    self.outkeys.append(okey)
        if ti == 4:
            for reim, dst in enumerate([self.o_s_ssm_re, self.o_s_ssm_im]):
                for half in range(2):
                    self.store_fm([(self.s5_hls[:, reim, half * 4 + j, :], 's5hls') for j in range(4)], 16,
                                  dst[l][:, half * 512:(half + 1) * 512], 'o_s_ssm%d' % reim)

    def gla_consts(self):
        S = self.S
        sb = self.sb
        self.triU = sb('triU', [64, 64], F32)
        self.blkU = sb('blkU', [64, 64], F32)
        self.segm = sb('segm', [64, 16], F32)
        self.segT = sb('segT', [16, 64], F32)
        self.rm64 = sb('rm64', [64, 256], F32)
        self.zsh = sb('zsh', [64, 192], F32R)
        self.ones64 = sb('ones64', [64, 64], F32R)
        self.eps5 = sb('eps5', [64, 1], F32)
        self.pc64 = {}
        for name in PC64:
            self.pc64[name] = sb('p64_' + name, [64, DEPTH, 4], F32)
        self.lbv = sb('lbv', [64, DEPTH, 4], F32)
        self.omlv = sb('omlv', [64, DEPTH, 4], F32)
        triU, blkU, segm, segT, rm64, zsh = self.triU, self.blkU, self.segm, self.segT, self.rm64, self.zsh
        K = 'glac'
        S.op('pool', lambda e: e.memset(triU[:], 1.0), writes=[K])
        S.op('pool', lambda e: e.affine_select(out=triU[:], in_=triU[:], pattern=[[1, 64]], compare_op=ALU.is_ge, fill=0.0,
                                               base=0, channel_multiplier=-1), reads=[K], writes=[K])
        S.op('pool', lambda e: e.memset(segm[:], 1.0), reads=[K], writes=[K])
        S.op('pool', lambda e: e.affine_select(out=segm[:], in_=segm[:], pattern=[[-4, 16]], compare_op=ALU.is_ge, fill=0.0,
                                               base=0, channel_multiplier=1), reads=[K], writes=[K])
        S.op('pool', lambda e: e.affine_select(out=segm[:], in_=segm[:], pattern=[[4, 16]], compare_op=ALU.is_ge, fill=0.0,
                                               base=3, channel_multiplier=-1), reads=[K], writes=[K])
        S.op('pool', lambda e: e.memset(segT[:], 1.0), reads=[K], writes=[K])
        S.op('pool', lambda e: e.affine_select(out=segT[:], in_=segT[:], pattern=[[1, 64]], compare_op=ALU.is_ge, fill=0.0,
                                               base=0, channel_multiplier=-4), reads=[K], writes=[K])
        S.op('pool', lambda e: e.affine_select(out=segT[:], in_=segT[:], pattern=[[-1, 64]], compare_op=ALU.is_ge, fill=0.0,
                                               base=3, channel_multiplier=4), reads=[K], writes=[K])
        pb, pk = self.pn()
        S.op('pe', lambda e: e.matmul(pb[:64, :64], lhsT=segT[:, :], rhs=segT[:, :], start=True, stop=True), reads=[K], writes=[pk])
        S.op('dve', lambda e: e.tensor_tensor(out=blkU[:], in0=pb[:64, :64], in1=triU[:], op=ALU.mult), reads=[pk, K], writes=[K])
        S.op('pool', lambda e: e.memset(rm64[:], 1.0), reads=[K], writes=[K])
        S.op('pool', lambda e: e.memset(rm64[:].rearrange('p (c j) -> p c j', j=64)[:, :, 0:1], 0.0), reads=[K], writes=[K])
        S.op('dve', lambda e: e.tensor_scalar(out=zsh[:], in0=rm64[:, 0:192], scalar1=0.0, scalar2=None, op0=ALU.mult), reads=[K], writes=[K])
        S.op('dve', lambda e: e.tensor_copy(out=zsh[:, 64:128], in_=self.ident[0:64, 0:64]), reads=[K, 'ident'], writes=[K])
        S.op('dve', lambda e: e.tensor_scalar(out=self.ones64[:], in0=rm64[:, 0:64], scalar1=0.0, scalar2=1.0, op0=ALU.mult, op1=ALU.add), reads=[K], writes=[K])
        S.op('pool', lambda e: e.memset(self.eps5[:], 1e-5), reads=[K], writes=[K])
        for name in PC64:
            t = self.pc64[name]
            for l in range(DEPTH):
                S.op('sp', (lambda e, t=t, l=l, name=name: e.dma_start(out=t[:, l, :], in_=self.P[name][l].rearrange('(h k) -> k h', k=64))),
                     writes=['pc'], dma='pc')
        lg = self.pc64['hgrn_lb_logits']
        lbv, omlv = self.lbv, self.omlv
        S.op('pool', lambda e: e.memset(lbv[:], 0.0), reads=[K], writes=[K])
        S.op('dve', lambda e: e.tensor_tensor(out=lbv[:, 1, :], in0=lg[:, 1, :], in1=lg[:, 0, :], op=ALU.subtract), reads=['pc', K], writes=[K])
        S.op('act', lambda e: e.activation(out=lbv[:, 1, :], in_=lbv[:, 1, :], func=AF.Sigmoid), reads=[K], writes=[K])
        S.op('dve', lambda e: e.tensor_scalar(out=omlv[:], in0=lbv[:], scalar1=-1.0, scalar2=1.0, op0=ALU.mult, op1=ALU.add), reads=[K], writes=[K])

    def proj_heads(self, win, colbase, sub, slot, evac):
        S = self.S
        c0, n = sub
        xn = self.xn
        for hh in range(2):
            pb, pk = self.pn()

            def f(e, pb=pb, hh=hh):
                r = None
                for j in range(2):
                    h = hh * 2 + j
                    for kt in range(8):
                        r = e.matmul(pb[:64, j * n:(j + 1) * n], lhsT=win[:, kt, colbase + h * 64:colbase + (h + 1) * 64],
                                     rhs=xn[:, kt, c0:c0 + n], start=(kt == 0), stop=(kt == 7))
                return r
            S.op('pe', f, reads=[('w', slot), ('xn', 0), ('xn', 1), ('xn', 2), ('xn', 3), ('xn', 4)], writes=[pk])
            evac(pb, pk, hh)

    def heads_to_ymix(self, src, skey, n):
        S = self.S
        zsh = self.zsh
        for t in range(2):
            pb, pk = self.pn()

            def f(e, pb=pb, t=t):
                e.matmul(pb[:, :n], lhsT=zsh[:, 64:192], rhs=R(src[:, 2 * t, :]), start=True, stop=False)
                return e.matmul(pb[:, :n], lhsT=zsh[:, 0:128], rhs=R(src[:, 2 * t + 1, :]), start=False, stop=True)
            S.op('pe', f, reads=[skey, 'glac'], writes=[pk])
            S.op('act', (lambda e, pb=pb, t=t: e.activation(out=self.ymix[:, t, :n], in_=pb[:, :n], func=AF.Copy)),
                 reads=[pk], writes=['ymix'])

    def gla_chunks(self, l, kind, n, sample, A_r, A_k, A_v, A_o, GL, Sst, off, S0=None, A_a=None, A_b=None):
        S = self.S
        av = self.av
        L = 64
        nch = n // L
        mask = self.blkU if sample else self.triU
        M4 = av(off, [64, 4, 64])
        Vtok = av(off + 256, [64, 4, 64])
        Ktok = av(off + 512, [64, 4, 64])
        Stmp = av(off + 768, [64, 4, 64])
        K4 = lambda nm: (kind + nm)
        Sr = av(off + 1344, [64, 4, 64])
        if not sample:
            S.op('act', lambda e: e.activation(out=R(Sr), in_=Sst, func=AF.Copy), reads=[K4('S')], writes=[K4('Sr')])
        for c in range(nch):
            cs = slice(c * L, (c + 1) * L)
            pbA, pkA = self.pn()

            def fA(e, pbA=pbA, cs=cs):
                r = None
                for h in range(4):
                    r = e.matmul(pbA[:64, h * 64:(h + 1) * 64], lhsT=R(A_k[:, h, cs]), rhs=R(A_r[:, h, cs]), start=True, stop=True)
                return r
            S.op('pe', fA, reads=[K4('k'), K4('r')], writes=[pkA])
            S.op('dve', (lambda e, pbA=pbA: e.tensor_tensor(out=R(M4), in0=pbA[:64, 0:256].rearrange('p (h t) -> p h t', h=4),
                                                            in1=mask[:, :].unsqueeze(1).to_broadcast([64, 4, 64]), op=ALU.mult)),
                 reads=[pkA, 'glac'], writes=[K4('M4')])
            for src, skey, dst, dkey, eng in ((A_k, K4('k'), Ktok, K4('Ktok'), 'act'), (A_v, K4('v'), Vtok, K4('Vtok'), 'act')):
                pbT, pkT = self.pn()

                def fT(e, pbT=pbT, src=src, cs=cs):
                    r = None
                    for h in range(4):
                        r = e.transpose(out=pbT[:64, h * 64:(h + 1) * 64], in_=src[:, h, cs], identity=self.ident[0:64, 0:64])
                    return r
                S.op('pe', fT, reads=[skey, 'ident'], writes=[pkT])
                S.op(eng, (lambda e, pbT=pbT, dst=dst: e.activation(out=R(dst), in_=pbT[:64, 0:256].rearrange('p (h t) -> p h t', h=4), func=AF.Copy)),
                     reads=[pkT], writes=[dkey])
            pbY, pkY = self.pn()

            def fY(e, pbY=pbY, cs=cs, c=c):
                r = None
                for h in range(4):
                    out = pbY[:64, h * 64:(h + 1) * 64]
                    if not sample:
                        e.matmul(out, lhsT=R(Sr[:, h, :]), rhs=R(A_r[:, h, cs]), start=True, stop=False)
                        r = e.matmul(out, lhsT=R(Vtok[:, h, :]), rhs=R(M4[:, h, :]), start=False, stop=True)
                    else:
                        e.matmul(out, lhsT=R(Vtok[:, h, :]), rhs=R(M4[:, h, :]), start=True, stop=False)
                        for j in range(NS):
                            r = e.matmul(pbY[:64, h * 64 + 4 * j:h * 64 + 4 * j + 4], lhsT=S0[:, j, h, :], rhs=A_r[:, h, 4 * j:4 * j + 4],
                                         start=False, stop=(j == NS - 1))
                return r
            S.op('pe', fY, reads=[K4('S'), K4('Sr'), K4('r'), K4('Vtok'), K4('M4')], writes=[pkY])
            S.op('act', (lambda e, pbY=pbY, cs=cs: e.activation(out=A_o[:, :, cs], in_=pbY[:64, 0:256].rearrange('p (h t) -> p h t', h=4), func=AF.Copy)),
                 reads=[pkY], writes=[K4('o')])
            if not sample:
                pbS, pkS = self.pn()

                def fS(e, pbS=pbS):
                    r = None
                    for h in range(4):
                        r = e.matmul(pbS[:64, h * 64:(h + 1) * 64], lhsT=R(Ktok[:, h, :]), rhs=R(Vtok[:, h, :]), start=True, stop=True)
                    return r
                S.op('pe', fS, reads=[K4('Ktok'), K4('Vtok')], writes=[pkS])
                S.op('dve', (lambda e, pbS=pbS: e.tensor_tensor(out=Stmp, in0=pbS[:64, 0:256].rearrange('p (h v) -> p h v', h=4), in1=Sst, op=ALU.add)),
                     reads=[pkS, K4('S')], writes=[K4('Stmp')])
                S.op('dve', (lambda e, c=c: e.tensor_tensor(out=Sst, in0=Stmp, in1=GL[:, :, c:c + 1].to_broadcast([64, 4, 64]), op=ALU.mult)),
                     reads=[K4('Stmp'), K4('GL')], writes=[K4('S')])
                S.op('act', lambda e: e.activation(out=R(Sr), in_=Sst, func=AF.Copy), reads=[K4('S')], writes=[K4('Sr')])
            else:
                hidf = self.hid[:].rearrange('p a b -> p (a b)').bitcast(F32)
                Vexp = hidf[0:64, 1024:2048].rearrange('p (j v) -> p j v', j=16)
                for h in range(4):
                    S.op('dve', (lambda e, h=h: e.tensor_tensor(out=Vexp, in0=Vtok[:, h, :].unsqueeze(1).to_broadcast([64, 16, 64]),
                                                                in1=self.segm[:, :].unsqueeze(2).to_broadcast([64, 16, 64]), op=ALU.mult)),
                         reads=[K4('Vtok'), 'glac'], writes=[K4('Vexp')])
                    for jj in range(2):
                        pbS, pkS = self.pn()
                        S.op('pe', (lambda e, pbS=pbS, h=h, jj=jj: e.matmul(
                            pbS[:64, :], lhsT=Ktok[:, h, :], rhs=Vexp[:, 8 * jj:8 * jj + 8, :], start=True, stop=True)),
                            reads=[K4('Ktok'), K4('Vexp')], writes=[pkS])
                        S.op('dve', (lambda e, pbS=pbS, h=h, jj=jj: e.tensor_tensor(
                            out=S0[:, 8 * jj:8 * jj + 8, h, :], in0=pbS[:64, :].rearrange('p (j v) -> p j v', j=8),
                            in1=S0[:, 8 * jj:8 * jj + 8, h, :], op=ALU.add)), reads=[pkS, K4('S')], writes=[K4('S')])
                        S.op('dve', (lambda e, h=h, jj=jj: e.tensor_tensor(
                            out=S0[:, 8 * jj:8 * jj + 8, h, :], in0=S0[:, 8 * jj:8 * jj + 8, h, :],
                            in1=GL[:, h, 8 * jj:8 * jj + 8].unsqueeze(2).to_broadcast([64, 8, 64]), op=ALU.mult)),
                            reads=[K4('S'), K4('GL')], writes=[K4('S')])

    def mix_hgrn(self, l, ti, win, slot):
        if not ENABLE['hgrn']:
            self.zero_ymix(ti)
            return self.apply_wout(self._wout, ti, slot)
        if ti < 4:
            for half in range(2):
                self.hgrn_sub(l, ti * 2 + half, win, slot)
                self.apply_wout_sub(self._wout, (TTS[ti][0] + half * 256, 256), ti, slot)
        else:
            self.S.barrier()
            self.hgrn_sub(l, 8, win, slot)
            self.apply_wout_sub(self._wout, TTS[4], ti, slot)

    def apply_wout_sub(self, wout, sub, ti, slot):
        S = self.S
        c0, n = sub
        hT, ymix = self.hT, self.ymix
        for o in range(8):
            pb, pk = self.pn()

            def f(e, pb=pb, o=o):
                r = None
                for k in range(2):
                    r = e.matmul(pb[:, :n], lhsT=wout[:, k, o * 128:(o + 1) * 128], rhs=ymix[:, k, :n], start=(k == 0), stop=(k == 1))
                return r
            S.op('pe', f, reads=[('w', slot), 'ymix'], writes=[pk])
            S.op('dve', (lambda e, pb=pb, o=o: e.tensor_tensor(out=hT[:, o, c0:c0 + n], in0=hT[:, o, c0:c0 + n], in1=pb[:, :n], op=ALU.add)),
                 reads=[pk, ('h', ti)], writes=[('h', ti)])

    def hgrn_sub(self, l, si, win, slot):
        S = self.S
        av = self.av
        TT = ALU
        sample = (si == 8)
        c0, n = (si * 256, 256) if not sample else (2048, 64)
        W = n
        A_q = av(0, [64, 4, W])
        A_k = av(1024, [64, 4, W])
        A_b = av(2048, [64, 4, W])
        A_t = av(3072, [64, 4, W])
        A_v = av(4096, [64, 4, W])
        A_o = A_b
        hidf = self.hid[:].rearrange('p a b -> p (a b)').bitcast(F32)
        A_g = hidf[0:64, 0:4 * W].rearrange('p (h t) -> p h t', h=4)
        off = 5120
        Sst = av(off + 1024, [64, 4, 64])
        nch = 1 if sample else 4
        GL = av(off + 1280, [64, 4, 16])
        S0 = av(1280, [64, 16, 4, 64]) if sample else None
        if sample:
            A_q = av(0, [64, 4, W])
            A_k = av(256, [64, 4, W])
            A_b = av(512, [64, 4, W])
            A_t = av(768, [64, 4, W])
            A_v = av(1024, [64, 4, W])
            A_o = A_b
            off = 5376
            GL = av(off + 1280, [64, 4, 16])
        lbc = self.lbv[:, l, :].unsqueeze(2).to_broadcast([64, 4, W])
        omlc = self.omlv[:, l, :].unsqueeze(2).to_broadcast([64, 4, W])
        ngc = self.pc64['hgrn_norm_g'][:, l, :].unsqueeze(2).to_broadcast([64, 4, W])
        sub = (c0, n)
        v3 = lambda pb: pb[:64, 0:2 * n].rearrange('p (j t) -> p j t', j=2)
        if si == 0:
            S.op('pool', lambda e: e.memset(Sst, 0.0), writes=['hS'])
        if sample:
            for q in range(4):
                S.op('sp', (lambda e, q=q: e.dma_start(out=S0[:, 4 * q:4 * q + 4, :, :],
                                                       in_=self.st_hgrn[l, 4 * q:4 * q + 4].rearrange('j h k v -> k j h v'))),
                     writes=['hS'], dma='hS0')
        self.proj_heads(win, 0, sub, slot, lambda pb, pk, hh: S.op(
            'act', (lambda e: e.activation(out=A_q[:, 2 * hh:2 * hh + 2, :], in_=v3(pb), func=AF.Silu)), reads=[pk], writes=['hr']))
        self.proj_heads(win, 256, sub, slot, lambda pb, pk, hh: S.op(
            'act', (lambda e: e.activation(out=A_b[:, 2 * hh:2 * hh + 2, :], in_=v3(pb), func=AF.Sigmoid)), reads=[pk], writes=['hb', 'ho']))
        self.proj_heads(win, 512, sub, slot, lambda pb, pk, hh: S.op(
            'act', (lambda e: e.activation(out=A_v[:, 2 * hh:2 * hh + 2, :], in_=v3(pb), func=AF.Copy)), reads=[pk], writes=['hv']))
        self.proj_heads(win, 768, sub, slot, lambda pb, pk, hh: S.op(
            'act', (lambda e: e.activation(out=A_g[:, 2 * hh:2 * hh + 2, :], in_=v3(pb), func=AF.Silu)), reads=[pk], writes=['hg']))
        S.op('dve', lambda e: e.tensor_tensor(out=A_b, in0=A_b, in1=omlc, op=TT.mult), reads=['hb', 'glac'], writes=['hb'])
        S.op('dve', lambda e: e.tensor_tensor(out=A_b, in0=A_b, in1=lbc, op=TT.add), reads=['hb', 'glac'], writes=['hb'])
        S.op('dve', lambda e: e.tensor_scalar(out=A_k, in0=A_b, scalar1=-1.0, scalar2=1.0, op0=TT.mult, op1=TT.add), reads=['hb'], writes=['hk'])
        S.op('act', lambda e: e.activation(out=A_b, in_=A_b, func=AF.Ln), reads=['hb', 'hk'], writes=['hb'])
        rmk = self.rmask[0:64, 0:64] if sample else self.rm64[:, :]
        for h in range(4):
            S.op('dve', (lambda e, h=h: e.tensor_tensor_scan(out=A_t[:, h, :], data0=rmk, data1=A_b[:, h, :], initial=0.0, op0=TT.mult, op1=TT.add)),
                 reads=['hb', 'glac', 'rmask'], writes=['ht'])
        S.op('act', lambda e: e.activation(out=A_b, in_=A_t, func=AF.Exp), reads=['ht'], writes=['hb'])
        S.op('dve', lambda e: e.tensor_tensor(out=R(A_q), in0=A_q, in1=A_b, op=TT.mult), reads=['hr', 'hb'], writes=['hr'])
        if sample:
            S.op('pool', lambda e: e.tensor_copy(out=GL, in_=A_b[:, :, :].rearrange('p h (j t) -> p h j t', t=4)[:, :, :, 3]), reads=['hb'], writes=['hGL'])
        else:
            S.op('pool', lambda e: e.tensor_copy(out=GL[:, :, 0:4], in_=A_b[:, :, :].rearrange('p h (c t) -> p h c t', t=64)[:, :, :, 63]),
                 reads=['hb'], writes=['hGL'])
        S.op('act', lambda e: e.activation(out=A_t, in_=A_t, func=AF.Exp, scale=-1.0), reads=['ht'], writes=['ht'])
        S.op('dve', lambda e: e.tensor_tensor(out=R(A_k), in0=A_k, in1=A_t, op=TT.mult), reads=['hk', 'ht'], writes=['hk'])
        S.op('pool', lambda e: e.tensor_copy(out=GL[:, 0, 15:16], in_=GL[:, 0, 15:16]), reads=['hGL', 'hr', 'hb'], writes=['ho', 'hGL'])
        self.gla_chunks(l, 'h', n, sample, A_q, A_k, A_v, A_o, GL, Sst, off, S0=S0)
        S.op('dve', lambda e: e.tensor_tensor(out=R(A_t), in0=A_o, in1=A_o, op=TT.mult), reads=['ho', 'hk'], writes=['ht'])
        flat = lambda a: a[:, :, :].rearrange('p h t -> p (h t)')
        tot = 4 * n
        for s0 in range(0, tot, 512):
            w = min(512, tot - s0)
            pb, pk = self.pn()
            S.op('pe', (lambda e, pb=pb, s0=s0, w=w: e.matmul(pb[:64, :w], lhsT=self.ones64[:, :], rhs=R(flat(A_t)[:, s0:s0 + w]), start=True, stop=True)),
                 reads=['ht', 'glac'], writes=[pk])
            S.op('act', (lambda e, pb=pb, s0=s0, w=w: e.activation(out=flat(A_v)[:, s0:s0 + w], in_=pb[:64, :w], func=AF.Ln, scale=1.0 / 64,
                                                                    bias=self.eps5[:])), reads=[pk, 'glac', 'hVtok'], writes=['hv'])
        S.op('act', lambda e: e.activation(out=A_v, in_=A_v, func=AF.Exp, scale=-0.5), reads=['hv'], writes=['hv'])
        S.op('dve', lambda e: e.tensor_tensor(out=A_o, in0=A_o, in1=A_v, op=TT.mult), reads=['ho', 'hv'], writes=['ho'])
        S.op('dve', lambda e: e.tensor_tensor(out=A_o, in0=A_o, in1=ngc, op=TT.mult), reads=['ho', 'pc'], writes=['ho'])
        S.op('dve', lambda e: e.tensor_tensor(out=R(A_o), in0=A_o, in1=A_g, op=TT.mult), reads=['ho', 'hg'], writes=['ho'])
        self.heads_to_ymix(A_o, 'ho', n)
        if si == 7:
            S.op('sp', lambda e: e.dma_start(out=self.o_p_hgrn[l].rearrange('h k v -> k h v'), in_=Sst), reads=['hS'], writes=['o_p_hgrn'], dma='o_p_hgrn')
            if 'o_p_hgrn' not in self.outkeys:
                self.outkeys.append('o_p_hgrn')
        if sample:
            for q in range(4):
                S.op('sp', (lambda e, q=q: e.dma_start(out=self.o_s_hgrn[l, 4 * q:4 * q + 4].rearrange('j h k v -> k j h v'),
                                                       in_=S0[:, 4 * q:4 * q + 4, :, :])), reads=['hS'], writes=['o_s_hgrn'], dma='o_s_hgrn')
            if 'o_s_hgrn' not in self.outkeys:
                self.outkeys.append('o_s_hgrn')

    C0 = 0.6065306597126334

    def rwkv_consts(self):
        S = self.S
        sb = self.sb
        self.sU = sb('sU', [64, 64], F32)
        self.sL = sb('sL', [64, 64], F32)
        self.bsU = sb('bsU', [64, 64], F32)
        self.bsL = sb('bsL', [64, 64], F32)
        self.mu64 = sb('mu64', [64, DEPTH, 16], F32)
        self.shlast = sb('shlast', [64, 16], F32)
        self.gn_eps = sb('gn_eps', [64, 1], F32)
        K = 'glac'
        id64 = self.ident[0:64, 0:64]
        S.op('dve', lambda e: e.tensor_tensor(out=self.sU[:], in0=self.triU[:], in1=id64, op=ALU.subtract), reads=[K, 'ident'], writes=[K])
        S.op('dve', lambda e: e.tensor_tensor(out=self.bsU[:], in0=self.blkU[:], in1=id64, op=ALU.subtract), reads=[K, 'ident'], writes=[K])
        S.op('pool', lambda e: e.memset(self.sL[:], 1.0), reads=[K], writes=[K])
        S.op('pool', lambda e: e.affine_select(out=self.sL[:], in_=self.sL[:], pattern=[[-1, 64]], compare_op=ALU.is_gt, fill=0.0,
                                               base=0, channel_multiplier=1), reads=[K], writes=[K])
        pb, pk = self.pn()
        S.op('pe', lambda e: e.matmul(pb[:64, :64], lhsT=self.segT[:, :], rhs=self.segT[:, :], start=True, stop=True), reads=[K], writes=[pk])
        S.op('dve', lambda e: e.tensor_tensor(out=self.bsL[:], in0=pb[:64, :64], in1=self.sL[:], op=ALU.mult), reads=[pk, K], writes=[K])
        S.op('pool', lambda e: e.memset(self.gn_eps[:], 64e-5), reads=[K], writes=[K])
        for l in range(DEPTH):
            S.op('sp', (lambda e, l=l: e.dma_start(out=self.mu64[:, l, :], in_=self.P['rwkv_mu'][l].rearrange('(c k) -> k c', k=64))),
                 writes=['pc'], dma='pc')

    def mix_rwkv(self, l, ti, win, slot):
        if not ENABLE['rwkv']:
            S = self.S
            self.zero_ymix(ti)
            if ti >= 3:
                for j in range(8):
                    pb, pk = self.proj(win, j, ti, slot)
                    self.shift_capture(l, ti, j, pb, pk)
                self.shift_store(l, ti)
            self.apply_wout(self._wout, ti, slot)
            return
        if ti == 0:
            S = self.S
            lw = self.rstd[:].bitcast(BF16)
            self.w2b = lw[0:64, 0:256]
            self.a2b = lw[0:64, 256:512]
            self.g2b = lw[0:64, 512:1024].rearrange('p (a c) -> p a c', a=2)
            S.op('pool', lambda e: e.dma_start(out=self.w2b, in_=self.P['rwkv_w2'][l]), writes=['rlw'], dma='rlw')
            S.op('pool', lambda e: e.dma_start(out=self.a2b, in_=self.P['rwkv_a2'][l]), writes=['rlw'], dma='rlw')
            S.op('pool', lambda e: e.dma_start(out=self.g2b, in_=self.P['rwkv_g2'][l].rearrange('(a p) c -> p a c', p=64)), writes=['rlw'], dma='rlw')
        if ti < 4:
            for q in range(4):
                si = ti * 4 + q
                self.rwkv_sub(l, si, win, slot)
                self.apply_wout_sub(self._wout, (si * 128, 128), ti, slot)
        else:
            self.rwkv_sub(l, 16, win, slot)
            self.apply_wout_sub(self._wout, TTS[4], ti, slot)

    def rwkv_sub(self, l, si, win, slot):
        S = self.S
        av = self.av
        TT = ALU
        C0 = self.C0
        sample = (si == 16)
        c0, n = (si * 128, 128) if not sample else (2048, 64)
        W = 4 * n
        S.barrier()
        sl = lambda i: av(i * W, [64, 4, n])
        S0a, S1a, S2a, S3a, S4a, S5a = [sl(i) for i in range(6)]
        TB = 6 * W
        tsl = lambda i: av(TB + i * W, [64, 4, n])
        T0, T1, T2, T3, T4, T5, T6 = [tsl(i) for i in range(7)]
        hidf = self.hid[:].rearrange('p a b -> p (a b)').bitcast(F32)
        Gg = hidf[0:64, 0:W].rearrange('p (h t) -> p h t', h=4)
        BON = hidf[0:64, W:2 * W].rearrange('p (h t) -> p h t', h=4)
        LB = self.hid[0:64, 7, 0:W].rearrange('p (h t) -> p h t', h=4) if not sample else self.hid[0:64, 2, 0:W].rearrange('p (h t) -> p h t', h=4)
        Sst = av(6912, [64, 4, 64])
        GL = av(7168, [64, 4, 16])
        sub = (c0, n)
        P64 = self.pc64
        bc = lambda name: P64[name][:, l, :].unsqueeze(2).to_broadcast([64, 4, n])
        if si == 0:
            S.op('pool', lambda e: e.memset(Sst, 0.0), writes=['rS'])
            S.op('pool', lambda e: e.memset(self.shlast[:], 0.0), writes=['shlast'])
        if not sample:
            pass
        if not sample:
            PF = av(TB, [64, 16, n + 1])
            S.op('pool', lambda e: e.tensor_copy(out=PF[:, :, 0], in_=self.shlast[:, :]), reads=['shlast'], writes=['PF'])
        else:
            PF = av(TB, [64, 16, 16, 5])
            stg, sk = self.next_stage()
            S.op('sp', lambda e: e.dma_start(out=stg[:16, :], in_=self.st_shift[l]), writes=[sk], dma=sk)
            pbx, pkx = self.pn()

            def fx(e):
                r = None
                for c in range(16):
                    r = e.transpose(out=pbx[:64, c * 16:(c + 1) * 16], in_=stg[:16, c * 64:(c + 1) * 64], identity=self.ident[:16, :16])
                return r
            S.op('pe', fx, reads=[sk, 'ident'], writes=[pkx])
            S.op('act', lambda e: e.activation(out=PF[:, :, :, 0], in_=pbx[:64, 0:256].rearrange('p (c j) -> p c j', c=16), func=AF.Copy),
                 reads=[pkx], writes=['PF'])
        xn = self.xn
        for g4 in range(4):
            pb, pk = self.pn()

            def f(e, pb=pb, g4=g4):
                r = None
                for j in range(4):
                    c = g4 * 4 + j
                    for kt in range(8):
                        r = e.matmul(pb[:64, j * n:(j + 1) * n], lhsT=win[:, kt, c * 64:(c + 1) * 64], rhs=xn[:, kt, c0:c0 + n],
                                     start=(kt == 0), stop=(kt == 7))
                return r
            S.op('pe', f, reads=[('w', slot), ('xn', 0), ('xn', 1), ('xn', 2), ('xn', 3), ('xn', 4)], writes=[pk])
            if not sample:
                S.op('act', (lambda e, pb=pb, g4=g4: e.activation(out=PF[:, g4 * 4:g4 * 4 + 4, 1:n + 1], in_=pb[:64, 0:4 * n].rearrange('p (c t) -> p c t', c=4),
                                                                  func=AF.Copy)), reads=[pk], writes=['PF'])
            else:
                for j in range(4):
                    S.op('act', (lambda e, pb=pb, g4=g4, j=j: e.activation(out=PF[:, g4 * 4 + j, :, 1:5],
                                                                           in_=pb[:64, j * n:(j + 1) * n].rearrange('p (s t) -> p s t', t=4), func=AF.Copy)),
                         reads=[pk], writes=['PF'])
        if not sample:
            S.op('pool', lambda e: e.tensor_copy(out=self.shlast[:, :], in_=PF[:, :, n]), reads=['PF'], writes=['shlast'])
            if si == 15:
                S.op('sp', lambda e: e.dma_start(out=self.o_p_shift[l].rearrange('(c k) -> k c', k=64), in_=self.shlast[:, :]),
                     reads=['shlast'], writes=['o_p_shift'], dma='o_p_shift')
                if 'o_p_shift' not in self.outkeys:
                    self.outkeys.append('o_p_shift')
        else:
            shs = av(TB + 3 * W + 1536, [64, 16, 16])
            S.op('pool', lambda e: e.tensor_copy(out=shs, in_=PF[:, :, :, 4]), reads=['PF'], writes=['shs2'])
            for half in range(2):
                pbs, pks = self.pn()

                def fs(e, pbs=pbs, half=half):
                    r = None
                    for c in range(8):
                        r = e.transpose(out=pbs[:16, c * 64:(c + 1) * 64], in_=shs[:, half * 8 + c, :], identity=self.ident[0:64, 0:64])
                    return r
                S.op('pe', fs, reads=['shs2', 'ident'], writes=[pks])
                stg, sk = self.next_stage()
                S.op('act', (lambda e, pbs=pbs, stg=stg: e.activation(out=stg[:16, 0:512], in_=pbs[:16, 0:512], func=AF.Copy)), reads=[pks], writes=[sk])
                S.op('sp', (lambda e, stg=stg, half=half: e.dma_start(out=self.o_s_shift[l][:, half * 512:(half + 1) * 512], in_=stg[:16, 0:512])),
                     reads=[sk], writes=['o_s_shift'], dma='o_s_shift')
            if 'o_s_shift' not in self.outkeys:
                self.outkeys.append('o_s_shift')
        XS = av(0, [64, 16, n]) if not sample else av(0, [64, 16, 16, 4])
        if not sample:
            prev, cur = PF[:, :, 0:n], PF[:, :, 1:n + 1]
            mub = self.mu64[:, l, :].unsqueeze(2).to_broadcast([64, 16, n])
        else:
            prev, cur = PF[:, :, :, 0:4], PF[:, :, :, 1:5]
            mub = self.mu64[:, l, :].unsqueeze(2).unsqueeze(3).to_broadcast([64, 16, 16, 4])
        S.op('dve', lambda e: e.tensor_tensor(out=XS, in0=prev, in1=cur, op=TT.subtract), reads=['PF'], writes=['XS'])
        S.op('dve', lambda e: e.tensor_tensor(out=XS, in0=XS, in1=mub, op=TT.mult), reads=['XS', 'pc'], writes=['XS'])
        S.op('dve', lambda e: e.tensor_tensor(out=XS, in0=XS, in1=cur, op=TT.add), reads=['XS', 'PF'], writes=['XS'])
        S.barrier()
        S.op('act', lambda e: e.activation(out=LB[:, 0, :], in_=S3a[:, 0, :], func=AF.Tanh), writes=['LB'])
        S.op('act', lambda e: e.activation(out=LB[:, 1, :], in_=S3a[:, 1, :], func=AF.Copy), writes=['LB'])
        S.op('act', lambda e: e.activation(out=LB[:, 2:4, :], in_=S3a[:, 2:4, :], func=AF.Sigmoid), writes=['LB'])
        pbw, pkw = self.pn()
        pba, pka = self.pn()
        pbg, pkg = self.pn()

        def flw(e):
            r = None
            for h in range(4):
                hs = slice(h * 64, (h + 1) * 64)
                r = e.matmul(pbw[:64, h * n:(h + 1) * n], lhsT=self.w2b[:, hs], rhs=LB[:, 0, :], start=True, stop=True)
            return r

        def fla(e):
            r = None
            for h in range(4):
                hs = slice(h * 64, (h + 1) * 64)
                r = e.matmul(pba[:64, h * n:(h + 1) * n], lhsT=self.a2b[:, hs], rhs=LB[:, 1, :], start=True, stop=True)
            return r

        def flg(e):
            r = None
            for h in range(4):
                hs = slice(h * 64, (h + 1) * 64)
                e.matmul(pbg[:64, h * n:(h + 1) * n], lhsT=self.g2b[:, 0, hs], rhs=LB[:, 2, :], start=True, stop=False)
                r = e.matmul(pbg[:64, h * n:(h + 1) * n], lhsT=self.g2b[:, 1, hs], rhs=LB[:, 3, :], start=False, stop=True)
            return r
        S.op('pe', flw, reads=['LB', 'rlw'], writes=[pkw])
        S.op('pe', fla, reads=['LB', 'rlw'], writes=[pka])
        S.op('pe', flg, reads=['LB', 'rlw'], writes=[pkg])
        for h in range(4):
            S.op('act', (lambda e, h=h: e.activation(out=T0[:, h, :], in_=pbw[:64, h * n:(h + 1) * n], func=AF.Sigmoid,
                                                     bias=P64['rwkv_w0'][:, l, h:h + 1])), reads=[pkw, 'pc'], writes=['T0'])
            S.op('act', (lambda e, h=h: e.activation(out=T1[:, h, :], in_=pba[:64, h * n:(h + 1) * n], func=AF.Sigmoid,
                                                     bias=P64['rwkv_a0'][:, l, h:h + 1])), reads=[pka, 'pc'], writes=['T1'])
        S.op('act', lambda e: e.activation(out=Gg, in_=pbg[:64, 0:W].rearrange('p (h t) -> p h t', h=4), func=AF.Copy), reads=[pkg], writes=['Gg'])
        rmk = self.rmask[0:64, 0:64] if sample else self.rm64[:, 0:n]
        for h in range(4):
            S.op('dve', (lambda e, h=h: e.tensor_tensor_scan(out=T2[:, h, :], data0=rmk, data1=T0[:, h, :], initial=0.0, op0=TT.mult, op1=TT.add)),
                 reads=['T0', 'glac', 'rmask'], writes=['T2'])
        S.op('act', lambda e: e.activation(out=T3, in_=T2, func=AF.Exp, scale=-C0), reads=['T2'], writes=['T3'])
        S.op('act', lambda e: e.activation(out=T4, in_=T2, func=AF.Exp, scale=C0), reads=['T2'], writes=['T4'])
        S.op('pool', lambda e: e.tensor_tensor(out=T6, in0=T2, in1=T0, op=TT.subtract), reads=['T2', 'T0'], writes=['T6'])
        S.op('act', lambda e: e.activation(out=T0, in_=T6, func=AF.Exp, scale=-C0), reads=['T6'], writes=['T0'])
        if sample:
            S.op('pool', lambda e: e.tensor_copy(out=GL, in_=T3[:, :, :].rearrange('p h (j t) -> p h j t', t=4)[:, :, :, 3]), reads=['T3'], writes=['rGL'])
        else:
            S.op('pool', lambda e: e.tensor_copy(out=GL[:, :, 0:2], in_=T3[:, :, :].rearrange('p h (c t) -> p h c t', t=64)[:, :, :, 63]),
                 reads=['T3'], writes=['rGL'])
        S.op('dve', lambda e: e.tensor_tensor(out=T5, in0=S1a, in1=bc('rwkv_k_k'), op=TT.mult), reads=['XS', 'pc'], writes=['T5'])
        S.op('dve', lambda e: e.tensor_tensor(out=R(T6), in0=T5, in1=T5, op=TT.mult), reads=['T5', 'T0'], writes=['T6'])
        flat = lambda a: a[:, :, :].rearrange('p h t -> p (h t)')
        pb1, pk1 = self.pn()
        S.op('pe', lambda e: e.matmul(pb1[:64, :W], lhsT=self.ones64[:, :], rhs=R(flat(T6)), start=True, stop=True), reads=['T6', 'glac'], writes=[pk1])
        S.op('act', lambda e: e.activation(out=flat(T6), in_=pb1[:64, :W], func=AF.Sqrt), reads=[pk1], writes=['T6'])
        S.op('dve', lambda e: e.tensor_scalar(out=T6, in0=T6, scalar1=1e-12, scalar2=None, op0=TT.max), reads=['T6'], writes=['T6'])
        S.op('dve', lambda e: e.reciprocal(out=T6, in_=T6), reads=['T6'], writes=['T6'])
        S.op('dve', lambda e: e.tensor_tensor(out=T5, in0=T5, in1=T6, op=TT.mult), reads=['T5', 'T6'], writes=['T5'])
        S.op('dve', lambda e: e.scalar_tensor_tensor(out=T6, in0=T1, scalar=-1.0, in1=bc('rwkv_k_a'), op0=TT.add, op1=TT.mult),
             reads=['T1', 'T5', 'pc'], writes=['T6'])
        S.op('dve', lambda e: e.scalar_tensor_tensor(out=S1a, in0=T6, scalar=1.0, in1=S1a, op0=TT.add, op1=TT.mult), reads=['T6', 'XS'], writes=['XS'])
        S.op('dve', lambda e: e.tensor_tensor(out=T6, in0=S0a, in1=S1a, op=TT.mult), reads=['XS'], writes=['T6'])
        S.op('dve', lambda e: e.tensor_tensor(out=R(T6), in0=T6, in1=bc('rwkv_r_k'), op=TT.mult), reads=['T6', 'pc'], writes=['T6'])
        pb2, pk2 = self.pn()
        S.op('pe', lambda e: e.matmul(pb2[:64, :W], lhsT=self.ones64[:, :], rhs=R(flat(T6)), start=True, stop=True), reads=['T6', 'glac'], writes=[pk2])
        S.op('dve', lambda e: e.tensor_tensor(out=BON, in0=pb2[:64, 0:W].rearrange('p (h t) -> p h t', h=4), in1=S2a, op=TT.mult),
             reads=[pk2, 'XS'], writes=['BON'])
        S.op('dve', lambda e: e.scalar_tensor_tensor(out=R(S3a), in0=T5, scalar=-1.0, in1=T0, op0=TT.mult, op1=TT.mult),
             reads=['T5', 'T0', 'LB'], writes=['XS'])
        S.op('pool', lambda e: e.tensor_tensor(out=T6, in0=T5, in1=T1, op=TT.mult), reads=['T5', 'T1', pk2], writes=['T6'])
        S.op('dve', lambda e: e.tensor_tensor(out=R(S4a), in0=T6, in1=T4, op=TT.mult), reads=['T6', 'T4'], writes=['XS'])
        S.op('dve', lambda e: e.tensor_tensor(out=R(S1a), in0=S1a, in1=T4, op=TT.mult), reads=['XS', 'T4'], writes=['XS'])
        S.op('dve', lambda e: e.tensor_tensor(out=R(S0a), in0=S0a, in1=T3, op=TT.mult), reads=['XS', 'T3'], writes=['XS'])
        S.barrier()
        if sample:
            self.rwkv_chunks(l, n, sample, S0a, S1a, S2a, S3a, S4a, S5a, GL, Sst, TB)
        else:
            self.rwkv_chunks_prompt(n, S0a, S1a, S2a, S3a, S4a, S5a, GL, Sst, TB)
        S.barrier()
        O = S5a
        pbm, pkm = self.pn()
        S.op('pe', lambda e: e.matmul(pbm[:64, :W], lhsT=self.ones64[:, :], rhs=R(flat(O)), start=True, stop=True), reads=['glac'], writes=[pkm])
        S.op('dve', lambda e: e.scalar_tensor_tensor(out=T0, in0=pbm[:64, 0:W].rearrange('p (h t) -> p h t', h=4), scalar=-1.0 / 64, in1=O,
                                                     op0=TT.mult, op1=TT.add), reads=[pkm], writes=['T0'])
        S.op('dve', lambda e: e.tensor_tensor(out=R(T1), in0=T0, in1=T0, op=TT.mult), reads=['T0'], writes=['T1'])
        pbv, pkv = self.pn()
        S.op('pe', lambda e: e.matmul(pbv[:64, :W], lhsT=self.ones64[:, :], rhs=R(flat(T1)), start=True, stop=True), reads=['T1', 'glac'], writes=[pkv])
        S.op('act', lambda e: e.activation(out=flat(T1), in_=pbv[:64, :W], func=AF.Ln, scale=1.0 / 64, bias=self.gn_eps[:]),
             reads=[pkv, 'glac'], writes=['T1'])
        S.op('act', lambda e: e.activation(out=T1, in_=T1, func=AF.Exp, scale=-0.5), reads=['T1'], writes=['T1'])
        S.op('dve', lambda e: e.tensor_tensor(out=T0, in0=T0, in1=T1, op=TT.mult), reads=['T0', 'T1'], writes=['T0'])
        S.op('dve', lambda e: e.tensor_tensor(out=T0, in0=T0, in1=bc('rwkv_ln_g'), op=TT.mult), reads=['T0', 'pc'], writes=['T0'])
        S.op('dve', lambda e: e.tensor_tensor(out=T0, in0=T0, in1=bc('rwkv_ln_b'), op=TT.add), reads=['T0', 'pc'], writes=['T0'])
        S.op('dve', lambda e: e.tensor_tensor(out=T0, in0=T0, in1=BON, op=TT.add), reads=['T0', 'BON'], writes=['T0'])
        S.op('dve', lambda e: e.tensor_tensor(out=R(T0), in0=T0, in1=Gg, op=TT.mult), reads=['T0', 'Gg'], writes=['T0'])
        self.heads_to_ymix(T0, 'T0', n)
        if si == 15:
            pbo, pko = self.pn()

            def fo(e):
                r = None
                for h in range(4):
                    r = e.transpose(out=pbo[:64, h * 64:(h + 1) * 64], in_=Sst[:, h, :], identity=self.ident[0:64, 0:64])
                return r
            S.op('pe', fo, reads=['rS', 'ident'], writes=[pko])
            stg, sk = self.next_stage()
            S.op('act', lambda e: e.activation(out=stg[:64, 0:256], in_=pbo[:64, 0:256], func=AF.Copy), reads=[pko], writes=[sk])
            S.op('sp', lambda e: e.dma_start(out=self.o_p_wkv[l].rearrange('h v k -> v h k'), in_=stg[:64, 0:256].rearrange('p (h k) -> p h k', h=4)),
                 reads=[sk], writes=['o_p_wkv'], dma='o_p_wkv')
            if 'o_p_wkv' not in self.outkeys:
                self.outkeys.append('o_p_wkv')

    def rwkv_unit(self, c, hs, sid, RT, KT, XV, AT, BT, O, GL, Sst, Sr, TB):
        S = self.S
        av = self.av
        L = 64
        nh = len(hs)
        h0 = hs[0]
        cs = slice(c * L, (c + 1) * L)
        W2 = nh * 64
        tv = lambda i: av(TB + sid * 1792 + i * 128, [64, nh, 64])
        M3, M2, M4, Q, QT, Qn, QTn, Tm, Vtok, Ktok, Btok, W0T, Utok, Stmp = [tv(i) for i in range(14)]
        mU, msU, msL = self.triU, self.sU, self.sL
        b4 = lambda m: m[:, :].unsqueeze(1).to_broadcast([64, nh, 64])
        p4 = lambda pb: pb[:64, 0:W2].rearrange('p (h t) -> p h t', h=nh)
        id4 = self.ident[0:64, 0:64].unsqueeze(1).to_broadcast([64, nh, 64])
        k = lambda nm: 'u%d%s' % (sid, nm)
        XSK = 'XS'
        SK = 'rS%d' % sid
        SRK = 'rSr%d' % sid

        def mm(lhs_fn, rhs_fn, reads):
            pb, pk = self.pn()

            def f(e, pb=pb):
                r = None
                for j in range(nh):
                    r = e.matmul(pb[:64, j * 64:(j + 1) * 64], lhsT=R(lhs_fn(j)), rhs=R(rhs_fn(j)), start=True, stop=True)
                return r
            S.op('pe', f, reads=reads, writes=[pk])
            return pb, pk
        A = lambda arr: (lambda j: arr[:, h0 + j, cs])
        Tt = lambda buf: (lambda j: buf[:, j, :])
        pb, pk = mm(A(BT), A(AT), [XSK])
        S.op('dve', (lambda e, pb=pb: e.tensor_tensor(out=R(Q), in0=p4(pb), in1=b4(msU), op=ALU.mult)), reads=[pk, 'glac'], writes=[k('Q')])
        pb, pk = mm(A(AT), A(BT), [XSK])
        S.op('dve', (lambda e, pb=pb: e.tensor_tensor(out=R(QT), in0=p4(pb), in1=b4(msL), op=ALU.mult)), reads=[pk, 'glac'], writes=[k('QT')])
        yield
        pb, pk = mm(A(BT), A(RT), [XSK])
        S.op('dve', (lambda e, pb=pb: e.tensor_tensor(out=R(M3), in0=p4(pb), in1=b4(mU), op=ALU.mult)), reads=[pk, 'glac'], writes=[k('M3')])
        pb, pk = mm(A(KT), A(AT), [XSK])
        S.op('dve', (lambda e, pb=pb: e.tensor_tensor(out=R(M2), in0=p4(pb), in1=b4(msU), op=ALU.mult)), reads=[pk, 'glac'], writes=[k('M2')])
        pb, pk = mm(A(KT), A(RT), [XSK])
        S.op('dve', (lambda e, pb=pb: e.tensor_tensor(out=R(M4), in0=p4(pb), in1=b4(mU), op=ALU.mult)), reads=[pk, 'glac'], writes=[k('M4')])
        S.op('dve', lambda e: e.tensor_tensor(out=R(Tm), in0=Q, in1=id4, op=ALU.add), reads=[k('Q'), 'ident'], writes=[k('T')])
        yield
        q, qt, qn, qtn = Q, QT, Qn, QTn
        kq, kqt, kqn, kqtn = k('Q'), k('QT'), k('Qn'), k('QTn')
        nlev = 5
        for lev in range(nlev):
            last = (lev == nlev - 1)
            pbq2, pkq2 = mm(Tt(q), Tt(qt), [kq, kqt])
            S.op('act', (lambda e, pb=pbq2, qtn=qtn: e.activation(out=R(qtn), in_=p4(pb), func=AF.Copy)), reads=[pkq2], writes=[kqtn])
            if not last:
                pbq1, pkq1 = mm(Tt(qt), Tt(q), [kq, kqt])
                S.op('dve', (lambda e, pb=pbq1, qn=qn: e.tensor_copy(out=R(qn), in_=p4(pb))), reads=[pkq1], writes=[kqn])
            yield
            pbt, pkt = mm(Tt(qtn), Tt(Tm), [kqtn, k('T')])
            S.op('dve', (lambda e, pb=pbt: e.tensor_tensor(out=R(Tm), in0=p4(pb), in1=Tm, op=ALU.add)), reads=[pkt, k('T')], writes=[k('T')])
            q, qt, qn, qtn = qn, qtn, q, qt
            kq, kqt, kqn, kqtn = kqn, kqtn, kq, kqt
            if lev == 1:
                for src, dst, dkey in ((XV, Vtok, k('Vtok')), (KT, Ktok, k('Ktok')), (BT, Btok, k('Btok'))):
                    pbT, pkT = self.pn()

                    def fT(e, pbT=pbT, src=src):
                        r = None
                        for j in range(nh):
                            r = e.transpose(out=pbT[:64, j * 64:(j + 1) * 64], in_=src[:, h0 + j, cs], identity=self.ident[0:64, 0:64])
                        return r
                    S.op('pe', fT, reads=[XSK, 'ident'], writes=[pkT])
                    S.op('act', (lambda e, pbT=pbT, dst=dst: e.activation(out=R(dst), in_=p4(pbT), func=AF.Copy)), reads=[pkT], writes=[dkey])
            yield
        pbw, pkw = self.pn()

        def fW(e, pbw=pbw):
            r = None
            for j in range(nh):
                o_ = pbw[:64, j * 64:(j + 1) * 64]
                e.matmul(o_, lhsT=R(AT[:, h0 + j, cs]), rhs=R(Sr[:, h0 + j, :]), start=True, stop=False)
                r = e.matmul(o_, lhsT=R(M2[:, j, :]), rhs=R(Vtok[:, j, :]), start=False, stop=True)
            return r
        S.op('pe', fW, reads=[XSK, SRK, k('M2'), k('Vtok')], writes=[pkw])
        S.op('act', (lambda e, pbw=pbw: e.activation(out=R(W0T), in_=p4(pbw), func=AF.Copy)), reads=[pkw], writes=[k('W0T')])
        yield
        pbu, pku = mm(Tt(Tm), Tt(W0T), [k('T'), k('W0T')])
        S.op('act', (lambda e, pbu=pbu: e.activation(out=R(Utok), in_=p4(pbu), func=AF.Copy)), reads=[pku], writes=[k('Utok')])
        yield
        pby, pky = self.pn()

        def fY(e, pby=pby):
            r = None
            for j in range(nh):
                o_ = pby[:64, j * 64:(j + 1) * 64]
                e.matmul(o_, lhsT=R(Sr[:, h0 + j, :]), rhs=R(RT[:, h0 + j, cs]), start=True, stop=False)
                e.matmul(o_, lhsT=R(Utok[:, j, :]), rhs=R(M3[:, j, :]), start=False, stop=False)
                r = e.matmul(o_, lhsT=R(Vtok[:, j, :]), rhs=R(M4[:, j, :]), start=False, stop=True)
            return r
        S.op('pe', fY, reads=[XSK, SRK, k('Utok'), k('M3'), k('Vtok'), k('M4')], writes=[pky])
        pbs, pks = self.pn()

        def fS(e, pbs=pbs):
            r = None
            for j in range(nh):
                o_ = pbs[:64, j * 64:(j + 1) * 64]
                e.matmul(o_, lhsT=R(Btok[:, j, :]), rhs=R(Utok[:, j, :]), start=True, stop=False)
                r = e.matmul(o_, lhsT=R(Ktok[:, j, :]), rhs=R(Vtok[:, j, :]), start=False, stop=True)
            return r
        S.op('pe', fS, reads=[k('Btok'), k('Utok'), k('Ktok'), k('Vtok')], writes=[pks])
        S.op('act', (lambda e, pby=pby: e.activation(out=R(O[:, h0:h0 + nh, cs]), in_=p4(pby), func=AF.Copy)), reads=[pky], writes=['rO%d' % sid])
        S.op('dve', (lambda e, pbs=pbs: e.tensor_tensor(out=Stmp, in0=p4(pbs), in1=Sst[:, h0:h0 + nh, :], op=ALU.add)), reads=[pks, SK], writes=[k('Stmp')])
        S.op('dve', lambda e: e.tensor_tensor(out=Sst[:, h0:h0 + nh, :], in0=Stmp, in1=GL[:, h0:h0 + nh, c:c + 1].to_broadcast([64, nh, 64]), op=ALU.mult),
             reads=[k('Stmp'), 'rGL'], writes=[SK])
        S.op('act', lambda e: e.activation(out=R(Sr[:, h0:h0 + nh, :]), in_=Sst[:, h0:h0 + nh, :], func=AF.Copy), reads=[SK], writes=[SRK])
        yield

    def rwkv_chunks_prompt(self, n, RT, KT, XV, AT, BT, O, GL, Sst, TB):
        S = self.S
        av = self.av
        nch = n // 64
        Sr = av(TB + 14 * 256, [64, 4, 64])
        chains = []
        for sid, hs in enumerate(([0, 1], [2, 3])):
            S.op('act', (lambda e, hs=hs: e.activation(out=R(Sr[:, hs[0]:hs[0] + 2, :]), in_=Sst[:, hs[0]:hs[0] + 2, :], func=AF.Copy)),
                 reads=['rS', 'rS%d' % sid], writes=['rSr%d' % sid])

            def chain(sid=sid, hs=hs):
                for c in range(nch):
                    yield from self.rwkv_unit(c, hs, sid, RT, KT, XV, AT, BT, O, GL, Sst, Sr, TB)
            chains.append(chain())
        alive = list(chains)
        while alive:
            for g in list(alive):
                try:
                    next(g)
                except StopIteration:
                    alive.remove(g)

    def rwkv_chunks(self, l, n, sample, RT, KT, XV, AT, BT, O, GL, Sst, TB):
        S = self.S
        av = self.av
        L = 64
        nch = n // L
        tv = lambda i: av(TB + i * 256, [64, 4, 64])
        M3, M2, M4, Q, QT, Qn, QTn, Tm, Vtok, Ktok, Btok, W0T, Utok, Stmp = [tv(i) for i in range(14)]
        Sr = av(TB + 14 * 256, [64, 4, 64])
        mU, msU, msL = (self.blkU, self.bsU, self.bsL) if sample else (self.triU, self.sU, self.sL)
        nlev = 1 if sample else 5
        b4 = lambda m: m[:, :].unsqueeze(1).to_broadcast([64, 4, 64])
        p4 = lambda pb: pb[:64, 0:256].rearrange('p (h t) -> p h t', h=4)
        id4 = self.ident[0:64, 0:64].unsqueeze(1).to_broadcast([64, 4, 64])
        KK = 'rc'

        def mm4(lhs_fn, rhs_fn, reads):
            pb, pk = self.pn()

            def f(e, pb=pb):
                r = None
                for h in range(4):
                    r = e.matmul(pb[:64, h * 64:(h + 1) * 64], lhsT=R(lhs_fn(h)), rhs=R(rhs_fn(h)), start=True, stop=True)
                return r
            S.op('pe', f, reads=reads, writes=[pk])
            return pb, pk

        if sample:
            S0h = av(TB + 14 * 256, [64, 16, 64])
            Uexp = av(TB + 14 * 256 + 1024, [64, 16, 64])
            hidf = self.hid[:].rearrange('p a b -> p (a b)').bitcast(F32)
            Vexp = hidf[0:64, 1024:2048].rearrange('p (j v) -> p j v', j=16)
        if not sample:
            S.op('act', lambda e: e.activation(out=R(Sr), in_=Sst, func=AF.Copy), reads=['rS'], writes=['rSr'])
        for c in range(nch):
            cs = slice(c * L, (c + 1) * L)
            pb, pk = mm4(lambda h, cs=cs: BT[:, h, cs], lambda h, cs=cs: AT[:, h, cs], ['XS'])
            S.op('dve', (lambda e, pb=pb: e.tensor_tensor(out=R(Q), in0=p4(pb), in1=b4(msU), op=ALU.mult)), reads=[pk, 'glac'], writes=['rQ'])
            pb, pk = mm4(lambda h, cs=cs: AT[:, h, cs], lambda h, cs=cs: BT[:, h, cs], ['XS'])
            S.op('dve', (lambda e, pb=pb: e.tensor_tensor(out=R(QT), in0=p4(pb), in1=b4(msL), op=ALU.mult)), reads=[pk, 'glac'], writes=['rQT'])
            pb, pk = mm4(lambda h, cs=cs: BT[:, h, cs], lambda h, cs=cs: RT[:, h, cs], ['XS'])
            S.op('dve', (lambda e, pb=pb: e.tensor_tensor(out=R(M3), in0=p4(pb), in1=b4(mU), op=ALU.mult)), reads=[pk, 'glac'], writes=['rM3'])
            pb, pk = mm4(lambda h, cs=cs: KT[:, h, cs], lambda h, cs=cs: AT[:, h, cs], ['XS'])
            S.op('dve', (lambda e, pb=pb: e.tensor_tensor(out=R(M2), in0=p4(pb), in1=b4(msU), op=ALU.mult)), reads=[pk, 'glac'], writes=['rM2'])
            pb, pk = mm4(lambda h, cs=cs: KT[:, h, cs], lambda h, cs=cs: RT[:, h, cs], ['XS'])
            S.op('dve', (lambda e, pb=pb: e.tensor_tensor(out=R(M4), in0=p4(pb), in1=b4(mU), op=ALU.mult)), reads=[pk, 'glac'], writes=['rM4'])
            S.op('dve', lambda e: e.tensor_tensor(out=R(Tm), in0=Q, in1=id4, op=ALU.add), reads=['rQ', 'ident'], writes=['rT'])
            q, qt, qn, qtn = Q, QT, Qn, QTn
            kq, kqt, kqn, kqtn = 'rQ', 'rQT', 'rQn', 'rQTn'
            for lev in range(nlev):
                last = (lev == nlev - 1)
                pbq2, pkq2 = mm4(lambda h, q=q: q[:, h, :], lambda h, qt=qt: qt[:, h, :], [kq, kqt])
                S.op('act', (lambda e, pb=pbq2, qtn=qtn: e.activation(out=R(qtn), in_=p4(pb), func=AF.Copy)), reads=[pkq2], writes=[kqtn])
                if not last:
                    pbq1, pkq1 = mm4(lambda h, qt=qt: qt[:, h, :], lambda h, q=q: q[:, h, :], [kq, kqt])
                    S.op('dve', (lambda e, pb=pbq1, qn=qn: e.tensor_copy(out=R(qn), in_=p4(pb))), reads=[pkq1], writes=[kqn])
                pbt, pkt = mm4(lambda h, qtn=qtn: qtn[:, h, :], lambda h: Tm[:, h, :], [kqtn, 'rT'])
                S.op('dve', (lambda e, pb=pbt: e.tensor_tensor(out=R(Tm), in0=p4(pb), in1=Tm, op=ALU.add)), reads=[pkt, 'rT'], writes=['rT'])
                q, qt, qn, qtn = qn, qtn, q, qt
                kq, kqt, kqn, kqtn = kqn, kqtn, kq, kqt
            for src, dst, dkey in ((XV, Vtok, 'rVtok'), (KT, Ktok, 'rKtok'), (BT, Btok, 'rBtok')):
                pbT, pkT = self.pn()

                def fT(e, pbT=pbT, src=src, cs=cs):
                    r = None
                    for h in range(4):
                        r = e.transpose(out=pbT[:64, h * 64:(h + 1) * 64], in_=src[:, h, cs], identity=self.ident[0:64, 0:64])
                    return r
                S.op('pe', fT, reads=['XS', 'ident'], writes=[pkT])
                S.op('act', (lambda e, pbT=pbT, dst=dst: e.activation(out=R(dst), in_=p4(pbT), func=AF.Copy)), reads=[pkT], writes=[dkey])
            if not sample:
                pbw, pkw = self.pn()

                def fW(e, pbw=pbw, cs=cs):
                    r = None
                    for h in range(4):
                        o_ = pbw[:64, h * 64:(h + 1) * 64]
                        e.matmul(o_, lhsT=R(AT[:, h, cs]), rhs=R(Sr[:, h, :]), start=True, stop=False)
                        r = e.matmul(o_, lhsT=R(M2[:, h, :]), rhs=R(Vtok[:, h, :]), start=False, stop=True)
                    return r
                S.op('pe', fW, reads=['XS', 'rSr', 'rM2', 'rVtok'], writes=[pkw])
                S.op('act', (lambda e, pbw=pbw: e.activation(out=R(W0T), in_=p4(pbw), func=AF.Copy)), reads=[pkw], writes=['rW0T'])
                pbu, pku = mm4(lambda h: Tm[:, h, :], lambda h: W0T[:, h, :], ['rT', 'rW0T'])
                S.op('act', (lambda e, pbu=pbu: e.activation(out=R(Utok), in_=p4(pbu), func=AF.Copy)), reads=[pku], writes=['rUtok'])
                pby, pky = self.pn()

                def fY(e, pby=pby, cs=cs):
                    r = None
                    for h in range(4):
                        o_ = pby[:64, h * 64:(h + 1) * 64]
                        e.matmul(o_, lhsT=R(Sr[:, h, :]), rhs=R(RT[:, h, cs]), start=True, stop=False)
                        e.matmul(o_, lhsT=R(Utok[:, h, :]), rhs=R(M3[:, h, :]), start=False, stop=False)
                        r = e.matmul(o_, lhsT=R(Vtok[:, h, :]), rhs=R(M4[:, h, :]), start=False, stop=True)
                    return r
                S.op('pe', fY, reads=['XS', 'rSr', 'rUtok', 'rM3', 'rVtok', 'rM4'], writes=[pky])
                S.op('act', (lambda e, pby=pby, cs=cs: e.activation(out=O[:, :, cs], in_=p4(pby), func=AF.Copy)), reads=[pky], writes=['rO'])
                pbs, pks = self.pn()

                def fS(e, pbs=pbs):
                    r = None
                    for h in range(4):
                        o_ = pbs[:64, h * 64:(h + 1) * 64]
                        e.matmul(o_, lhsT=R(Btok[:, h, :]), rhs=R(Utok[:, h, :]), start=True, stop=False)
                        r = e.matmul(o_, lhsT=R(Ktok[:, h, :]), rhs=R(Vtok[:, h, :]), start=False, stop=True)
                    return r
                S.op('pe', fS, reads=['rBtok', 'rUtok', 'rKtok', 'rVtok'], writes=[pks])
                S.op('dve', (lambda e, pbs=pbs: e.tensor_tensor(out=Stmp, in0=p4(pbs), in1=Sst, op=ALU.add)), reads=[pks, 'rS'], writes=['rStmp'])
                S.op('dve', (lambda e, c=c: e.tensor_tensor(out=Sst, in0=Stmp, in1=GL[:, :, c:c + 1].to_broadcast([64, 4, 64]), op=ALU.mult)),
                     reads=['rStmp', 'rGL'], writes=['rS'])
                S.op('act', lambda e: e.activation(out=R(Sr), in_=Sst, func=AF.Copy), reads=['rS'], writes=['rSr'])
            else:
                for h in range(4):
                    stg, sk = self.next_stage()
                    for jj in range(2):
                        S.op('sp', (lambda e, stg=stg, h=h, jj=jj: e.dma_start(
                            out=stg[jj * 64:(jj + 1) * 64, 0:512].rearrange('p (a k) -> p a k', a=8),
                            in_=self.st_wkv[l, :, h].rearrange('(jp jj) v k -> jj v jp k', jj=2)[jj])), writes=[sk], dma=sk)
                    for half in range(2):
                        pbl, pkl = self.pn()

                        def fl(e, pbl=pbl, stg=stg, half=half):
                            r = None
                            for a in range(4):
                                jp = half * 4 + a
                                r = e.transpose(out=pbl[:64, a * 128:(a + 1) * 128], in_=stg[:, jp * 64:(jp + 1) * 64], identity=self.ident[:, :])
                            return r
                        S.op('pe', fl, reads=[sk, 'ident'], writes=[pkl])
                        S.op('act', (lambda e, pbl=pbl, half=half: e.activation(out=R(S0h[:, half * 8:(half + 1) * 8, :]),
                                                                               in_=pbl[:64, :].rearrange('p (j v) -> p j v', j=8), func=AF.Copy)),
                             reads=[pkl], writes=['rS0h'])
                    pbw, pkw = self.pn()

                    def fW(e, pbw=pbw, h=h):
                        e.matmul(pbw[:64, 0:64], lhsT=R(Vtok[:, h, :]), rhs=R(M2[:, h, :]), start=True, stop=False)
                        r = None
                        for j in range(NS):
                            r = e.matmul(pbw[:64, 4 * j:4 * j + 4], lhsT=R(S0h[:, j, :]), rhs=R(AT[:, h, 4 * j:4 * j + 4]), start=False, stop=(j == NS - 1))
                        return r
                    S.op('pe', fW, reads=['XS', 'rS0h', 'rM2', 'rVtok'], writes=[pkw])
                    S.op('act', (lambda e, pbw=pbw, h=h: e.activation(out=Stmp[:, h, :], in_=pbw[:64, 0:64], func=AF.Copy)), reads=[pkw], writes=['rStmp'])
                    pbx, pkx = self.pn()
                    S.op('pe', (lambda e, pbx=pbx, h=h: e.transpose(out=pbx[:64, 0:64], in_=Stmp[:, h, :], identity=self.ident[0:64, 0:64])),
                         reads=['rStmp', 'ident'], writes=[pkx])
                    S.op('act', (lambda e, pbx=pbx, h=h: e.activation(out=R(W0T[:, h, :]), in_=pbx[:64, 0:64], func=AF.Copy)), reads=[pkx], writes=['rW0T'])
                    pbu, pku = self.pn()
                    S.op('pe', (lambda e, pbu=pbu, h=h: e.matmul(pbu[:64, 0:64], lhsT=R(Tm[:, h, :]), rhs=R(W0T[:, h, :]), start=True, stop=True)),
                         reads=['rT', 'rW0T'], writes=[pku])
                    S.op('act', (lambda e, pbu=pbu, h=h: e.activation(out=R(Utok[:, h, :]), in_=pbu[:64, 0:64], func=AF.Copy)), reads=[pku], writes=['rUtok'])
                    pby, pky = self.pn()

                    def fY(e, pby=pby, h=h):
                        e.matmul(pby[:64, 0:64], lhsT=R(Utok[:, h, :]), rhs=R(M3[:, h, :]), start=True, stop=False)
                        e.matmul(pby[:64, 0:64], lhsT=R(Vtok[:, h, :]), rhs=R(M4[:, h, :]), start=False, stop=False)
                        r = None
                        for j in range(NS):
                            r = e.matmul(pby[:64, 4 * j:4 * j + 4], lhsT=R(S0h[:, j, :]), rhs=R(RT[:, h, 4 * j:4 * j + 4]), start=False, stop=(j == NS - 1))
                        return r
                    S.op('pe', fY, reads=['XS', 'rS0h', 'rUtok', 'rM3', 'rVtok', 'rM4'], writes=[pky])
                    S.op('act', (lambda e, pby=pby, h=h: e.activation(out=R(O[:, h, :]), in_=pby[:64, 0:64], func=AF.Copy)), reads=[pky], writes=['rO'])
                    segb = self.segm[:, :].unsqueeze(2).to_broadcast([64, 16, 64])
                    S.op('dve', (lambda e, h=h: e.tensor_tensor(out=R(Uexp), in0=Utok[:, h, :].unsqueeze(1).to_broadcast([64, 16, 64]), in1=segb, op=ALU.mult)),
                         reads=['rUtok', 'glac'], writes=['rUexp'])
                    S.op('dve', (lambda e, h=h: e.tensor_tensor(out=Vexp, in0=Vtok[:, h, :].unsqueeze(1).to_broadcast([64, 16, 64]), in1=segb, op=ALU.mult)),
                         reads=['rVtok', 'glac'], writes=['rVexp'])
                    for jj in range(2):
                        js = slice(8 * jj, 8 * jj + 8)
                        pbs, pks = self.pn()

                        def fS(e, pbs=pbs, h=h, js=js):
                            e.matmul(pbs[:64, :], lhsT=R(Btok[:, h, :]), rhs=R(Uexp[:, js, :]), start=True, stop=False)
                            return e.matmul(pbs[:64, :], lhsT=Ktok[:, h, :], rhs=Vexp[:, js, :], start=False, stop=True)
                        S.op('pe', fS, reads=['rBtok', 'rUexp', 'rKtok', 'rVexp'], writes=[pks])
                        S.op('dve', (lambda e, pbs=pbs, js=js: e.tensor_tensor(out=S0h[:, js, :], in0=pbs[:64, :].rearrange('p (j v) -> p j v', j=8),
                                                                               in1=S0h[:, js, :], op=ALU.add)), reads=[pks, 'rS0h'], writes=['rS0h'])
                        S.op('dve', (lambda e, h=h, js=js: e.tensor_tensor(out=S0h[:, js, :], in0=S0h[:, js, :],
                                                                           in1=GL[:, h, js].unsqueeze(2).to_broadcast([64, 8, 64]), op=ALU.mult)),
                             reads=['rS0h', 'rGL'], writes=['rS0h'])
                    pbo, pko = self.pn()

                    def fo(e, pbo=pbo):
                        r = None
                        for jp in range(8):
                            r = e.transpose(out=pbo[:, jp * 64:(jp + 1) * 64], in_=S0h[:, 2 * jp:2 * jp + 2, :].rearrange('p j v -> p (j v)'),
                                            identity=self.ident[0:64, 0:64])
                        return r
                    S.op('pe', fo, reads=['rS0h', 'ident'], writes=[pko])
                    stg2, sk2 = self.next_stage()
                    S.op('act', (lambda e, pbo=pbo, stg2=stg2: e.activation(out=stg2[:, 0:512], in_=pbo[:, 0:512], func=AF.Copy)), reads=[pko], writes=[sk2])
                    for jj in range(2):
                        S.op('sp', (lambda e, stg2=stg2, h=h, jj=jj: e.dma_start(
                            out=self.o_s_wkv[l, :, h].rearrange('(jp jj) v k -> jj v jp k', jj=2)[jj],
                            in_=stg2[jj * 64:(jj + 1) * 64, 0:512].rearrange('p (a k) -> p a k', a=8))),
                            reads=[sk2], writes=['o_s_wkv'], dma='o_s_wkv')
                if 'o_s_wkv' not in self.outkeys:
                    self.outkeys.append('o_s_wkv')

    def shift_capture(self, l, ti, j, pb, pk):
        S = self.S
        if ti == 3:
            S.op('act', (lambda e: e.activation(out=self.shp[:, j:j + 1], in_=pb[:, 511:512], func=AF.Copy)), reads=[pk], writes=['shp'])
        elif ti == 4:
            S.op('act', (lambda e: e.activation(out=self.shs[:, j, :], in_=pb[:, 0:64].rearrange('p (s t) -> p s t', t=4)[:, :, 3], func=AF.Copy)),
                 reads=[pk], writes=['shs'])

    def shift_store(self, l, ti):
        S = self.S
        if ti == 3:
            S.op('sp', lambda e: e.dma_start(out=self.o_p_shift[l].rearrange('(t p) -> p t', p=128), in_=self.shp[:, :]),
                 reads=['shp'], writes=['o_p_shift'], dma='o_p_shift')
            if 'o_p_shift' not in self.outkeys:
                self.outkeys.append('o_p_shift')
        elif ti == 4:
            for half in range(2):
                self.store_fm([(self.shs[:, half * 4 + j, :], 'shs') for j in range(4)], 16,
                              self.o_s_shift[l][:, half * 512:(half + 1) * 512], 'o_s_shift')

    def pool_sample(self, l, win, slot):
        S = self.S
        TT = ALU
        ti = 4
        c0, n = TTS[ti]
        pext, ps2, ps4, pooled, u_s = self.pext_s, self.ps2_s, self.ps4_s, self.pooled, self.u_s
        wins = [2, 4, 8, 16]
        for blk in range(2):
            stg, sk = self.next_stage()
            S.op('sp', (lambda e, stg=stg, blk=blk: e.dma_start(
                out=stg[:120, 0:256], in_=self.st_pool[l, blk * 8:(blk + 1) * 8].rearrange('j r f -> (j r) f'))), writes=[sk], dma=sk)
            pb, pk = self.pn()

            def f(e, pb=pb, stg=stg):
                r = None
                for t in range(2):
                    r = e.transpose(out=pb[:, t * 128:t * 128 + 120], in_=stg[:120, t * 128:(t + 1) * 128], identity=self.ident[:120, :120])
                return r
            S.op('pe', f, reads=[sk, 'ident'], writes=[pk])
            for t in range(2):
                S.op('act', (lambda e, pb=pb, t=t, blk=blk: e.activation(
                    out=pext[:, t, blk * 8:(blk + 1) * 8, 1:16], in_=pb[:, t * 128:t * 128 + 120].rearrange('p (j r) -> p j r', j=8), func=AF.Copy)),
                    reads=[pk], writes=['pext_s'])
        for j in range(2):
            pb, pk = self.proj(win, j, ti, slot)
            S.op('act', (lambda e, pb=pb, j=j: e.activation(out=u_s[:, j, :], in_=pb[:, :64], func=AF.Copy)), reads=[pk], writes=['u_s'])
            S.op('pool', (lambda e, j=j: e.tensor_copy(out=pext[:, j, :, 16:20], in_=u_s[:, j, :].rearrange('p (s t) -> p s t', t=4))),
                 reads=['u_s'], writes=['pext_s'])
        W = 20
        for t in range(2):
            S.op('pool', (lambda e, t=t: e.tensor_tensor(out=ps2[:, t, :, 1:W], in0=pext[:, t, :, 1:W], in1=pext[:, t, :, 0:W - 1], op=TT.add)),
                 reads=['pext_s'], writes=[('ps2_s', t)])
            S.op('pool', (lambda e, t=t: e.tensor_tensor(out=ps4[:, t, :, 3:W], in0=ps2[:, t, :, 3:W], in1=ps2[:, t, :, 1:W - 2], op=TT.add)),
                 reads=[('ps2_s', t)], writes=[('ps4_s', t)])

        def grp(e, src, g):
            t, hf = g // 2, g % 2
            sl = slice(hf * 64, (hf + 1) * 64)
            return e.scalar_tensor_tensor(out=pooled[sl, t, 0:64].rearrange('p (s t) -> p s t', t=4), in0=src[sl, t, :, 16:20], scalar=1.0 / wins[g],
                                          in1=pext[sl, t, :, 16:20], op0=TT.mult, op1=TT.subtract)
        S.op('dve', lambda e: grp(e, ps2, 0), reads=[('ps2_s', 0), 'pext_s'], writes=['pooled'])
        S.op('dve', lambda e: grp(e, ps4, 1), reads=[('ps4_s', 0), 'pext_s'], writes=['pooled'])
        t = 1
        S.op('pool', lambda e: e.tensor_tensor(out=ps2[:, t, :, 7:W], in0=ps4[:, t, :, 7:W], in1=ps4[:, t, :, 3:W - 4], op=TT.add),
             reads=[('ps4_s', 1)], writes=[('ps2_s', 1)])
        S.op('dve', lambda e: grp(e, ps2, 2), reads=[('ps2_s', 1), 'pext_s'], writes=['pooled'])
        S.op('pool', lambda e: e.tensor_tensor(out=ps4[:, t, :, 15:W], in0=ps2[:, t, :, 15:W], in1=ps2[:, t, :, 7:W - 8], op=TT.add),
             reads=[('ps2_s', 1), 'pooled'], writes=[('ps4_s', 1)])
        S.op('dve', lambda e: grp(e, ps4, 3), reads=[('ps4_s', 1), 'pext_s'], writes=['pooled'])
        self.pool_out(l, ti)
        S.op('sp', lambda e: e.dma_start(out=self.o_s_pool[l][:, 0:11, :], in_=self.st_pool[l][:, 4:15, :]), writes=['o_s_pool'], dma='o_s_pool')
        stg, sk = self.next_stage()
        pb, pk = self.pn()

        def f(e):
            r = None
            for t in range(2):
                r = e.transpose(out=pb[:64, t * 128:(t + 1) * 128], in_=u_s[:, t, :], identity=self.ident[:, :])
            return r
        S.op('pe', f, reads=['u_s', 'ident'], writes=[pk])
        S.op('act', lambda e: e.activation(out=stg[:64, 0:256], in_=pb[:64, 0:256], func=AF.Copy), reads=[pk], writes=[sk])
        for j in range(NS):
            S.op('sp', (lambda e, j=j: e.dma_start(out=self.o_s_pool[l][j, 11:15, :], in_=stg[4 * j:4 * j + 4, 0:256])),
                 reads=[sk], writes=['o_s_pool'], dma='o_s_pool')
        if 'o_s_pool' not in self.outkeys:
            self.outkeys.append('o_s_pool')

    def mix_pool(self, l, ti, win, slot):
        if not ENABLE['pool']:
            return self.zero_ymix(ti)
        S = self.S
        c0, n = TTS[ti]
        pext, ps2, ps4, pooled = self.pext, self.ps2, self.ps4, self.pooled
        wins = [2, 4, 8, 16]
        if ti < 4:
            if ti == 0:
                S.op('pool', lambda e: e.memset(pext[:, :, 0:16], 0.0), writes=['pext'])
            else:
                S.op('pool', lambda e: e.tensor_copy(out=pext[:, :, 0:16], in_=pext[:, :, 512:528]),
                     reads=['pext'], writes=['pext'])
            for j in range(2):
                pb, pk = self.proj(win, j, ti, slot)
                S.op('act', (lambda e, pb=pb, j=j: e.activation(out=pext[:, j, 16:16 + n], in_=pb[:, :n], func=AF.Copy)),
                     reads=[pk], writes=['pext'])
            W = 16 + n
            x3 = lambda a, b: pext[:, :, a:b]
            S.op('pool', lambda e: e.tensor_tensor(out=ps2[:, :, 1:W], in0=pext[:, :, 1:W], in1=pext[:, :, 0:W - 1], op=ALU.add),
                 reads=['pext'], writes=['ps2'])
            S.op('pool', lambda e: e.tensor_tensor(out=ps4[:, :, 3:W], in0=ps2[:, :, 3:W], in1=ps2[:, :, 1:W - 2], op=ALU.add),
                 reads=['ps2'], writes=['ps4'])
            def grp(e, src, g):
                t, hf = g // 2, g % 2
                sl = slice(hf * 64, (hf + 1) * 64)
                return e.scalar_tensor_tensor(out=pooled[sl, t, :n], in0=src[sl, t, 16:16 + n], scalar=1.0 / wins[g],
                                              in1=pext[sl, t, 16:16 + n], op0=ALU.mult, op1=ALU.subtract)
            S.op('dve', lambda e: grp(e, ps2, 0), reads=['ps2', 'pext'], writes=['pooled'])
            S.op('dve', lambda e: grp(e, ps4, 1), reads=['ps4', 'pext'], writes=['pooled'])
            if ti == 0:
                self.pool_fix(ps2, 0)
                self.pool_fix(ps4, 1)
            S.op('pool', lambda e: e.tensor_tensor(out=ps2[:, :, 7:W], in0=ps4[:, :, 7:W], in1=ps4[:, :, 3:W - 4], op=ALU.add),
                 reads=['ps4', 'pooled'], writes=['ps2'])
            S.op('dve', lambda e: grp(e, ps2, 2), reads=['ps2', 'pext'], writes=['pooled'])
            if ti == 0:
                self.pool_fix(ps2, 2)
            S.op('pool', lambda e: e.tensor_tensor(out=ps4[:, :, 15:W], in0=ps2[:, :, 15:W], in1=ps2[:, :, 7:W - 8], op=ALU.add),
                 reads=['ps2', 'pooled'], writes=['ps4'])
            S.op('dve', lambda e: grp(e, ps4, 3), reads=['ps4', 'pext'], writes=['pooled'])
            if ti == 0:
                self.pool_fix(ps4, 3)
            self.pool_out(l, ti)
            if ti == 3:
                self.store_fm([(pext[:, t, 16 + 512 - 15:16 + 512], 'pext') for t in range(2)], 15,
                              self.o_p_pool[l], 'o_p_pool')
        else:
            self.pool_sample(l, win, slot)

    def pool_fix(self, src, g):
        S = self.S
        t, hf = g // 2, g % 2
        sl = slice(hf * 64, (hf + 1) * 64)
        pooled, pext, tab = self.pooled, self.pext, self.pool_tab
        S.op('dve', lambda e: e.tensor_tensor(out=pooled[sl, t, 0:16], in0=src[sl, t, 16:32], in1=tab[sl, t, :], op=ALU.mult),
             reads=['pool_tab'], writes=['pooled'])
        S.op('dve', lambda e: e.tensor_tensor(out=pooled[sl, t, 0:16], in0=pooled[sl, t, 0:16], in1=pext[sl, t, 16:32], op=ALU.subtract),
             reads=['pooled', 'pext'], writes=['pooled'])

    def pool_out(self, l, ti):
        S = self.S
        c0, n = TTS[ti]
        pooled, ymix, pw = self.pooled, self.ymix, self.poolw
        for t in range(2):
            pb, pk = self.pn()
            S.op('pe', (lambda e, pb=pb, t=t: e.matmul(pb[:, :n], lhsT=pw[:, l, t, :], rhs=pooled[:, t, :n], start=True, stop=True)),
                 reads=['pooled', 'poolw'], writes=[pk])
            S.op('dve', (lambda e, pb=pb, t=t: e.tensor_scalar(out=ymix[:, t, :n], in0=pb[:, :n],
                                                               scalar1=self.pc['pool_scale'][:, l, t:t + 1], scalar2=None, op0=ALU.mult)),
                 reads=[pk, 'pc'], writes=['ymix'])

    def store_fm(self, blocks, ncols, dram_ap, okey):
        S = self.S
        stg, sk = self.next_stage()
        pb, pk = self.pn()
        nb = len(blocks)
        assert nb <= 4

        def f(e):
            r = None
            for j, (ap, k) in enumerate(blocks):
                r = e.transpose(out=pb[:ncols, j * 128:(j + 1) * 128], in_=ap, identity=self.ident[:, :])
            return r
        S.op('pe', f, reads=[k for _, k in blocks] + ['ident'], writes=[pk])
        S.op('act', lambda e: e.activation(out=stg[:ncols, :nb * 128], in_=pb[:ncols, :nb * 128], func=AF.Copy),
             reads=[pk], writes=[sk])
        S.op('sp', lambda e: e.dma_start(out=dram_ap, in_=stg[:ncols, :nb * 128]), reads=[sk], writes=[okey], dma=okey)
        if okey not in self.outkeys:
            self.outkeys.append(okey)

    def final(self):
        S = self.S
        xf = self.xf
        S.barrier()
        for ti in range(5):
            c0, n = TTS[ti]
            self.rmsnorm(ti, lambda kt: self.pc_normf[:, kt:kt + 1],
                         lambda kt, n: (xf[:, kt, :n], ('xf', kt)))
            for s0 in range(0, n, 128):
                w = min(128, n - s0)
                for half in range(2):
                    stg, sk = self.next_stage()
                    pb, pk = self.pn()

                    def f(e, pb=pb, half=half, s0=s0, w=w):
                        r = None
                        for j in range(4):
                            r = e.transpose(out=pb[:w, j * 128:(j + 1) * 128], in_=xf[:, half * 4 + j, s0:s0 + w],
                                            identity=self.ident[:, :])
                        return r
                    S.op('pe', f, reads=[('xf', half * 4 + j) for j in range(4)] + ['ident'], writes=[pk])
                    S.op('act', (lambda e, pb=pb, stg=stg, w=w: e.activation(out=stg[:w, :512], in_=pb[:w, :], func=AF.Copy)),
                         reads=[pk], writes=[sk])
                    S.op('sp', (lambda e, stg=stg, w=w, half=half, r0=c0 + s0: e.dma_start(
                        out=self.o_y[r0:r0 + w, half * 512:(half + 1) * 512], in_=stg[:w, :512])),
                        reads=[sk], writes=['o_y'], dma='o_y')
        self.outkeys.append('o_y')


PARAM_SHAPES = {
    'norm1_g': [DEPTH, 1024], 'w_in': [DEPTH, 1024, PROJ],
    'ssm_lambda_re': [DEPTH, 16, 64], 'ssm_lambda_im': [DEPTH, 16, 64], 'ssm_log_dt': [DEPTH, 16],
    'ssm_b_re': [DEPTH, 16, 64, 16], 'ssm_b_im': [DEPTH, 16, 64, 16],
    'ssm_c_re': [DEPTH, 16, 16, 64], 'ssm_c_im': [DEPTH, 16, 16, 64],
    'ssm_d': [DEPTH, 256], 'ssm_glu_w': [DEPTH, 256, 256], 'ssm_glu_b': [DEPTH, 256],
    'hgrn_lb_logits': [DEPTH, 256], 'hgrn_norm_g': [DEPTH, 256],
    'rwkv_mu': [DEPTH, 1024], 'rwkv_w0': [DEPTH, 256], 'rwkv_w2': [DEPTH, 64, 256],
    'rwkv_a0': [DEPTH, 256], 'rwkv_a2': [DEPTH, 64, 256], 'rwkv_g2': [DEPTH, 128, 256],
    'rwkv_k_k': [DEPTH, 256], 'rwkv_k_a': [DEPTH, 256], 'rwkv_r_k': [DEPTH, 256],
    'rwkv_ln_g': [DEPTH, 256], 'rwkv_ln_b': [DEPTH, 256],
    'pool_w': [DEPTH, 4, 64, 64], 'pool_scale': [DEPTH, 256],
    'w_out': [DEPTH, 1024, 1024], 'norm2_g': [DEPTH, 1024],
    'mlp_up': [DEPTH, 1024, DFF], 'mlp_down': [DEPTH, DFF, 1024], 'norm_f_g': [1024],
}
PCOLS = {'norm1_g': 1024, 'norm2_g': 1024, 'ssm_d': 256, 'ssm_glu_b': 256, 'hgrn_lb_logits': 256, 'hgrn_norm_g': 256,
         'rwkv_mu': 1024, 'rwkv_w0': 256, 'rwkv_a0': 256, 'rwkv_k_k': 256, 'rwkv_k_a': 256, 'rwkv_r_k': 256,
         'rwkv_ln_g': 256, 'rwkv_ln_b': 256, 'pool_scale': 256}
PC64 = ['hgrn_lb_logits', 'hgrn_norm_g', 'rwkv_w0', 'rwkv_a0', 'rwkv_k_k', 'rwkv_k_a', 'rwkv_r_k', 'rwkv_ln_g', 'rwkv_ln_b']
MIXER_COLS = [(0, 256), (256, 1024), (1280, 1024), (2304, 256)]

_NC_CACHE = {}
_RUN_KW = {}


def _build():
    if 'nc' not in _NC_CACHE:
        b = Builder()
        b.wcount = 0
        b.wissued = {}
        _orig_alloc = b.alloc

        def alloc2():
            _orig_alloc()
            b.S.op('pool', lambda e: e.memset(b.eps_col[:], NORM_EPS), writes=['eps'])
        b.alloc = alloc2
        _NC_CACHE['nc'] = b.build()
        _NC_CACHE['nops'] = b.S.nops
    return _NC_CACHE['nc']


def kernel(**inputs):
    inp = {k: np.ascontiguousarray(np.asarray(v, dtype=np.float32)) for k, v in inputs.items()}
    nc = _build()
    in_maps = []
    xs = inp['x_sample'].reshape(128 * TS, D)
    for c in range(NCORES):
        m = {}
        m['x'] = np.concatenate([inp['x_prompt'][c], xs[c * 64:(c + 1) * 64]], axis=0)
        sl = slice(c * NS, (c + 1) * NS)
        m['state_ssm_re'] = inp['state_ssm_re'][:, sl].reshape(DEPTH, NS, 1024)
        m['state_ssm_im'] = inp['state_ssm_im'][:, sl].reshape(DEPTH, NS, 1024)
        m['state_hgrn'] = inp['state_hgrn'][:, sl]
        m['state_wkv'] = inp['state_wkv'][:, sl]
        m['state_shift'] = inp['state_shift'][:, sl].reshape(DEPTH, NS, 1024)
        m['state_pool'] = inp['state_pool'][:, sl]
        for name in PARAM_SHAPES:
            m[name] = inp[name]
        in_maps.append({k: np.ascontiguousarray(v) for k, v in m.items()})
    res = run_bass_kernel_spmd(nc, in_maps, core_ids=list(range(NCORES)), **_RUN_KW)
    _NC_CACHE['res'] = res
    R = res.results
    cat = lambda name, ax: np.concatenate([np.expand_dims(r[name], ax) if False else r[name] for r in R], axis=ax)
    y_prompt = np.stack([r['o_y'][:SEQ] for r in R], axis=0)
    y_sample = np.concatenate([r['o_y'][SEQ:].reshape(NS, TS, D) for r in R], axis=0)
    pst = lambda name, shp: np.stack([r[name] for r in R], axis=1).reshape(shp)
    sst = lambda name, shp: np.concatenate([r[name] for r in R], axis=1).reshape(shp)
    outs = (
        y_prompt, y_sample,
        pst('o_p_ssm_re', (DEPTH, 8, 16, 64)), pst('o_p_ssm_im', (DEPTH, 8, 16, 64)),
        pst('o_p_hgrn', (DEPTH, 8, 4, 64, 64)), pst('o_p_wkv', (DEPTH, 8, 4, 64, 64)),
        pst('o_p_shift', (DEPTH, 8, 1, 1024)), pst('o_p_pool', (DEPTH, 8, 15, 256)),
        sst('o_s_ssm_re', (DEPTH, 128, 16, 64)), sst('o_s_ssm_im', (DEPTH, 128, 16, 64)),
        sst('o_s_hgrn', (DEPTH, 128, 4, 64, 64)), sst('o_s_wkv', (DEPTH, 128, 4, 64, 64)),
        sst('o_s_shift', (DEPTH, 128, 1, 1024)), sst('o_s_pool', (DEPTH, 128, 15, 256)),
    )
    return tuple(np.ascontiguousarray(o.astype(np.float32)) for o in outs)
```
